# Optimizing a Trainium2 kernel written in Bass

```python
import math
import jax, jax.numpy as jnp
from jax import lax
import numpy as np

D_MODEL = 2048
BATCH = 4
SEQ = 2048
DEPTH = 2
DEC_BATCH = 8
DEC_SEQ = 4
PAST_LEN = 16384
PAGE_SIZE = 128

N_BRANCH = 4
RMS_EPS = 1e-6
CONV_WIDTH = 4
POOL_WINDOWS = (2, 4, 8, 16)
POOL_GROUPS = 4
POOL_WIDTH = D_MODEL // 4
POOL_GW = POOL_WIDTH // POOL_GROUPS
POOL_BUF = max(POOL_WINDOWS) - 1
DN_HEADS = 4
DN_DK = 128
DN_DV = 128
DN_QK = DN_HEADS * DN_DK
DN_VW = DN_HEADS * DN_DV
DN_CONV_DIM = 2 * DN_QK + DN_VW
DN_CHUNK = 64
SB_HEADS = 4
SB_HEAD_DIM = 128
SB_WIDTH = SB_HEADS * SB_HEAD_DIM
SB_BLOCK = 128
SB_BIAS_INIT = -6.0
SSM_D_INNER = D_MODEL // 2
SSM_HEAD_DIM = 64
SSM_HEADS = SSM_D_INNER // SSM_HEAD_DIM
SSM_GROUPS = 2
SSM_HPG = SSM_HEADS // SSM_GROUPS
SSM_STATE = 128
SSM_CONV_DIM = SSM_D_INNER + 2 * SSM_GROUPS * SSM_STATE
SSM_CHUNK = 64
D_FF = 4 * D_MODEL
IN_SPLITS = (POOL_WIDTH, DN_CONV_DIM, DN_VW, DN_HEADS, DN_HEADS, 3 * SB_WIDTH, SSM_D_INNER, SSM_CONV_DIM, SSM_HEADS, N_BRANCH * D_MODEL)
IN_COLS = sum(IN_SPLITS)
BRANCH_WIDTHS = (POOL_WIDTH, DN_VW, SB_WIDTH, SSM_D_INNER)
BRANCH_ROWS = sum(BRANCH_WIDTHS)

kernel_name = 'hybrid_pool_delta_stickbreak_ssd_step'


def rmsnorm(x, g):
    xf = x.astype(jnp.float32)
    y = xf * lax.rsqrt(jnp.mean(xf * xf, axis=-1, keepdims=True) + RMS_EPS)
    return (y * g.astype(jnp.float32)).astype(x.dtype)


def split_sizes(a, sizes, axis):
    return jnp.split(a, np.cumsum(sizes)[:-1].tolist(), axis=axis)


def causal_conv_silu(u, buf, w, b):
    L = u.shape[1]
    ext = jnp.concatenate([buf.astype(u.dtype), u], axis=1)
    ef = ext.astype(jnp.float32)
    wf = w.astype(jnp.float32)
    out = sum(ef[:, i:i + L] * wf[i] for i in range(CONV_WIDTH))
    if b is not None:
        out = out + b.astype(jnp.float32)
    return jax.nn.silu(out), ext[:, L:]


def pool_mixer(u, buf, pos0, w_grp, scale):
    B, L, _ = u.shape
    uf = u.astype(jnp.float32)
    ext = jnp.concatenate([buf.astype(jnp.float32), uf], axis=1)
    csum = jnp.concatenate([jnp.zeros((B, 1, POOL_WIDTH), jnp.float32), jnp.cumsum(ext, axis=1)], axis=1)
    hi = csum[:, POOL_BUF + 1:]
    pos = pos0 + jnp.arange(L)
    outs = []
    for gi, w in enumerate(POOL_WINDOWS):
        cols = slice(gi * POOL_GW, (gi + 1) * POOL_GW)
        lo = csum[:, POOL_BUF + 1 - w:POOL_BUF + 1 - w + L, cols]
        cnt = jnp.minimum(pos + 1, w).astype(jnp.float32)[None, :, None]
        outs.append((hi[..., cols] - lo) / cnt - uf[..., cols])
    y = jnp.stack(outs, axis=2)
    y = jnp.einsum('blgc,gcd->blgd', y, w_grp.astype(jnp.float32)).reshape(B, L, POOL_WIDTH)
    y = y * scale.astype(jnp.float32)
    return y.astype(u.dtype), ext[:, L:].astype(u.dtype)


def l2norm(x):
    return x * lax.rsqrt(jnp.sum(x * x, axis=-1, keepdims=True) + 1e-6)


def gated_delta_chunked(q, k, v, g, beta, s0):
    B, L, H, _ = q.shape
    DV = v.shape[-1]
    C = math.gcd(L, DN_CHUNK)
    n = L // C

    def chunks(t):
        return jnp.moveaxis(t.reshape((B, n, C, H) + t.shape[3:]), (1, 3), (0, 2))

    tri = jnp.tril(jnp.ones((C, C), bool))
    stri = jnp.tril(jnp.ones((C, C), bool), -1)

    def step(S, inp):
        qc, kc, vc, gc, bc = inp
        gcum = jnp.cumsum(gc, axis=-1)
        decay = jnp.exp(jnp.where(tri, gcum[..., :, None] - gcum[..., None, :], -jnp.inf))
        kb = kc * bc[..., None]
        lmat = jnp.where(stri, jnp.einsum('bhcd,bhed->bhce', kb, kc) * decay, 0.0)
        rhs = jnp.concatenate([vc * bc[..., None], kb * jnp.exp(gcum)[..., None]], axis=-1)
        sol = lax.linalg.triangular_solve(lmat, rhs, left_side=True, lower=True, unit_diagonal=True)
        u_c, w_c = sol[..., :DV], sol[..., DV:]
        v_new = u_c - jnp.einsum('bhck,bhkv->bhcv', w_c, S)
        attn = jnp.einsum('bhck,bhek->bhce', qc, kc) * decay
        o = jnp.einsum('bhck,bhkv->bhcv', qc * jnp.exp(gcum)[..., None], S) + jnp.einsum('bhce,bhev->bhcv', attn, v_new)
        glast = gcum[..., -1:]
        S = S * jnp.exp(glast)[..., None] + jnp.einsum('bhck,bhcv->bhkv', kc * jnp.exp(glast - gcum)[..., None], v_new)
        return S, o

    S, o = lax.scan(step, s0, (chunks(q), chunks(k), chunks(v), chunks(g), chunks(beta)))
    o = jnp.moveaxis(o, (0, 2), (1, 3)).reshape(B, L, H, DV)
    return o, S


def deltanet_branch(u_qkv, u_z, u_b, u_a, conv_buf, s0, conv_w, a_log, dt_bias, norm_w):
    B, L, _ = u_qkv.shape
    qkv, conv_new = causal_conv_silu(u_qkv, conv_buf, conv_w, None)
    q, k, v = split_sizes(qkv, (DN_QK, DN_QK, DN_VW), -1)
    q = l2norm(q.reshape(B, L, DN_HEADS, DN_DK)) * (DN_DK ** -0.5)
    k = l2norm(k.reshape(B, L, DN_HEADS, DN_DK))
    v = v.reshape(B, L, DN_HEADS, DN_DV)
    beta = jax.nn.sigmoid(u_b.astype(jnp.float32))
    g = -jnp.exp(a_log.astype(jnp.float32)) * jax.nn.softplus(u_a.astype(jnp.float32) + dt_bias.astype(jnp.float32))
    o, s_new = gated_delta_chunked(q, k, v, g, beta, s0.astype(jnp.float32))
    zf = u_z.astype(jnp.float32).reshape(B, L, DN_HEADS, DN_DV)
    o = rmsnorm(o, norm_w) * jax.nn.silu(zf)
    dt_ = u_qkv.dtype
    return o.reshape(B, L, DN_VW).astype(dt_), conv_new.astype(dt_), s_new.astype(dt_)


def ssd_chunked(x, dt, a, bm, cm, h0):
    B, L = x.shape[:2]
    C = math.gcd(L, SSM_CHUNK)
    n = L // C

    def chunks(t):
        return jnp.moveaxis(t.reshape((B, n, C) + t.shape[2:]), 1, 0)

    tri = jnp.tril(jnp.ones((C, C), bool))[None, :, :, None, None]

    def step(h, inp):
        xc, dtc, bc, cc = inp
        cum = jnp.cumsum(dtc * a, axis=1)
        lmat = jnp.exp(jnp.where(tri, cum[:, :, None] - cum[:, None, :], -jnp.inf))
        cb = jnp.einsum('btgn,bsgn->btsg', cc, bc)
        wts = cb[..., None] * lmat * dtc[:, None]
        y = jnp.einsum('btsgr,bsgrp->btgrp', wts, xc)
        y = y + jnp.einsum('btgn,bgrpn->btgrp', cc, h) * jnp.exp(cum)[..., None]
        last = cum[:, -1]
        h = h * jnp.exp(last)[..., None, None] + jnp.einsum('bsgr,bsgrp,bsgn->bgrpn', jnp.exp(last[:, None] - cum) * dtc, xc, bc)
        return h, y

    h, y = lax.scan(step, h0, (chunks(x), chunks(dt), chunks(bm), chunks(cm)))
    return jnp.moveaxis(y, 0, 1).reshape(x.shape), h


def mamba_branch(u_z, u_xbc, u_dt, conv_buf, h0, conv_w, conv_b, a_log, dt_bias, d_skip, norm_w):
    B, L, _ = u_xbc.shape
    xbc, conv_new = causal_conv_silu(u_xbc, conv_buf, conv_w, conv_b)
    xs, bm, cm = split_sizes(xbc, (SSM_D_INNER, SSM_GROUPS * SSM_STATE, SSM_GROUPS * SSM_STATE), -1)
    xs = xs.reshape(B, L, SSM_GROUPS, SSM_HPG, SSM_HEAD_DIM)
    bm = bm.reshape(B, L, SSM_GROUPS, SSM_STATE)
    cm = cm.reshape(B, L, SSM_GROUPS, SSM_STATE)
    dt = jax.nn.softplus(u_dt.astype(jnp.float32) + dt_bias.astype(jnp.float32)).reshape(B, L, SSM_GROUPS, SSM_HPG)
    a = -jnp.exp(a_log.astype(jnp.float32)).reshape(SSM_GROUPS, SSM_HPG)
    h0f = h0.astype(jnp.float32).reshape(B, SSM_GROUPS, SSM_HPG, SSM_HEAD_DIM, SSM_STATE)
    y, h = ssd_chunked(xs, dt, a, bm, cm, h0f)
    y = y + d_skip.astype(jnp.float32).reshape(SSM_GROUPS, SSM_HPG)[..., None] * xs
    y = y.reshape(B, L, SSM_D_INNER) * jax.nn.silu(u_z.astype(jnp.float32))
    gw = SSM_D_INNER // SSM_GROUPS
    y = rmsnorm(y.reshape(B, L, SSM_GROUPS, gw), norm_w.reshape(SSM_GROUPS, gw)).reshape(B, L, SSM_D_INNER)
    dt_ = u_xbc.dtype
    return y.astype(dt_), conv_new.astype(dt_), h.reshape(B, SSM_HEADS, SSM_HEAD_DIM, SSM_STATE).astype(dt_)


def stick_breaking_block(q, qpos, k, v, kpos, bias):
    z = jnp.einsum('bqhd,bkhd->bhqk', q.astype(jnp.float32), k.astype(jnp.float32)) * (SB_HEAD_DIM ** -0.5)
    z = z + bias.astype(jnp.float32)[None, :, None, None]
    valid = kpos[None, :] < qpos[:, None]
    la = jnp.where(valid, jax.nn.log_sigmoid(-z), 0.0)
    surv = lax.cumsum(la, axis=3, reverse=True) - la
    att = jnp.where(valid, jnp.exp(jax.nn.log_sigmoid(z) + surv), 0.0)
    return jnp.einsum('bhqk,bkhd->bqhd', att, v.astype(jnp.float32))


def stick_breaking(q, k, v, qpos, kpos, bias):
    B, Lq, H, Dh = q.shape
    blk = SB_BLOCK if Lq % SB_BLOCK == 0 else Lq
    nb = Lq // blk
    qb = jnp.moveaxis(q.reshape(B, nb, blk, H, Dh), 1, 0)
    pb = qpos.reshape(nb, blk)
    o = lax.map(lambda qp: stick_breaking_block(qp[0], qp[1], k, v, kpos, bias), (qb, pb))
    return jnp.moveaxis(o, 0, 1).reshape(B, Lq, H * Dh)


def trunk_layer(x, past_k, past_v, pool_buf, dn_conv, dn_s, ssm_conv, ssm_h, lw):
    (n_mix_pre, n_mix_post, n_mlp_pre, n_mlp_post, w_in, pool_w, pool_scale,
     dn_conv_w, dn_a_log, dn_dt_bias, dn_norm_w, sb_bias, ssm_conv_w, ssm_conv_b,
     ssm_a_log, ssm_dt_bias, ssm_d, ssm_norm_w, w_branch, w_out, w_up, w_down) = lw
    B, L, _ = x.shape
    past = 0 if past_k is None else past_k.shape[1]
    h = rmsnorm(x, n_mix_pre)
    u = jnp.einsum('bld,dc->blc', h, w_in)
    (u_pool, u_dn_qkv, u_dn_z, u_dn_b, u_dn_a, u_sb, u_ss_z, u_ss_xbc, u_ss_dt, u_gate) = split_sizes(u, IN_SPLITS, -1)
    o_pool, pool_new = pool_mixer(u_pool, pool_buf, past, pool_w, pool_scale)
    o_dn, dn_conv_new, dn_s_new = deltanet_branch(u_dn_qkv, u_dn_z, u_dn_b, u_dn_a, dn_conv, dn_s, dn_conv_w, dn_a_log, dn_dt_bias, dn_norm_w)
    q, k, v = [t.reshape(B, L, SB_HEADS, SB_HEAD_DIM) for t in jnp.split(u_sb, 3, axis=-1)]
    k_all = k if past_k is None else jnp.concatenate([past_k.astype(k.dtype), k], axis=1)
    v_all = v if past_v is None else jnp.concatenate([past_v.astype(v.dtype), v], axis=1)
    o_sb = stick_breaking(q, k_all, v_all, past + jnp.arange(L), jnp.arange(past + L), sb_bias).astype(x.dtype)
    o_ss, ss_conv_new, ss_h_new = mamba_branch(u_ss_z, u_ss_xbc, u_ss_dt, ssm_conv, ssm_h, ssm_conv_w, ssm_conv_b, ssm_a_log, ssm_dt_bias, ssm_d, ssm_norm_w)
    gates = jax.nn.sigmoid(u_gate.astype(jnp.float32)).reshape(B, L, N_BRANCH, D_MODEL)
    w_rows = split_sizes(w_branch, BRANCH_WIDTHS, 0)
    outs = (o_pool, o_dn, o_sb, o_ss)
    merged = sum(gates[:, :, i] * jnp.einsum('blc,cd->bld', o, wr).astype(jnp.float32) for i, (o, wr) in enumerate(zip(outs, w_rows)))
    mix = jnp.einsum('bld,de->ble', merged.astype(x.dtype), w_out)
    x = x + rmsnorm(mix, n_mix_post)
    h2 = rmsnorm(x, n_mlp_pre)
    f = jnp.einsum('blf,fd->bld', jnp.square(jax.nn.relu(jnp.einsum('bld,df->blf', h2, w_up))), w_down)
    x = x + rmsnorm(f, n_mlp_post)
    return x, (k, v, pool_new, dn_conv_new, dn_s_new, ss_conv_new, ss_h_new)


def _dt_bias_init(key, shape):
    dt = jnp.exp(jax.random.uniform(key, shape, jnp.float32, minval=math.log(1e-3), maxval=math.log(1e-1)))
    return dt + jnp.log(-jnp.expm1(-dt))


def setup_inputs(seed: int = 0) -> dict:
    key = jax.random.key(seed)
    ks = jax.random.split(key, 40)
    kit = iter(range(40))
    f32 = jnp.float32

    def nrm(shape, s):
        return jax.random.normal(ks[next(kit)], shape, f32) * s

    n_pages = PAST_LEN // PAGE_SIZE
    n_used = DEC_BATCH * n_pages
    n_pool = n_used + n_used // 4
    x_prompt = nrm((BATCH, SEQ, D_MODEL), 1.0)
    x_sample = nrm((DEC_BATCH, DEC_SEQ, D_MODEL), 1.0)
    cache_sb_k = nrm((DEPTH, n_pool, PAGE_SIZE, SB_HEADS, SB_HEAD_DIM), 1.0)
    cache_sb_v = nrm((DEPTH, n_pool, PAGE_SIZE, SB_HEADS, SB_HEAD_DIM), 1.0)
    state_pool = nrm((DEPTH, DEC_BATCH, POOL_BUF, POOL_WIDTH), 1.0)
    state_dn_conv = nrm((DEPTH, DEC_BATCH, CONV_WIDTH - 1, DN_CONV_DIM), 1.0)
    state_dn_s = nrm((DEPTH, DEC_BATCH, DN_HEADS, DN_DK, DN_DV), 0.1)
    state_ssm_conv = nrm((DEPTH, DEC_BATCH, CONV_WIDTH - 1, SSM_CONV_DIM), 1.0)
    state_ssm_h = nrm((DEPTH, DEC_BATCH, SSM_HEADS, SSM_HEAD_DIM, SSM_STATE), 0.1)
    page_table = jax.random.permutation(ks[next(kit)], n_pool)[:n_used].reshape(DEC_BATCH, n_pages).astype(jnp.int32)
    norm_mix_pre = 1.0 + nrm((DEPTH, D_MODEL), 0.02)
    norm_mix_post = 1.0 + nrm((DEPTH, D_MODEL), 0.02)
    norm_mlp_pre = 1.0 + nrm((DEPTH, D_MODEL), 0.02)
    norm_mlp_post = 1.0 + nrm((DEPTH, D_MODEL), 0.02)
    w_in = nrm((DEPTH, D_MODEL, IN_COLS), D_MODEL ** -0.5)
    pool_w = nrm((DEPTH, POOL_GROUPS, POOL_GW, POOL_GW), POOL_GW ** -0.5)
    pool_scale = 1.0 + nrm((DEPTH, POOL_WIDTH), 0.1)
    dn_conv_w = nrm((DEPTH, CONV_WIDTH, DN_CONV_DIM), CONV_WIDTH ** -0.5)
    dn_a_log = jnp.log(jax.random.uniform(ks[next(kit)], (DEPTH, DN_HEADS), f32, minval=1.0, maxval=16.0))
    dn_dt_bias = _dt_bias_init(ks[next(kit)], (DEPTH, DN_HEADS))
    dn_norm_w = 1.0 + nrm((DEPTH, DN_DV), 0.02)
    sb_bias = SB_BIAS_INIT + nrm((DEPTH, SB_HEADS), 0.5)
    ssm_conv_w = nrm((DEPTH, CONV_WIDTH, SSM_CONV_DIM), CONV_WIDTH ** -0.5)
    ssm_conv_b = nrm((DEPTH, SSM_CONV_DIM), 0.01)
    ssm_a_log = jnp.log(jax.random.uniform(ks[next(kit)], (DEPTH, SSM_HEADS), f32, minval=1.0, maxval=16.0))
    ssm_dt_bias = _dt_bias_init(ks[next(kit)], (DEPTH, SSM_HEADS))
    ssm_d = 1.0 + nrm((DEPTH, SSM_HEADS), 0.1)
    ssm_norm_w = 1.0 + nrm((DEPTH, SSM_D_INNER), 0.02)
    w_branch = nrm((DEPTH, BRANCH_ROWS, D_MODEL), POOL_WIDTH ** -0.5)
    w_out = nrm((DEPTH, D_MODEL, D_MODEL), D_MODEL ** -0.5)
    w_up = nrm((DEPTH, D_MODEL, D_FF), D_MODEL ** -0.5)
    w_down = nrm((DEPTH, D_FF, D_MODEL), D_FF ** -0.5)
    return {'x_prompt': x_prompt, 'x_sample': x_sample, 'cache_sb_k': cache_sb_k, 'cache_sb_v': cache_sb_v,
            'state_pool': state_pool, 'state_dn_conv': state_dn_conv, 'state_dn_s': state_dn_s,
            'state_ssm_conv': state_ssm_conv, 'state_ssm_h': state_ssm_h, 'page_table': page_table,
            'norm_mix_pre': norm_mix_pre, 'norm_mix_post': norm_mix_post, 'norm_mlp_pre': norm_mlp_pre,
            'norm_mlp_post': norm_mlp_post, 'w_in': w_in, 'pool_w': pool_w, 'pool_scale': pool_scale,
            'dn_conv_w': dn_conv_w, 'dn_a_log': dn_a_log, 'dn_dt_bias': dn_dt_bias, 'dn_norm_w': dn_norm_w,
            'sb_bias': sb_bias,
            'ssm_conv_w': ssm_conv_w, 'ssm_conv_b': ssm_conv_b, 'ssm_a_log': ssm_a_log, 'ssm_dt_bias': ssm_dt_bias,
            'ssm_d': ssm_d, 'ssm_norm_w': ssm_norm_w, 'w_branch': w_branch, 'w_out': w_out,
            'w_up': w_up, 'w_down': w_down}


def reference(x_prompt, x_sample, cache_sb_k, cache_sb_v, state_pool, state_dn_conv, state_dn_s,
              state_ssm_conv, state_ssm_h, page_table, norm_mix_pre, norm_mix_post, norm_mlp_pre,
              norm_mlp_post, w_in, pool_w, pool_scale, dn_conv_w, dn_a_log, dn_dt_bias, dn_norm_w,
              sb_bias, ssm_conv_w, ssm_conv_b, ssm_a_log, ssm_dt_bias, ssm_d, ssm_norm_w, w_branch, w_out,
              w_up, w_down):
    bp = x_prompt.shape[0]
    bs = page_table.shape[0]
    dt_ = x_prompt.dtype
    zero_states = (jnp.zeros((bp, POOL_BUF, POOL_WIDTH), dt_),
                   jnp.zeros((bp, CONV_WIDTH - 1, DN_CONV_DIM), dt_),
                   jnp.zeros((bp, DN_HEADS, DN_DK, DN_DV), dt_),
                   jnp.zeros((bp, CONV_WIDTH - 1, SSM_CONV_DIM), dt_),
                   jnp.zeros((bp, SSM_HEADS, SSM_HEAD_DIM, SSM_STATE), dt_))
    y_prompt = x_prompt
    y_sample = x_sample
    new_p = []
    new_s = []
    for l in range(DEPTH):
        lw = (norm_mix_pre[l], norm_mix_post[l], norm_mlp_pre[l], norm_mlp_post[l], w_in[l], pool_w[l],
              pool_scale[l], dn_conv_w[l], dn_a_log[l], dn_dt_bias[l], dn_norm_w[l], sb_bias[l], ssm_conv_w[l],
              ssm_conv_b[l], ssm_a_log[l], ssm_dt_bias[l], ssm_d[l], ssm_norm_w[l], w_branch[l],
              w_out[l], w_up[l], w_down[l])
        y_prompt, st_p = trunk_layer(y_prompt, None, None, *zero_states, lw)
        past_k = jnp.take(cache_sb_k[l], page_table, axis=0).reshape(bs, -1, SB_HEADS, SB_HEAD_DIM)
        past_v = jnp.take(cache_sb_v[l], page_table, axis=0).reshape(bs, -1, SB_HEADS, SB_HEAD_DIM)
        y_sample, st_s = trunk_layer(y_sample, past_k, past_v, state_pool[l], state_dn_conv[l], state_dn_s[l],
                                     state_ssm_conv[l], state_ssm_h[l], lw)
        new_p.append(st_p)
        new_s.append(st_s)
    k_p, v_p, pool_p, dnc_p, dns_p, ssc_p, ssh_p = [jnp.stack(t) for t in zip(*new_p)]
    k_s, v_s, pool_s, dnc_s, dns_s, ssc_s, ssh_s = [jnp.stack(t) for t in zip(*new_s)]
    return (y_prompt, y_sample, k_p, v_p, pool_p, dnc_p, dns_p, ssc_p, ssh_p,
            k_s, v_s, pool_s, dnc_s, dns_s, ssc_s, ssh_s)
```

```python
import numpy as np
from contextlib import ExitStack
import concourse.bass as bass
import concourse.mybir as mybir
from concourse.bass_utils import run_bass_kernel_spmd

F32 = mybir.dt.float32
BF16 = mybir.dt.bfloat16
I32 = mybir.dt.int32
AF = mybir.ActivationFunctionType
ALU = mybir.AluOpType

D = 2048
L = 2048
LS = 4
NT = L + LS
DEPTH = 2
NKC = D // 128
INC = 14872
C_POOL = 0
C_DNQKV = 512
C_DNZ = 2048
C_DNB = 2560
C_DNA = 2564
C_SBQ = 2568
C_SBK = 3080
C_SBV = 3592
C_SSZ = 4104
C_SSX = 5128
C_SSB = 6152
C_SSC = 6408
C_SSDT = 6664
C_GATE = 6680
NPOOL = 1280
EPS = 1e-6
TCH = [(0, 512), (512, 512), (1024, 512), (1536, 512), (2048, 4)]
TTL = [(i * 128, 128) for i in range(16)] + [(2048, 4)]


class Buf:
    __slots__ = ("w", "r", "multi", "excl")

    def __init__(self, multi=False):
        self.w = {}
        self.r = {}
        self.multi = multi
        self.excl = False


class T:
    def __init__(self, t, multi=False):
        self.t = t
        self.b = Buf(multi)

    def __getitem__(self, idx):
        return self.t[idx]


class KB:
    def __init__(self):
        self.nc = bass.Bass("TRN2", target_bir_lowering=False)
        nc = self.nc
        self.es = ExitStack()
        self.engs = dict(pe=nc.tensor, act=nc.scalar, dve=nc.vector, pool=nc.gpsimd, sp=nc.sync)
        self.semobj = {}
        self.cnt = {}
        self.seen = {e: {} for e in self.engs}
        for e in self.engs:
            self.semobj[e] = self.es.enter_context(nc.semaphore("c_" + e))
            self.cnt[e] = 0
        self.slots = {}
        self.slotv = {}
        self.slotrr = {}
        for q, n in (("sp", 8), ("pool", 8), ("act", 4)):
            ks = []
            for i in range(n):
                k = "d_%s%d" % (q, i)
                self.semobj[k] = self.es.enter_context(nc.semaphore(k))
                self.slotv[k] = 0
                ks.append(k)
            self.slots[q] = ks
            self.slotrr[q] = 0

    def wait(self, e, key, val, force=False):
        if key == e and not force:
            return
        if self.seen[e].get(key, 0) >= val:
            return
        self.engs[e].wait_ge(self.semobj[key], val)
        self.seen[e][key] = val

    def _deps(self, R, W):
        deps = {}
        for b in R:
            for k, v in b.w.items():
                deps[k] = max(deps.get(k, 0), v)
        for b in W:
            if not b.multi:
                for k, v in b.w.items():
                    deps[k] = max(deps.get(k, 0), v)
            for k, v in b.r.items():
                deps[k] = max(deps.get(k, 0), v)
        return deps

    def _record(self, ev, R, W):
        k, v = ev
        for b in R:
            if b.r.get(k, 0) < v:
                b.r[k] = v
        for b in W:
            if b.multi:
                if b.w.get(k, 0) < v:
                    b.w[k] = v
            else:
                b.w = {k: v}
                b.r = {}

    def op(self, e, emit, R=(), W=(), signal=True):
        R = [x.b if isinstance(x, T) else x for x in R]
        W = [x.b if isinstance(x, T) else x for x in W]
        R0 = R
        W = W + [b for b in R if b.excl and b not in W]
        R = [b for b in R if not b.excl]
        for k, v in self._deps(R, W).items():
            self.wait(e, k, v)
        R = R0
        if e != "pe":
            v = 0
            for b in R:
                v = max(v, b.w.get(e, 0))
            if v > 0:
                self.wait(e, e, v, force=True)
        ins = emit(self.engs[e])
        if signal:
            self.cnt[e] += 1
            ins.then_inc(self.semobj[e], 1)
            ev = (e, self.cnt[e])
        else:
            ev = (e, self.cnt[e] + 1)
        self._record(ev, R, W)

    def dma(self, q, out, in_, R=(), W=(), **kw):
        R = [x.b if isinstance(x, T) else x for x in R]
        W = [x.b if isinstance(x, T) else x for x in W]
        ks = self.slots[q]
        k = ks[self.slotrr[q] % len(ks)]
        self.slotrr[q] += 1
        if self.slotv[k] > 0:
            self.wait(q, k, self.slotv[k])
        for dk, dv in self._deps(R, W).items():
            self.wait(q, dk, dv, force=True)
        self.slotv[k] += 16
        self.engs[q].dma_start(out=out, in_=in_, **kw).then_inc(self.semobj[k], 16)
        self._record((k, self.slotv[k]), R, W)

    def idma(self, out, in_, idx_ap, R=(), W=()):
        q = "pool"
        R = [x.b if isinstance(x, T) else x for x in R]
        W = [x.b if isinstance(x, T) else x for x in W]
        ks = self.slots[q]
        k = ks[self.slotrr[q] % len(ks)]
        self.slotrr[q] += 1
        if self.slotv[k] > 0:
            self.wait(q, k, self.slotv[k])
        for dk, dv in self._deps(R, W).items():
            self.wait(q, dk, dv, force=True)
        self.slotv[k] += 16
        self.engs[q].indirect_dma_start(out=out, out_offset=None, in_=in_,
                                        in_offset=bass.IndirectOffsetOnAxis(ap=idx_ap, axis=0)).then_inc(self.semobj[k], 16)
        self._record((k, self.slotv[k]), R, W)

    def barrier(self):
        for e in self.engs:
            for e2 in self.engs:
                if e2 != e and self.cnt[e2] > 0:
                    self.wait(e, e2, self.cnt[e2])
            for k, v in self.slotv.items():
                if v > 0:
                    self.wait(e, k, v)

    def sb(self, stack, name, shape, dt=F32, multi=False):
        self.uid = getattr(self, "uid", 0) + 1
        name = "%s_%d" % (name, self.uid)
        return T(stack.enter_context(self.nc.sbuf_tensor(name, list(shape), dt)), multi)

    def dram(self, name, shape, dt, kind):
        return T(self.nc.dram_tensor(name, list(shape), dt, kind=kind).ap(), True)

    def mm(self, ps, out_ap, pairs, R):
        n = len(pairs)
        for i, (a, b) in enumerate(pairs):
            self.op("pe", lambda e, a=a, b=b, i=i: e.matmul(out_ap, a, b, start=(i == 0), stop=(i == n - 1)),
                    R=R, W=[ps], signal=(i == n - 1))

    def tr(self, ps, out_ap, in_ap, ident_ap, R):
        self.op("pe", lambda e: e.transpose(out_ap, in_ap, ident_ap), R=R, W=[ps])

    def act(self, out, in_, func, R, W, **kw):
        self.op("act", lambda e: e.activation(out=out, in_=in_, func=func, **kw), R=R, W=W)

    def tt(self, out, in0, in1, op, R, W, e="dve"):
        self.op(e, lambda g: g.tensor_tensor(out=out, in0=in0, in1=in1, op=op), R=R, W=W)

    def ts(self, out, in0, s1, s2, op0, op1, R, W, e="dve"):
        if s2 is None:
            self.op(e, lambda g: g.tensor_scalar(out=out, in0=in0, scalar1=s1, scalar2=None, op0=op0), R=R, W=W)
        else:
            self.op(e, lambda g: g.tensor_scalar(out=out, in0=in0, scalar1=s1, scalar2=s2, op0=op0, op1=op1), R=R, W=W)

    def stt(self, out, in0, sc, in1, op0, op1, R, W):
        self.op("dve", lambda g: g.scalar_tensor_tensor(out=out, in0=in0, scalar=sc, in1=in1, op0=op0, op1=op1), R=R, W=W)

    def cp(self, out, in_, R, W, e="dve"):
        if e == "act":
            self.act(out, in_, AF.Copy, R, W)
        else:
            self.op(e, lambda g: g.tensor_copy(out=out, in_=in_), R=R, W=W)


def make_consts():
    j = np.arange(128)[:, None]
    t = np.arange(128)[None, :]
    c = {}
    c["ident"] = (j == t).astype(np.float32)
    c["triu"] = (j <= t).astype(np.float32)
    c["tgt"] = (j > t).astype(np.float32)
    c["ones"] = np.ones((128, 128), np.float32)
    c["tril"] = (j >= t).astype(np.float32)
    q = np.arange(512)[None, :]
    for r in range(4):
        c["am%d" % r] = ((r * 128 + j) < q).astype(np.float32)
    return c


CONST_ORDER = ["ident", "triu", "tgt", "ones", "tril", "am0", "am1", "am2", "am3"]
CONST_OFF = {}
_o = 0
for _n in CONST_ORDER:
    CONST_OFF[_n] = _o
    _o += 128 if not _n.startswith("am") else 512
CONST_W = _o


class Ctx:
    pass


def build(dbg=()):
    kb = KB()
    nc = kb.nc
    g = Ctx()
    g.kb = kb
    g.dbg = set(dbg)
    IN = lambda n, s, dt=F32: kb.dram(n, s, dt, "ExternalInput")
    OUT = lambda n, s, dt=F32: kb.dram(n, s, dt, "ExternalOutput")
    SCR = lambda n, s, dt=F32: kb.dram(n, s, dt, "Internal")
    g.x_prompt = IN("x_prompt", [L, D])
    g.x_sample = IN("x_sample", [LS, D])
    ncr = 128 if "small_cache" in g.dbg else DEPTH * NPOOL * 128
    g.cache_k = IN("cache_sb_k", [ncr, 512])
    g.cache_v = IN("cache_sb_v", [ncr, 512])
    g.state_pool = IN("state_pool", [DEPTH, 15, 512])
    g.state_dn_conv = IN("state_dn_conv", [DEPTH, 3, 1536])
    g.state_dn_s = IN("state_dn_s", [DEPTH, 4, 128, 128])
    g.state_ssm_conv = IN("state_ssm_conv", [DEPTH, 3, 1536])
    g.state_ssm_h = IN("state_ssm_h", [DEPTH, 1024, 128])
    g.page_table = IN("page_table", [1, 128], I32)
    for n in ["norm_mix_pre", "norm_mix_post", "norm_mlp_pre", "norm_mlp_post"]:
        setattr(g, n, IN(n, [DEPTH, D]))
    g.w_in = IN("w_in", [DEPTH, D, INC])
    g.pool_w = IN("pool_w", [DEPTH, 4, 128, 128])
    g.pool_scale = IN("pool_scale", [DEPTH, 512])
    g.dn_conv_w = IN("dn_conv_w", [DEPTH, 4, 1536])
    g.dn_a_log = IN("dn_a_log", [DEPTH, 4])
    g.dn_dt_bias = IN("dn_dt_bias", [DEPTH, 4])
    g.dn_norm_w = IN("dn_norm_w", [DEPTH, 128])
    g.sb_bias = IN("sb_bias", [DEPTH, 4])
    g.ssm_conv_w = IN("ssm_conv_w", [DEPTH, 4, 1536])
    g.ssm_conv_b = IN("ssm_conv_b", [DEPTH, 1536])
    g.ssm_a_log = IN("ssm_a_log", [DEPTH, 16])
    g.ssm_dt_bias = IN("ssm_dt_bias", [DEPTH, 16])
    g.ssm_d = IN("ssm_d", [DEPTH, 16])
    g.ssm_norm_w = IN("ssm_norm_w", [DEPTH, 1024])
    g.w_branch = IN("w_branch", [DEPTH, 2560, D])
    g.w_out = IN("w_out", [DEPTH, D, D])
    g.w_up = IN("w_up", [DEPTH, D, 4 * D])
    g.w_down = IN("w_down", [DEPTH, 4 * D, D])
    g.consts = IN("consts", [128, CONST_W])
    g.y_prompt = OUT("y_prompt", [L, D])
    g.y_sample = OUT("y_sample", [LS, D])
    g.k_p = OUT("k_p", [DEPTH, L, 512])
    g.v_p = OUT("v_p", [DEPTH, L, 512])
    g.pool_p = OUT("pool_p", [DEPTH, 15, 512])
    g.dnc_p = OUT("dnc_p", [DEPTH, 3, 1536])
    g.dns_p = OUT("dns_p", [DEPTH, 4, 128, 128])
    g.ssc_p = OUT("ssc_p", [DEPTH, 3, 1536])
    g.ssh_p = OUT("ssh_p", [DEPTH, 1024, 128])
    g.k_s = OUT("k_s", [DEPTH, LS, 512])
    g.v_s = OUT("v_s", [DEPTH, LS, 512])
    g.pool_s = OUT("pool_s", [DEPTH, 15, 512])
    g.dnc_s = OUT("dnc_s", [DEPTH, 3, 1536])
    g.dns_s = OUT("dns_s", [DEPTH, 4, 128, 128])
    g.ssc_s = OUT("ssc_s", [DEPTH, 3, 1536])
    g.ssh_s = OUT("ssh_s", [DEPTH, 1024, 128])
    g.xa = SCR("xa", [NT, D])
    g.xb = SCR("xb", [NT, D])
    g.oT = SCR("oT", [2560, NT], BF16)
    g.mT = SCR("mT", [D, NT], BF16)
    g.ftm = SCR("ftm", [NT, D])
    if "oT_in" in g.dbg:
        g.oT_in = IN("oT_in", [DEPTH, 2560, NT])
    if "oT_out" in g.dbg:
        g.oT_out = OUT("oT_out", [2560, NT], BF16)
    if "dump0" in g.dbg:
        g.x1_out = OUT("x1_out", [NT, D])
        g.x2_out = OUT("x2_out", [NT, D])
        g.mT_out = OUT("mT_out", [D, NT], BF16)
        g.ftm_out = OUT("ftm_out", [NT, D])
    if "hT_out" in g.dbg:
        g.hT_out = OUT("hT_out", [128, NKC, NT], BF16)

    with ExitStack() as top:
        g.cst = kb.sb(top, "cst", [128, 640])
        kb.dma("sp", g.cst[:], g.consts[:, 0:640], R=[g.consts], W=[g.cst])
        g.cstb = kb.sb(top, "cstb", [128, 640], BF16)
        kb.dma("pool", g.cstb[:], g.consts[:, 0:640], R=[g.consts], W=[g.cstb])
        g.ps = [T(top.enter_context(nc.psum_tensor("ps%d" % i, [128, 512], F32))) for i in range(8)]
        for p_ in g.ps:
            p_.b.excl = True
        g.psi = 0
        g.eps_t = kb.sb(top, "eps_t", [128, 1])
        kb.op("dve", lambda e: e.memset(g.eps_t[:], EPS), W=[g.eps_t])
        for l in range(1 if "mix_only" in g.dbg else DEPTH):
            xin_p = g.x_prompt if l == 0 else T(g.xb[0:L, :])
            xin_s = g.x_sample if l == 0 else T(g.xb[L:NT, :])
            if l > 0:
                xin_p.b = g.xb.b
                xin_s.b = g.xb.b
            xout_p = T(g.xb[0:L, :]) if l == 0 else g.y_prompt
            xout_s = T(g.xb[L:NT, :]) if l == 0 else g.y_sample
            if l == 0:
                xout_p.b = g.xb.b
                xout_s.b = g.xb.b
            with ExitStack() as lst:
                g.hT = kb.sb(lst, "hT", [128, NKC, NT], BF16, multi=True)
                phase_norm1(g, l, xin_p, xin_s)
                if "hT_out" in g.dbg and l == 0:
                    kb.dma("sp", g.hT_out[:], g.hT[:], R=[g.hT], W=[g.hT_out])
                if "oT_in" in g.dbg:
                    pass
                else:
                    if "no_pool" not in g.dbg:
                        phase_pool(g, l)
                    if "no_attn" not in g.dbg:
                        phase_attn(g, l)
                    if "no_dn" not in g.dbg:
                        phase_dn(g, l)
                    if "no_ssd" not in g.dbg:
                        phase_ssd(g, l)
                if "oT_out" in g.dbg and l == 0:
                    kb.barrier()
                    kb.dma("sp", g.oT_out[:], g.oT[:], R=[g.oT], W=[g.oT_out])
                if "mix_only" not in g.dbg:
                    phase_merge(g, l)
            if "mix_only" not in g.dbg:
                phase_mlp(g, l, xin_p, xin_s, xout_p, xout_s)
            if "dump0" in g.dbg and l == 0:
                kb.barrier()
                kb.dma("sp", g.x1_out[:], g.xa[:], R=[g.xa], W=[g.x1_out])
                kb.dma("sp", g.x2_out[:], g.xb[:], R=[g.xb], W=[g.x2_out])
                kb.dma("sp", g.mT_out[:], g.mT[:], R=[g.mT], W=[g.mT_out])
                kb.dma("sp", g.ftm_out[:], g.ftm[:], R=[g.ftm], W=[g.ftm_out])
                kb.barrier()
        kb.barrier()
    return kb


def nps(g):
    p = g.ps[g.psi % 6]
    g.psi += 1
    return p


def xrows(xp, xs, tok0, n):
    if tok0 < L:
        return xp, xp[tok0:tok0 + n, :]
    return xs, xs[tok0 - L:tok0 - L + n, :]


def rstd_from_ss(g, st, ss, n, tag):
    kb = g.kb
    kb.act(ss[0:n, :], ss[0:n, :], AF.Ln, R=[ss, g.eps_t], W=[ss], scale=1.0 / D, bias=g.eps_t[0:n, 0:1])
    kb.act(ss[0:n, :], ss[0:n, :], AF.Exp, R=[ss], W=[ss], scale=-0.5)


def tm_to_hT(g, src, n, tok0, dstT, evi=0):
    kb = g.kb
    ident = g.cst[:, CONST_OFF["ident"]:CONST_OFF["ident"] + 128]
    for j in range(4):
        p = nps(g)
        for i in range(4):
            kc = j * 4 + i
            kb.tr(p, p[:, i * 128:i * 128 + n], src[0:n, kc * 128:(kc + 1) * 128], ident[0:n, 0:n], R=[src, g.cst])
        o = dstT[:, j * 4:j * 4 + 4, tok0:tok0 + n]
        i_ = p[:, :].rearrange("p (a b) -> p a b", b=128)[:, :, 0:n]
        kb.cp(o, i_, R=[p], W=[dstT], e=("act" if (j + evi) % 2 else "dve"))


def phase_norm1(g, l, xin_p, xin_s):
    kb = g.kb
    with ExitStack() as st:
        gb = kb.sb(st, "gb", [128, D])
        kb.dma("sp", gb[:], g.norm_mix_pre[l].partition_broadcast(128), R=[g.norm_mix_pre], W=[gb])
        xt = [kb.sb(st, "xt%d" % i, [128, D]) for i in range(2)]
        xn = [kb.sb(st, "xn%d" % i, [128, D]) for i in range(2)]
        junk = kb.sb(st, "junk", [128, D], BF16)
        ss = [kb.sb(st, "ss%d" % i, [128, 1]) for i in range(2)]
        for ti, (tok0, n) in enumerate(TTL):
            b = ti % 2
            xsrc, rows = xrows(xin_p, xin_s, tok0, n)
            kb.dma("sp", xt[b][0:n, :], rows, R=[xsrc], W=[xt[b]])
            kb.act(junk[0:n, :], xt[b][0:n, :], AF.Square, R=[xt[b]], W=[junk, ss[b]], accum_out=ss[b][0:n, :])
            rstd_from_ss(g, st, ss[b], n, "n1")
            kb.stt(xn[b][0:n, :], xt[b][0:n, :], ss[b][0:n, :], gb[0:n, :], ALU.mult, ALU.mult, R=[xt[b], ss[b], gb], W=[xn[b]])
            tm_to_hT(g, xn[b], n, tok0, g.hT, ti)
        kb.barrier()


HALVES = [(0, 1024, [(0, 512), (512, 512)]), (1024, 1028, [(1024, 512), (1536, 512), (2048, 4)])]
BRCH = [list(range(0, 4)), list(range(4, 8)), list(range(8, 12)), list(range(12, 20))]


def phase_merge(g, l):
    kb = g.kb
    with ExitStack() as st:
        oTs = kb.sb(st, "oTs", [128, 20, 1028], BF16)
        wg = [kb.sb(st, "wg%d" % i, [128, NKC, 4, 128], BF16) for i in range(2)]
        wb = [kb.sb(st, "wb%d" % i, [128, 20, 128], BF16) for i in range(2)]
        sg = [kb.sb(st, "sg%d" % i, [128, 512]) for i in range(2)]
        tmp = [kb.sb(st, "mtmp%d" % i, [128, 512]) for i in range(2)]
        acc = [kb.sb(st, "macc%d" % i, [128, 512]) for i in range(2)]
        mo = [kb.sb(st, "mo%d" % i, [128, 1028], BF16) for i in range(2)]
        it = 0
        for (h0, hn, chunks) in HALVES:
            if "oT_in" in g.dbg:
                kb.dma("pool", oTs[:, :, 0:hn], g.oT_in[l, :, h0:h0 + hn].rearrange("(bc p) t -> p bc t", p=128),
                       R=[g.oT_in], W=[oTs])
            else:
                kb.dma("sp", oTs[:, :, 0:hn], g.oT[:, h0:h0 + hn].rearrange("(bc p) t -> p bc t", p=128),
                       R=[g.oT], W=[oTs])
            for dj in range(NKC):
                b = dj % 2
                for i in range(4):
                    c0 = C_GATE + i * D + dj * 128
                    kb.dma("pool", wg[b][:, :, i, :], g.w_in[l, :, c0:c0 + 128].rearrange("(kc p) c -> p kc c", p=128),
                           R=[g.w_in], W=[wg[b]])
                kb.dma("pool", wb[b][:], g.w_branch[l, :, dj * 128:(dj + 1) * 128].rearrange("(bc p) c -> p bc c", p=128),
                       R=[g.w_branch], W=[wb[b]])
                for (tok0, n) in chunks:
                    a = acc[it % 2]
                    for i in range(4):
                        pg = nps(g)
                        kb.mm(pg, pg[:, 0:n], [(wg[b][:, kc, i, :], g.hT[:, kc, tok0:tok0 + n]) for kc in range(NKC)],
                              R=[wg[b], g.hT])
                        pp = nps(g)
                        kb.mm(pp, pp[:, 0:n], [(wb[b][:, bc, :], oTs[:, bc, tok0 - h0:tok0 - h0 + n]) for bc in BRCH[i]],
                              R=[wb[b], oTs])
                        s_ = sg[i % 2]
                        kb.act(s_[:, 0:n], pg[:, 0:n], AF.Sigmoid, R=[pg], W=[s_])
                        last = (i == 3)
                        if i == 0:
                            kb.tt(a[:, 0:n], pp[:, 0:n], s_[:, 0:n], ALU.mult, R=[pp, s_], W=[a])
                        else:
                            t_ = tmp[i % 2]
                            kb.tt(t_[:, 0:n], pp[:, 0:n], s_[:, 0:n], ALU.mult, R=[pp, s_], W=[t_])
                            if last:
                                kb.tt(mo[b][:, tok0 - h0:tok0 - h0 + n], a[:, 0:n], t_[:, 0:n], ALU.add, R=[a, t_], W=[mo[b]])
                            else:
                                kb.tt(a[:, 0:n], a[:, 0:n], t_[:, 0:n], ALU.add, R=[a, t_], W=[a])
                    it += 1
                kb.dma("sp", g.mT[dj * 128:(dj + 1) * 128, h0:h0 + hn], mo[b][:, 0:hn], R=[mo[b]], W=[g.mT])
        kb.barrier()


def phase_mlp(g, l, xin_p, xin_s, xout_p, xout_s):
    kb = g.kb
    ident = g.cst[:, CONST_OFF["ident"]:CONST_OFF["ident"] + 128]
    for (h0, hn, chunks) in HALVES:
        tiles = [(t0, n) for (t0, n) in TTL if h0 <= t0 < h0 + hn]
        with ExitStack() as sth:
            h2T = kb.sb(sth, "h2T", [128, NKC, 1028], BF16, multi=True)
            with ExitStack() as st:
                wo = kb.sb(st, "wo", [128, NKC, D], BF16, multi=True)
                for ec in range(4):
                    kb.dma("pool", wo[:, :, ec * 512:(ec + 1) * 512],
                           g.w_out[l, :, ec * 512:(ec + 1) * 512].rearrange("(kc p) c -> p kc c", p=128), R=[g.w_out], W=[wo])
                mTs = kb.sb(st, "mTs", [128, NKC, 1028], BF16)
                kb.dma("sp", mTs[:, :, 0:hn], g.mT[:, h0:h0 + hn].rearrange("(kc p) t -> p kc t", p=128), R=[g.mT], W=[mTs])
                gpost = kb.sb(st, "gpost", [128, D])
                kb.dma("sp", gpost[:], g.norm_mix_post[l].partition_broadcast(128), R=[g.norm_mix_post], W=[gpost])
                gpre2 = kb.sb(st, "gpre2", [128, D])
                kb.dma("sp", gpre2[:], g.norm_mlp_pre[l].partition_broadcast(128), R=[g.norm_mlp_pre], W=[gpre2])
                mix = [kb.sb(st, "mix%d" % i, [128, D], multi=True) for i in range(2)]
                xt = [kb.sb(st, "xt2_%d" % i, [128, D]) for i in range(2)]
                junk = kb.sb(st, "junk2", [128, D], BF16)
                ss = [kb.sb(st, "ss2_%d" % i, [128, 1]) for i in range(4)]
                def s1_front(ti, tok0, n):
                    b = ti % 2
                    tl = tok0 - h0
                    xsrc, rows = xrows(xin_p, xin_s, tok0, n)
                    kb.dma("sp", xt[b][0:n, :], rows, R=[xsrc], W=[xt[b]])
                    for ec in range(4):
                        p = nps(g)
                        kb.mm(p, p[0:n, :], [(mTs[:, kc, tl:tl + n], wo[:, kc, ec * 512:(ec + 1) * 512]) for kc in range(NKC)],
                              R=[mTs, wo])
                        kb.cp(mix[b][0:n, ec * 512:(ec + 1) * 512], p[0:n, :], R=[p], W=[mix[b]], e=("act" if ec % 2 else "dve"))

                def s1_back(ti, tok0, n):
                    b = ti % 2
                    tl = tok0 - h0
                    s1 = ss[2 * b]
                    s2 = ss[2 * b + 1]
                    kb.act(junk[0:n, :], mix[b][0:n, :], AF.Square, R=[mix[b]], W=[junk, s1], accum_out=s1[0:n, :])
                    rstd_from_ss(g, st, s1, n, "a")
                    kb.stt(mix[b][0:n, :], mix[b][0:n, :], s1[0:n, :], gpost[0:n, :], ALU.mult, ALU.mult, R=[mix[b], s1, gpost], W=[mix[b]])
                    kb.tt(xt[b][0:n, :], xt[b][0:n, :], mix[b][0:n, :], ALU.add, R=[xt[b], mix[b]], W=[xt[b]])
                    kb.dma("sp", g.xa[tok0:tok0 + n, :], xt[b][0:n, :], R=[xt[b]], W=[g.xa])
                    kb.act(junk[0:n, :], xt[b][0:n, :], AF.Square, R=[xt[b]], W=[junk, s2], accum_out=s2[0:n, :])
                    rstd_from_ss(g, st, s2, n, "b")
                    kb.stt(mix[b][0:n, :], xt[b][0:n, :], s2[0:n, :], gpre2[0:n, :], ALU.mult, ALU.mult, R=[xt[b], s2, gpre2, mix[b]], W=[mix[b]])
                    tm_to_hT(g, mix[b], n, tl, h2T, ti)

                s1_front(0, *tiles[0])
                for ti, (tok0, n) in enumerate(tiles):
                    if ti + 1 < len(tiles):
                        s1_front(ti + 1, *tiles[ti + 1])
                    s1_back(ti, tok0, n)
                kb.barrier()
            with ExitStack() as st:
                fT = kb.sb(st, "fT", [128, 64, 1028], BF16, multi=True)
                with ExitStack() as st2:
                    wu = [kb.sb(st2, "wu%d" % i, [128, NKC, 256], BF16) for i in range(2)]
                    rl = [kb.sb(st2, "rl%d" % i, [128, 512]) for i in range(2)]
                    it = 0
                    for fb in range(32):
                        b = fb % 2
                        kb.dma("pool", wu[b][:], g.w_up[l, :, fb * 256:(fb + 1) * 256].rearrange("(kc p) c -> p kc c", p=128),
                               R=[g.w_up], W=[wu[b]])
                        for j in range(2):
                            for (tok0, n) in chunks:
                                cl = tok0 - h0
                                p = nps(g)
                                kb.mm(p, p[:, 0:n], [(wu[b][:, kc, j * 128:(j + 1) * 128], h2T[:, kc, cl:cl + n]) for kc in range(NKC)],
                                      R=[wu[b], h2T])
                                r_ = rl[it % 2]
                                kb.act(r_[:, 0:n], p[:, 0:n], AF.Relu, R=[p], W=[r_])
                                kb.tt(fT[:, fb * 2 + j, cl:cl + n], r_[:, 0:n], r_[:, 0:n], ALU.mult, R=[r_], W=[fT])
                                it += 1
                    kb.barrier()
                with ExitStack() as st2:
                    wd = [kb.sb(st2, "wd%d" % i, [128, 64, 128], BF16) for i in range(2)]
                    fo = [kb.sb(st2, "fo%d" % i, [128, 512]) for i in range(2)]
                    stg = [kb.sb(st2, "stg%d" % i, [128, 4, 128]) for i in range(2)]
                    it = 0
                    for dj in range(NKC):
                        b = dj % 2
                        kb.dma("pool", wd[b][:], g.w_down[l, :, dj * 128:(dj + 1) * 128].rearrange("(fc p) c -> p fc c", p=128),
                               R=[g.w_down], W=[wd[b]])
                        for (tok0, n) in chunks:
                            cl = tok0 - h0
                            p = nps(g)
                            kb.mm(p, p[:, 0:n], [(wd[b][:, fc, :], fT[:, fc, cl:cl + n]) for fc in range(64)], R=[wd[b], fT])
                            f_ = fo[it % 2]
                            s_ = stg[it % 2]
                            kb.cp(f_[:, 0:n], p[:, 0:n], R=[p], W=[f_], e="act")
                            p2 = nps(g)
                            nt_ = (n + 127) // 128
                            for a in range(nt_):
                                na = min(128, n - a * 128)
                                kb.tr(p2, p2[0:na, a * 128:(a + 1) * 128], f_[:, a * 128:a * 128 + na], ident, R=[f_, g.cst])
                            if n == 512:
                                kb.cp(s_[:, :, :], p2[:, :].rearrange("p (a c) -> p a c", c=128), R=[p2], W=[s_])
                                kb.dma("sp", g.ftm[tok0:tok0 + n, dj * 128:(dj + 1) * 128].rearrange("(a p) c -> p a c", p=128),
                                       s_[:, :, :], R=[s_], W=[g.ftm])
                            else:
                                kb.cp(s_[0:n, 0, :], p2[0:n, 0:128], R=[p2], W=[s_])
                                kb.dma("sp", g.ftm[tok0:tok0 + n, dj * 128:(dj + 1) * 128], s_[0:n, 0, :], R=[s_], W=[g.ftm])
                            it += 1
                    kb.barrier()
        with ExitStack() as st:
            gp2 = kb.sb(st, "gp2", [128, D])
            kb.dma("sp", gp2[:], g.norm_mlp_post[l].partition_broadcast(128), R=[g.norm_mlp_post], W=[gp2])
            ft = [kb.sb(st, "ft%d" % i, [128, D]) for i in range(2)]
            x1 = [kb.sb(st, "x1_%d" % i, [128, D]) for i in range(2)]
            junk = kb.sb(st, "junk3", [128, D], BF16)
            ss = [kb.sb(st, "ss3_%d" % i, [128, 1]) for i in range(2)]
            for ti, (tok0, n) in enumerate(tiles):
                b = ti % 2
                kb.dma("sp", ft[b][0:n, :], g.ftm[tok0:tok0 + n, :], R=[g.ftm], W=[ft[b]])
                kb.dma("sp", x1[b][0:n, :], g.xa[tok0:tok0 + n, :], R=[g.xa], W=[x1[b]])
                kb.act(junk[0:n, :], ft[b][0:n, :], AF.Square, R=[ft[b]], W=[junk, ss[b]], accum_out=ss[b][0:n, :])
                rstd_from_ss(g, st, ss[b], n, "c")
                kb.stt(ft[b][0:n, :], ft[b][0:n, :], ss[b][0:n, :], gp2[0:n, :], ALU.mult, ALU.mult, R=[ft[b], ss[b], gp2], W=[ft[b]])
                kb.tt(x1[b][0:n, :], x1[b][0:n, :], ft[b][0:n, :], ALU.add, R=[x1[b], ft[b]], W=[x1[b]])
                xdst, rows = xrows(xout_p, xout_s, tok0, n)
                kb.dma("sp", rows, x1[b][0:n, :], R=[x1[b]], W=[xdst])
            kb.barrier()


PW = 16 + L + 16 + LS
PS0 = 16 + L


def ucol(tok0):
    return 16 + tok0 if tok0 < L else PS0 + 16 + (tok0 - L)


def phase_pool(g, l):
    kb = g.kb
    ident = g.cst[:, 0:128]
    with ExitStack() as st:
        wpl = kb.sb(st, "wpl", [128, NKC, 512], BF16)
        kb.dma("pool", wpl[:], g.w_in[l, :, C_POOL:C_POOL + 512].rearrange("(kc p) c -> p kc c", p=128), R=[g.w_in], W=[wpl])
        pw = kb.sb(st, "pw", [128, 4, 128], BF16)
        kb.dma("pool", pw[:], g.pool_w[l].rearrange("g c d -> c g d"), R=[g.pool_w], W=[pw])
        psc = kb.sb(st, "psc", [128, 4])
        kb.dma("sp", psc[:], g.pool_scale[l].rearrange("(g p) -> p g", p=128), R=[g.pool_scale], W=[psc],
               allow_slow_non_contiguous=True)
        invs = kb.sb(st, "invs", [128, 4, 16])
        for gi in range(4):
            w = 2 << gi
            for pos in range(15):
                kb.op("dve", lambda e, gi=gi, pos=pos, w=w: e.memset(invs[:, gi, pos:pos + 1], 1.0 / min(pos + 1, w)), W=[invs])
        spt = kb.sb(st, "spt", [16, 512])
        kb.dma("sp", spt[0:15, :], g.state_pool[l], R=[g.state_pool], W=[spt])
        A = kb.sb(st, "pA", [128, PW])
        B = kb.sb(st, "pB", [128, PW])
        C = kb.sb(st, "pC", [128, PW])
        yb = kb.sb(st, "pyb", [128, PW], BF16)
        tmpf = kb.sb(st, "ptmp", [128, 16])
        opT = [kb.sb(st, "opT%d" % i, [128, NT], BF16) for i in range(2)]
        for gi in range(4):
            w = 2 << gi
            kb.op("dve", lambda e: e.memset(A[:, 0:16], 0.0), W=[A])
            kb.op("dve", lambda e: e.memset(A[:, PS0:PS0 + 1], 0.0), W=[A])
            p = nps(g)
            kb.tr(p, p[:, 0:15], spt[0:15, gi * 128:(gi + 1) * 128], ident[0:15, 0:15], R=[spt, g.cst])
            kb.cp(A[:, PS0 + 1:PS0 + 16], p[:, 0:15], R=[p], W=[A])
            for ci, (tok0, n) in enumerate(TCH):
                p = nps(g)
                kb.mm(p, p[:, 0:n], [(wpl[:, kc, gi * 128:(gi + 1) * 128], g.hT[:, kc, tok0:tok0 + n]) for kc in range(NKC)],
                      R=[wpl, g.hT])
                c0 = ucol(tok0)
                kb.cp(A[:, c0:c0 + n], p[:, 0:n], R=[p], W=[A], e=("act" if ci % 2 else "dve"))
            src = A
            bufs = [B, C]
            sh = 1
            lo = 0
            for stp in range(gi + 1):
                dst = bufs[stp % 2]
                lo += sh
                kb.tt(dst[:, lo:PW], src[:, lo:PW], src[:, lo - sh:PW - sh], ALU.add, R=[src], W=[dst])
                src = dst
                sh *= 2
            kb.stt(yb[:, 16:PW], src[:, 16:PW], 1.0 / w, A[:, 16:PW], ALU.mult, ALU.subtract, R=[src, A], W=[yb])
            kb.tt(tmpf[:, 0:15], src[:, 16:31], invs[:, gi, 0:15], ALU.mult, R=[src, invs], W=[tmpf])
            kb.tt(yb[:, 16:31], tmpf[:, 0:15], A[:, 16:31], ALU.subtract, R=[tmpf, A, yb], W=[yb])
            o_ = opT[gi % 2]
            for ci, (tok0, n) in enumerate(TCH):
                p = nps(g)
                c0 = ucol(tok0)
                kb.mm(p, p[:, 0:n], [(pw[:, gi, :], yb[:, c0:c0 + n])], R=[pw, yb])
                kb.ts(o_[:, tok0:tok0 + n], p[:, 0:n], psc[:, gi:gi + 1], None, ALU.mult, None, R=[p, psc], W=[o_])
            kb.dma("sp", g.oT[gi * 128:(gi + 1) * 128, :], o_[:, :], R=[o_], W=[g.oT])
        pn = kb.sb(st, "pn", [16, 512])
        p = nps(g)
        kb.mm(p, p[0:15, :], [(g.hT[:, kc, L - 15:L], wpl[:, kc, :]) for kc in range(NKC)], R=[g.hT, wpl])
        kb.cp(pn[0:15, :], p[0:15, :], R=[p], W=[pn])
        kb.dma("sp", g.pool_p[l], pn[0:15, :], R=[pn], W=[g.pool_p])
        pn2 = kb.sb(st, "pn2", [4, 512])
        p = nps(g)
        kb.mm(p, p[0:LS, :], [(g.hT[:, kc, L:NT], wpl[:, kc, :]) for kc in range(NKC)], R=[g.hT, wpl])
        kb.cp(pn2[0:LS, :], p[0:LS, :], R=[p], W=[pn2])
        kb.dma("sp", g.pool_s[l, 15 - LS:15, :], pn2[0:LS, :], R=[pn2], W=[g.pool_s])
        kb.dma("sp", g.pool_s[l, 0:15 - LS, :], g.state_pool[l, LS:15, :], R=[g.state_pool], W=[g.pool_s])
        kb.barrier()


def proj_fm(g, wt, c0, dst_fn, scale=None, evs=("act", "dve")):
    kb = g.kb
    for ci, (tok0, n) in enumerate(TCH):
        p = nps(g)
        kb.mm(p, p[:, 0:n], [(wt[:, kc, c0:c0 + 128], g.hT[:, kc, tok0:tok0 + n]) for kc in range(NKC)], R=[wt, g.hT])
        dT, dap = dst_fn(tok0, n)
        e = evs[ci % len(evs)]
        if scale is None:
            kb.cp(dap, p[:, 0:n], R=[p], W=[dT], e=e)
        elif e == "act":
            kb.act(dap, p[:, 0:n], AF.Copy, R=[p], W=[dT], scale=scale)
        else:
            kb.ts(dap, p[:, 0:n], scale, None, ALU.mult, None, R=[p], W=[dT])


def phase_attn(g, l):
    kb = g.kb
    ident = g.cst[:, 0:128]
    SC = 128.0 ** -0.5
    with ExitStack() as st:
        qT = kb.sb(st, "qT", [128, 4, NT], BF16, multi=True)
        kT = kb.sb(st, "kT", [128, 4, NT], BF16, multi=True)
        vtm = kb.sb(st, "vtm", [128, 16, 512], BF16, multi=True)
        vs = kb.sb(st, "vs", [4, 512], BF16)
        am = kb.sb(st, "am", [128, 4, 512], BF16)
        kb.dma("pool", am[:], g.consts[:, 640:2688].rearrange("p (r q) -> p r q", q=512), R=[g.consts], W=[am])
        bsb = kb.sb(st, "bsb", [128, 4])
        kb.dma("sp", bsb[:], g.sb_bias[l].partition_broadcast(128), R=[g.sb_bias], W=[bsb])
        ntgt = kb.sb(st, "ntgt", [128, 128], BF16)
        kb.ts(ntgt[:], g.cst[:, 256:384], -1.0, None, ALU.mult, None, R=[g.cst], W=[ntgt])
        nonesf = kb.sb(st, "nonesf", [128, 128])
        kb.ts(nonesf[:], g.cst[:, 384:512], -1.0, None, ALU.mult, None, R=[g.cst], W=[nonesf])
        ntgtf = kb.sb(st, "ntgtf", [128, 128])
        kb.ts(ntgtf[:], g.cst[:, 256:384], -1.0, None, ALU.mult, None, R=[g.cst], W=[ntgtf])
        zer = kb.sb(st, "zer", [128, 128], BF16)
        kb.op("dve", lambda e: e.memset(zer[:], 0.0), W=[zer])
        with ExitStack() as st2:
            wb_ = [kb.sb(st2, "aw%d" % i, [128, NKC, 512], BF16) for i in range(2)]
            stg = [kb.sb(st2, "astg%d" % i, [128, 512]) for i in range(2)]
            if "stop_a" in g.dbg:
                kb.barrier()
                return
            kb.dma("pool", wb_[0][:], g.w_in[l, :, C_SBQ:C_SBQ + 512].rearrange("(kc p) c -> p kc c", p=128), R=[g.w_in], W=[wb_[0]])
            kb.dma("pool", wb_[1][:], g.w_in[l, :, C_SBK:C_SBK + 512].rearrange("(kc p) c -> p kc c", p=128), R=[g.w_in], W=[wb_[1]])
            for h in range(4):
                proj_fm(g, wb_[0], h * 128, lambda tok0, n, h=h: (qT, qT[:, h, tok0:tok0 + n]), scale=SC)
            for h in range(4):
                proj_fm(g, wb_[1], h * 128, lambda tok0, n, h=h: (kT, kT[:, h, tok0:tok0 + n]))
            if "stop_b" in g.dbg:
                kb.barrier()
                return
            kb.dma("pool", wb_[0][:], g.w_in[l, :, C_SBV:C_SBV + 512].rearrange("(kc p) c -> p kc c", p=128), R=[g.w_in], W=[wb_[0]])
            it = 0
            for which, wt in (("k", wb_[1]), ("v", wb_[0])):
                if which == "v" and "stop_c" in g.dbg:
                    break
                for ti, (tok0, n) in enumerate(TTL):
                    if "stop_d" in g.dbg and tok0 >= L:
                        break
                    p = nps(g)
                    kb.mm(p, p[0:n, :], [(g.hT[:, kc, tok0:tok0 + n], wt[:, kc, :]) for kc in range(NKC)], R=[g.hT, wt])
                    s_ = stg[it % 2]
                    it += 1
                    kb.cp(s_[0:n, :], p[0:n, :], R=[p], W=[s_], e="act")
                    if which == "v":
                        if tok0 < L:
                            kb.cp(vtm[0:n, ti, :], p[0:n, :], R=[p], W=[vtm])
                        else:
                            kb.cp(vs[0:n, :], p[0:n, :], R=[p], W=[vs])
                    dst = (g.k_p if which == "k" else g.v_p) if tok0 < L else (g.k_s if which == "k" else g.v_s)
                    r0 = tok0 if tok0 < L else tok0 - L
                    kb.dma("sp", dst[l, r0:r0 + n, :], s_[0:n, :], R=[s_], W=[dst])
            kb.barrier()
        with ExitStack() as st2:
            NB = 4
            ex = [kb.sb(st2, "aex%d" % i, [128, 512]) for i in range(NB)]
            spb = [kb.sb(st2, "aspb%d" % i, [128, 512], BF16) for i in range(NB)]
            spm = [kb.sb(st2, "aspm%d" % i, [128, 512], BF16) for i in range(NB)]
            t1 = [kb.sb(st2, "at1%d" % i, [128, 512]) for i in range(NB)]
            ee = [kb.sb(st2, "aee%d" % i, [128, 512]) for i in range(NB)]
            att = [kb.sb(st2, "aatt%d" % i, [128, 512], BF16) for i in range(NB)]
            attm = [kb.sb(st2, "aattm%d" % i, [128, 512], BF16) for i in range(NB)]
            acc = [kb.sb(st2, "aacc%d" % i, [128, 512]) for i in range(2)]
            osb = [kb.sb(st2, "aosb%d" % i, [128, 512], BF16) for i in range(2)]

            def stream(h, qc, sid):
                po = g.ps[6 + sid]
                ac = acc[sid]
                nkb = 4 * qc + 4
                it = 0
                for kb_ in range(nkb - 1, -1, -1):
                    b = sid * 2 + it % 2
                    it += 1
                    r = kb_ - 4 * qc
                    first = (kb_ == nkb - 1)
                    pz = nps(g)
                    kb.mm(pz, pz[:, :], [(kT[:, h, kb_ * 128:(kb_ + 1) * 128], qT[:, h, qc * 512:(qc + 1) * 512])], R=[kT, qT])
                    yield
                    kb.act(ex[b][:], pz[:, :], AF.Exp, R=[pz, bsb], W=[ex[b]], bias=bsb[:, h:h + 1])
                    kb.act(spb[b][:], ex[b][:], AF.Ln, R=[ex[b]], W=[spb[b]], bias=1.0)
                    sp_ = spb[b]
                    if r >= 0:
                        kb.tt(spm[b][:], spb[b][:], am[:, r, :], ALU.mult, R=[spb[b], am], W=[spm[b]])
                        sp_ = spm[b]
                    kb.stt(t1[b][:], pz[:, :], bsb[:, h:h + 1], spb[b][:], ALU.add, ALU.subtract, R=[pz, bsb, spb[b]], W=[t1[b]])
                    yield
                    ps2 = nps(g)
                    pairs = [(ntgt[:], sp_[:])]
                    if not first:
                        pairs.append((nonesf[:], ac[:]))
                    kb.mm(ps2, ps2[:, :], pairs, R=[ntgt, nonesf, sp_, ac])
                    yield
                    kb.tt(ee[b][:], ps2[:, :], t1[b][:], ALU.add, R=[ps2, t1[b]], W=[ee[b]])
                    kb.act(att[b][:], ee[b][:], AF.Exp, R=[ee[b]], W=[att[b]])
                    a_ = att[b]
                    if r >= 0:
                        kb.tt(attm[b][:], att[b][:], am[:, r, :], ALU.mult, R=[att[b], am], W=[attm[b]])
                        a_ = attm[b]
                    if first:
                        kb.cp(ac[:], sp_[:], R=[sp_], W=[ac])
                    elif kb_ > 0:
                        kb.tt(ac[:], ac[:], sp_[:], ALU.add, R=[ac, sp_], W=[ac])
                    yield
                    kb.op("pe", lambda e, kb_=kb_, a_=a_, first=first: e.matmul(po[:, :], vtm[:, kb_, h * 128:(h + 1) * 128], a_[:],
                                                                               start=first, stop=(kb_ == 0)),
                          R=[vtm, a_], W=[po])
                    yield
                o_ = osb[sid]
                kb.cp(o_[:], po[:, :], R=[po], W=[o_], e="act")
                kb.dma("sp", g.oT[1024 + h * 128:1024 + (h + 1) * 128, qc * 512:(qc + 1) * 512], o_[:], R=[o_], W=[g.oT])

            todo = [(h, qc) for h in range(0 if "no_attn_p" in g.dbg else 4) for qc in (3, 0, 2, 1)]
            active = [None, None]
            while todo or any(a is not None for a in active):
                for sid in range(2):
                    if active[sid] is None and todo:
                        h, qc = todo.pop(0)
                        active[sid] = stream(h, qc, sid)
                    if active[sid] is not None:
                        try:
                            next(active[sid])
                        except StopIteration:
                            active[sid] = None
            kb.barrier()
        with ExitStack() as st2:
            if "no_attn_s" in g.dbg:
                return
            ptb = kb.sb(st2, "ptb", [128, 128], I32)
            kb.dma("sp", ptb[:], g.page_table[0].partition_broadcast(128), R=[g.page_table], W=[ptb])
            iop = kb.sb(st2, "iop", [128, 1], I32)
            kb.op("pool", lambda e: e.iota(iop[:], [[0, 1]], base=l * NPOOL * 128, channel_multiplier=1), W=[iop])
            idx = kb.sb(st2, "idx", [128, 128], I32)
            kb.ts(idx[:], ptb[:], 128.0, iop[:, 0:1], ALU.mult, ALU.add, R=[ptb, iop], W=[idx])
            segm = kb.sb(st2, "segm", [128, 16, 32])
            kb.op("dve", lambda e: e.memset(segm[:], 1.0), W=[segm])
            kb.op("dve", lambda e: e.memset(segm[:, :, 0:1], 0.0), W=[segm])
            carry = kb.sb(st2, "carry", [128, 16])
            kpg = [kb.sb(st2, "kpg%d" % i, [128, 512]) for i in range(4)]
            kTp = [kb.sb(st2, "kTp%d" % i, [128, 4, 128], BF16) for i in range(4)]
            vpg = kb.sb(st2, "vpg", [128, 32, 512], BF16, multi=True)
            sxe = kb.sb(st2, "sxe", [128, 512])
            zs = kb.sb(st2, "zs", [128, 512])
            ssp = kb.sb(st2, "ssp", [128, 512])
            sinc = kb.sb(st2, "sinc", [128, 512])
            st1 = kb.sb(st2, "st1", [128, 512])
            sS = kb.sb(st2, "sS", [128, 512])
            satt = kb.sb(st2, "satt", [128, 512], BF16)
            po = g.ps[6]
            kb.op("pe", lambda e: e.matmul(po[:, 0:16], zer[:, :], zer[:, 0:16], start=True, stop=False), R=[zer], W=[po])
            pzn = nps(g)
            for h in range(4):
                kb.mm(pzn, pzn[0:LS, h * 4:(h + 1) * 4], [(kT[:, h, L:NT], qT[:, h, L:NT])], R=[kT, qT])
            n_ex = kb.sb(st2, "n_ex", [4, 16])
            n_sp = kb.sb(st2, "n_sp", [4, 16])
            n_spm = kb.sb(st2, "n_spm", [4, 16])
            n_t1 = kb.sb(st2, "n_t1", [4, 16])
            n_e = kb.sb(st2, "n_e", [4, 16])
            n_att = kb.sb(st2, "n_att", [4, 16])
            n_attb = kb.sb(st2, "n_attb", [4, 16], BF16)
            am4 = g.cst[0:4, 256:260]
            m4 = kb.sb(st2, "m4", [4, 4, 4])
            for h in range(4):
                kb.cp(m4[0:4, h, :], am[0:4, 0, 0:4], R=[am], W=[m4])
            for h in range(4):
                kb.act(n_ex[:, h * 4:(h + 1) * 4], pzn[0:LS, h * 4:(h + 1) * 4], AF.Exp, R=[pzn, bsb], W=[n_ex], bias=bsb[0:LS, h:h + 1])
            kb.act(n_sp[:], n_ex[:], AF.Ln, R=[n_ex], W=[n_sp], bias=1.0)
            m4f = m4[:, :, :].rearrange("p a b -> p (a b)")
            kb.tt(n_spm[:], n_sp[:], m4f, ALU.mult, R=[n_sp, m4], W=[n_spm])
            psn = nps(g)
            kb.mm(psn, psn[0:LS, 0:16], [(ntgtf[0:LS, 0:LS], n_spm[:])], R=[ntgtf, n_spm])
            for h in range(4):
                kb.stt(n_t1[:, h * 4:(h + 1) * 4], pzn[0:LS, h * 4:(h + 1) * 4], bsb[0:LS, h:h + 1], n_sp[:, h * 4:(h + 1) * 4],
                       ALU.add, ALU.subtract, R=[pzn, bsb, n_sp], W=[n_t1])
            kb.tt(n_e[:], psn[0:LS, 0:16], n_t1[:], ALU.add, R=[psn, n_t1], W=[n_e])
            kb.act(n_att[:], n_e[:], AF.Exp, R=[n_e], W=[n_att])
            kb.tt(n_attb[:], n_att[:], m4f, ALU.mult, R=[n_att, m4], W=[n_attb])
            for h in range(4):
                kb.op("pe", lambda e, h=h: e.matmul(po[:, h * 4:(h + 1) * 4], vs[0:LS, h * 128:(h + 1) * 128], n_attb[:, h * 4:(h + 1) * 4],
                                                    start=False, stop=False), R=[vs, n_attb], W=[po])
            pcr = nps(g)
            kb.mm(pcr, pcr[:, 0:16], [(g.cst[0:LS, 384:512], n_spm[:])], R=[g.cst, n_spm])
            kb.cp(carry[:], pcr[:, 0:16], R=[pcr], W=[carry])
            pgi = 0
            for G in range(3, -1, -1):
                pz = zs
                pzv = zs[:, :].rearrange("k (h q s) -> k h q s", h=4, q=4)
                for s_ in range(32):
                    page = G * 32 + 31 - s_
                    b = pgi % 4
                    pgi += 1
                    kb.idma(kpg[b][:], g.cache_k[:, :], idx[:, page:page + 1], R=[g.cache_k, idx], W=[kpg[b]])
                    kb.idma(vpg[:, s_, :], g.cache_v[:, :], idx[:, page:page + 1], R=[g.cache_v, idx], W=[vpg])
                    pt_ = nps(g)
                    for h in range(4):
                        kb.tr(pt_, pt_[:, h * 128:(h + 1) * 128], kpg[b][:, h * 128:(h + 1) * 128], ident, R=[kpg[b], g.cst])
                    kb.cp(kTp[b][:, :, :], pt_[:, :].rearrange("p (h k) -> p h k", h=4), R=[pt_], W=[kTp[b]], e=("act" if s_ % 2 else "dve"))
                    pzp = nps(g)
                    for h in range(4):
                        kb.mm(pzp, pzp[:, h * 4:(h + 1) * 4], [(kTp[b][:, h, :], qT[:, h, L:NT])], R=[kTp[b], qT])
                    kb.cp(pzv[:, :, :, s_], pzp[:, 0:16].rearrange("k (h q) -> k h q", h=4), R=[pzp], W=[zs])
                for h in range(4):
                    kb.act(sxe[:, h * 128:(h + 1) * 128], pz[:, h * 128:(h + 1) * 128], AF.Exp, R=[pz, bsb], W=[sxe], bias=bsb[:, h:h + 1])
                kb.act(ssp[:], sxe[:], AF.Ln, R=[sxe], W=[ssp], bias=1.0)
                pw_ = nps(g)
                kb.mm(pw_, pw_[:, :], [(g.cst[:, 256:384], ssp[:])], R=[g.cst, ssp])
                pp_ = nps(g)
                kb.mm(pp_, pp_[:, :], [(g.cst[:, 384:512], ssp[:])], R=[g.cst, ssp])
                kb.op("dve", lambda e: e.tensor_tensor_scan(out=sinc[:], data0=segm[:, :, :].rearrange("p a b -> p (a b)"), data1=pp_[:, :],
                                                            initial=0.0, op0=ALU.mult, op1=ALU.add), R=[segm, pp_], W=[sinc])
                kb.tt(sS[:], sinc[:], pp_[:, :], ALU.subtract, R=[sinc, pp_], W=[sS])
                kb.tt(sS[:], sS[:], pw_[:, :], ALU.add, R=[sS, pw_], W=[sS])
                sSv = sS[:, :].rearrange("p (a b) -> p a b", b=32)
                kb.tt(sSv, sSv, carry[:, :].unsqueeze(2).to_broadcast([128, 16, 32]), ALU.add, R=[sS, carry], W=[sS])
                for h in range(4):
                    kb.stt(st1[:, h * 128:(h + 1) * 128], pz[:, h * 128:(h + 1) * 128], bsb[:, h:h + 1], ssp[:, h * 128:(h + 1) * 128],
                           ALU.add, ALU.subtract, R=[pz, bsb, ssp], W=[st1])
                kb.tt(st1[:], st1[:], sS[:], ALU.subtract, R=[st1, sS], W=[st1])
                kb.act(satt[:], st1[:], AF.Exp, R=[st1], W=[satt])
                sincv = sinc[:, :].rearrange("p (a b) -> p a b", b=32)
                kb.tt(carry[:, :].unsqueeze(2), carry[:, :].unsqueeze(2), sincv[:, :, 31:32], ALU.add, R=[carry, sinc], W=[carry])
                sattv = satt[:, :].rearrange("k (h q s) -> k h q s", h=4, q=4)
                for s_ in range(32):
                    for h in range(4):
                        kb.op("pe", lambda e, s_=s_, h=h: e.matmul(po[:, h * 4:(h + 1) * 4], vpg[:, s_, h * 128:(h + 1) * 128], sattv[:, h, :, s_],
                                                                    start=False, stop=False), R=[vpg, satt], W=[po])
            oss = kb.sb(st2, "oss", [128, 16], BF16)
            kb.cp(oss[:], po[:, 0:16], R=[po], W=[oss])
            for h in range(4):
                kb.dma("sp", g.oT[1024 + h * 128:1024 + (h + 1) * 128, L:NT], oss[:, h * 4:(h + 1) * 4], R=[oss], W=[g.oT])
            kb.barrier()


def phase_dn(g, l):
    kb = g.kb
    ident = g.cst[:, 0:128]
    triu = g.cst[:, 128:256]
    tgt = g.cst[:, 256:384]
    ones = g.cst[:, 384:512]
    identb = g.cstb[:, 0:128]
    onesb = g.cstb[:, 384:512]
    with ExitStack() as st:
        qT = kb.sb(st, "d_qT", [128, 4, NT], BF16, multi=True)
        kT = kb.sb(st, "d_kT", [128, 4, NT], BF16, multi=True)
        ktm = kb.sb(st, "d_ktm", [128, 17, 512], BF16, multi=True)
        vtm = kb.sb(st, "d_vtm", [128, 17, 512], BF16, multi=True)
        bg = kb.sb(st, "d_bg", [128, 17, 8], multi=True)
        with ExitStack() as st1:
            cwt, hfm = load_conv_w(g, st1, g.dn_conv_w, None, g.state_dn_conv, l, "d_cw")
            wb_ = [kb.sb(st1, "d_w%d" % i, [128, NKC, 512], BF16) for i in range(2)]
            ext = kb.sb(st1, "d_ext", [128, PW])
            tmp = kb.sb(st1, "d_tmp", [128, PW])
            xc = kb.sb(st1, "d_xc", [128, PW])
            sqb = kb.sb(st1, "d_sqb", [128, PW], BF16)
            rn = [kb.sb(st1, "d_rn%d" % i, [128, 512]) for i in range(2)]
            cn = kb.sb(st1, "d_cn", [4, 512])
            it = 0
            for blk in range(3):
                wt = wb_[blk % 2]
                kb.dma("pool", wt[:], g.w_in[l, :, C_DNQKV + blk * 512:C_DNQKV + (blk + 1) * 512].rearrange("(kc p) c -> p kc c", p=128),
                       R=[g.w_in], W=[wt])
                for h in range(4):
                    j = blk * 4 + h
                    conv_fm(g, wt, h * 128, cwt[:, j, :], (hfm, hfm[:, j, :]), ext, tmp, xc)
                    if blk < 2:
                        kb.act(sqb[:, 16:PW], xc[:, 16:PW], AF.Square, R=[xc], W=[sqb])
                        for (tok0, n) in TCH:
                            c = ucol(tok0)
                            p = nps(g)
                            kb.mm(p, p[:, 0:n], [(onesb, sqb[:, c:c + n])], R=[g.cstb, sqb])
                            r_ = rn[it % 2]
                            it += 1
                            kb.act(r_[:, 0:n], p[:, 0:n], AF.Ln, R=[p], W=[r_], bias=g.eps_t[:, 0:1])
                            kb.act(r_[:, 0:n], r_[:, 0:n], AF.Exp, R=[r_], W=[r_], scale=-0.5)
                            if blk == 0:
                                kb.stt(qT[:, h, tok0:tok0 + n], xc[:, c:c + n], 128.0 ** -0.5, r_[:, 0:n], ALU.mult, ALU.mult, R=[xc, r_], W=[qT])
                            else:
                                kb.tt(xc[:, c:c + n], xc[:, c:c + n], r_[:, 0:n], ALU.mult, R=[xc, r_], W=[xc])
                                kb.cp(kT[:, h, tok0:tok0 + n], xc[:, c:c + n], R=[xc], W=[kT], e="act")
                        if blk == 1:
                            fm_to_tm(g, xc, lambda ti, n, h=h: (ktm, ktm[0:n, ti, h * 128:(h + 1) * 128]), h)
                    else:
                        fm_to_tm(g, xc, lambda ti, n, h=h: (vtm, vtm[0:n, ti, h * 128:(h + 1) * 128]), h)
                p = nps(g)
                kb.mm(p, p[0:3, :], [(g.hT[:, kc, L - 3:L], wt[:, kc, :]) for kc in range(NKC)], R=[g.hT, wt])
                kb.cp(cn[0:3, :], p[0:3, :], R=[p], W=[cn])
                kb.dma("sp", g.dnc_p[l, :, blk * 512:(blk + 1) * 512], cn[0:3, :], R=[cn], W=[g.dnc_p])
                p = nps(g)
                kb.mm(p, p[0:LS, :], [(g.hT[:, kc, L:NT], wt[:, kc, :]) for kc in range(NKC)], R=[g.hT, wt])
                kb.cp(cn[0:LS, :], p[0:LS, :], R=[p], W=[cn])
                kb.dma("sp", g.dnc_s[l, :, blk * 512:(blk + 1) * 512], cn[1:LS, :], R=[cn], W=[g.dnc_s])
            wbg = kb.sb(st1, "d_wbg", [128, NKC, 8], BF16)
            kb.dma("pool", wbg[:], g.w_in[l, :, C_DNB:C_DNB + 8].rearrange("(kc p) c -> p kc c", p=128), R=[g.w_in], W=[wbg])
            nea = kb.sb(st1, "d_nea", [128, 4])
            kb.dma("sp", nea[:], g.dn_a_log[l].partition_broadcast(128), R=[g.dn_a_log], W=[nea])
            kb.act(nea[:], nea[:], AF.Exp, R=[nea], W=[nea])
            kb.ts(nea[:], nea[:], -1.0, None, ALU.mult, None, R=[nea], W=[nea])
            dtb = kb.sb(st1, "d_dtb", [128, 4])
            kb.dma("sp", dtb[:], g.dn_dt_bias[l].partition_broadcast(128), R=[g.dn_dt_bias], W=[dtb])
            for ti, (tok0, n) in enumerate(TTL):
                p = nps(g)
                kb.mm(p, p[0:n, 0:8], [(g.hT[:, kc, tok0:tok0 + n], wbg[:, kc, :]) for kc in range(NKC)], R=[g.hT, wbg])
                kb.cp(bg[0:n, ti, 0:4], p[0:n, 0:4], R=[p], W=[bg])
                kb.tt(bg[0:n, ti, 4:8], p[0:n, 4:8], dtb[0:n, :], ALU.add, R=[p, dtb], W=[bg])
            kb.act(bg[:, :, 0:4], bg[:, :, 0:4], AF.Sigmoid, R=[bg], W=[bg])
            kb.act(bg[:, :, 4:8], bg[:, :, 4:8], AF.Exp, R=[bg], W=[bg])
            kb.act(bg[:, :, 4:8], bg[:, :, 4:8], AF.Ln, R=[bg], W=[bg], bias=1.0)
            kb.tt(bg[:, :, 4:8], bg[:, :, 4:8], nea[:, :].unsqueeze(1).to_broadcast([128, 17, 4]), ALU.mult, R=[bg, nea], W=[bg])
            kb.barrier()
        with ExitStack() as st2:
            wz = kb.sb(st2, "d_wz", [128, NKC, 512], BF16)
            kb.dma("pool", wz[:], g.w_in[l, :, C_DNZ:C_DNZ + 512].rearrange("(kc p) c -> p kc c", p=128), R=[g.w_in], W=[wz])
            nwb = kb.sb(st2, "d_nwb", [128, 4, 128])
            for h in range(4):
                kb.dma("sp", nwb[:, h, :], g.dn_norm_w[l].partition_broadcast(128), R=[g.dn_norm_w], W=[nwb])
            F = lambda nm: kb.sb(st2, nm, [128, 4, 128])
            Bt = lambda nm: kb.sb(st2, nm, [128, 4, 128], BF16)
            Wg, A1, A2, Nf = F("d_Wg"), F("d_A1"), F("d_A2"), F("d_Nf")
            curM = [F("d_M0"), F("d_M1")]
            curN = [F("d_N0"), F("d_N1")]
            Pp = [F("d_P0"), F("d_P1")]
            Qq = [F("d_Q0"), F("d_Q1")]
            attnT, sz, vn16 = Bt("d_at"), Bt("d_sz"), Bt("d_vn16")
            RHSv, RHSk, kd, nwc, vnb = F("d_rv"), F("d_rk"), F("d_kd"), F("d_nwc"), F("d_vn")
            S = F("d_S")
            Sb = Bt("d_Sb")
            oo, ty = F("d_oo"), F("d_ty")
            gcum = kb.sb(st2, "d_gcum", [128, 4])
            gtot = kb.sb(st2, "d_gtot", [128, 4])
            eg = kb.sb(st2, "d_eg", [128, 4])
            bge = kb.sb(st2, "d_bge", [128, 4])
            egd = kb.sb(st2, "d_egd", [128, 4])
            egt = kb.sb(st2, "d_egt", [128, 4])
            ss = kb.sb(st2, "d_ss", [128, 4])
            junk = kb.sb(st2, "d_junk", [128, 128], BF16)
            ostg = [kb.sb(st2, "d_ostg%d" % i, [128, 4, 128], BF16) for i in range(2)]
            for ti, (tok0, n) in enumerate(TTL):
                if ti == 0:
                    kb.op("dve", lambda e: e.memset(S[:], 0.0), W=[S])
                    kb.op("dve", lambda e: e.memset(Sb[:], 0.0), W=[Sb])
                if ti == 16:
                    kb.dma("sp", g.dns_p[l].rearrange("h k v -> k h v"), S[:, :, :], R=[S], W=[g.dns_p])
                    kb.dma("sp", S[:, :, :], g.state_dn_s[l].rearrange("h k v -> k h v"), R=[g.state_dn_s], W=[S])
                    kb.cp(Sb[:], S[:], R=[S], W=[Sb])
                gq = bg[0:n, ti, 4:8]
                be = bg[0:n, ti, 0:4]
                p = nps(g)
                kb.mm(p, p[0:n, 0:4], [(triu[0:n, 0:n], gq)], R=[g.cst, bg])
                kb.cp(gcum[0:n, :], p[0:n, 0:4], R=[p], W=[gcum])
                p = nps(g)
                kb.mm(p, p[:, 0:4], [(ones[0:n, :], gq)], R=[g.cst, bg])
                kb.cp(gtot[:], p[:, 0:4], R=[p], W=[gtot])
                kb.tt(Wg[0:n, :, 0:n], triu[0:n, 0:n].unsqueeze(1).to_broadcast([n, 4, n]), bc3(gq, n, n), ALU.mult, R=[g.cst, bg], W=[Wg])
                pr = nps(g)
                prv = pr[0:n, 0:4 * n].rearrange("p (h t) -> p h t", t=n)
                kb.mm(pr, prv, [(ones[0:n, 0:n], Wg[0:n, :, 0:n])], R=[g.cst, Wg])
                for h in range(4):
                    kb.ts(A1[0:n, h, 0:n], prv[:, h, :], gcum[0:n, h:h + 1], 0.0, ALU.subtract, ALU.max, R=[pr, gcum], W=[A1])
                    kb.ts(A2[0:n, h, 0:n], prv[:, h, :], gcum[0:n, h:h + 1], 0.0, ALU.subtract, ALU.min, R=[pr, gcum], W=[A2])
                kb.act(A1[0:n, :, 0:n], A1[0:n, :, 0:n], AF.Exp, R=[A1], W=[A1], scale=-1.0)
                kb.act(A2[0:n, :, 0:n], A2[0:n, :, 0:n], AF.Exp, R=[A2], W=[A2])
                kb.tt(A1[0:n, :, 0:n], A1[0:n, :, 0:n], tgt[0:n, 0:n].unsqueeze(1).to_broadcast([n, 4, n]), ALU.mult, R=[A1, g.cst], W=[A1])
                kb.tt(A2[0:n, :, 0:n], A2[0:n, :, 0:n], triu[0:n, 0:n].unsqueeze(1).to_broadcast([n, 4, n]), ALU.mult, R=[A2, g.cst], W=[A2])
                pk = nps(g)
                pkv = pk[0:n, 0:4 * n].rearrange("p (h t) -> p h t", t=n)
                for h in range(4):
                    kb.mm(pk, pkv[:, h, :], [(kT[:, h, tok0:tok0 + n], kT[:, h, tok0:tok0 + n])], R=[kT])
                kb.stt(Nf[0:n, :, 0:n], pkv, -1.0, A1[0:n, :, 0:n], ALU.mult, ALU.mult, R=[pk, A1], W=[Nf])
                kb.tt(Nf[0:n, :, 0:n], Nf[0:n, :, 0:n], bc3(be, n, n), ALU.mult, R=[Nf, bg], W=[Nf])
                kb.cp(curN[0][0:n, :, 0:n], Nf[0:n, :, 0:n], R=[Nf], W=[curN[0]], e="act")
                pm = nps(g)
                pmv = pm[0:n, 0:4 * n].rearrange("p (h t) -> p h t", t=n)
                for h in range(4):
                    kb.tr(pm, pmv[:, h, :], Nf[0:n, h, 0:n], ident[0:n, 0:n], R=[Nf, g.cst])
                kb.cp(curM[0][0:n, :, 0:n], pmv, R=[pm], W=[curM[0]])
                pq = nps(g)
                pqv = pq[0:n, 0:4 * n].rearrange("p (h t) -> p h t", t=n)
                for h in range(4):
                    kb.mm(pq, pqv[:, h, :], [(kT[:, h, tok0:tok0 + n], qT[:, h, tok0:tok0 + n])], R=[kT, qT])
                kb.tt(attnT[0:n, :, 0:n], pqv, A2[0:n, :, 0:n], ALU.mult, R=[pq, A2], W=[attnT])
                idb = ident[0:n, 0:n].unsqueeze(1).to_broadcast([n, 4, n])
                kb.tt(Pp[0][0:n, :, 0:n], curM[0][0:n, :, 0:n], idb, ALU.add, R=[curM[0], g.cst], W=[Pp[0]])
                kb.tt(Qq[0][0:n, :, 0:n], curN[0][0:n, :, 0:n], idb, ALU.add, R=[curN[0], g.cst], W=[Qq[0]])
                nsq = 0
                while (2 << nsq) < n:
                    nsq += 1
                cp_, cq_, cm_, cn_ = Pp[0], Qq[0], curM[0], curN[0]
                for k in range(nsq):
                    lastk = (k == nsq - 1)
                    nm_, nn_ = curM[(k + 1) % 2], curN[(k + 1) % 2]
                    np_, nq_ = Pp[(k + 1) % 2], Qq[(k + 1) % 2]
                    pa = nps(g)
                    pav = pa[0:n, 0:4 * n].rearrange("p (h t) -> p h t", t=n)
                    for h in range(4):
                        kb.mm(pa, pav[:, h, :], [(cn_[0:n, h, 0:n], cm_[0:n, h, 0:n])], R=[cn_, cm_])
                    kb.cp(nm_[0:n, :, 0:n], pav, R=[pa], W=[nm_], e="act")
                    if not lastk:
                        pb = nps(g)
                        pbv = pb[0:n, 0:4 * n].rearrange("p (h t) -> p h t", t=n)
                        for h in range(4):
                            kb.mm(pb, pbv[:, h, :], [(cm_[0:n, h, 0:n], cn_[0:n, h, 0:n])], R=[cn_, cm_])
                        kb.cp(nn_[0:n, :, 0:n], pbv, R=[pb], W=[nn_], e="act")
                    pc = nps(g)
                    pcv = pc[0:n, 0:4 * n].rearrange("p (h t) -> p h t", t=n)
                    for h in range(4):
                        kb.mm(pc, pcv[:, h, :], [(cq_[0:n, h, 0:n], nm_[0:n, h, 0:n])], R=[cq_, nm_])
                    kb.tt(np_[0:n, :, 0:n], pcv, cp_[0:n, :, 0:n], ALU.add, R=[pc, cp_], W=[np_])
                    if not lastk:
                        pd = nps(g)
                        pdv = pd[0:n, 0:4 * n].rearrange("p (h t) -> p h t", t=n)
                        for h in range(4):
                            kb.mm(pd, pdv[:, h, :], [(cp_[0:n, h, 0:n], nn_[0:n, h, 0:n])], R=[cp_, nn_])
                        kb.tt(nq_[0:n, :, 0:n], pdv, cq_[0:n, :, 0:n], ALU.add, R=[pd, cq_], W=[nq_])
                    cp_, cq_, cm_, cn_ = np_, nq_, nm_, nn_
                TT = cp_
                kb.act(eg[0:n, :], gcum[0:n, :], AF.Exp, R=[gcum], W=[eg])
                kb.tt(bge[0:n, :], eg[0:n, :], be, ALU.mult, R=[eg, bg], W=[bge])
                kb.tt(egd[0:n, :], gtot[0:n, :], gcum[0:n, :], ALU.subtract, R=[gtot, gcum], W=[egd])
                kb.act(egd[0:n, :], egd[0:n, :], AF.Exp, R=[egd], W=[egd])
                kb.act(egt[:], gtot[:], AF.Exp, R=[gtot], W=[egt])
                vv = vtm[0:n, ti, :].rearrange("p (h d) -> p h d", d=128)
                kv = ktm[0:n, ti, :].rearrange("p (h d) -> p h d", d=128)
                kb.tt(RHSv[0:n, :, :], vv, bc3(be, n, 128), ALU.mult, R=[vtm, bg], W=[RHSv])
                kb.tt(RHSk[0:n, :, :], kv, bc3(bge[0:n, :], n, 128), ALU.mult, R=[ktm, bge], W=[RHSk])
                kb.tt(kd[0:n, :, :], kv, bc3(egd[0:n, :], n, 128), ALU.mult, R=[ktm, egd], W=[kd])
                pw_ = nps(g)
                pwv = pw_[:, 0:4 * n].rearrange("p (h t) -> p h t", t=n)
                for h in range(4):
                    kb.mm(pw_, pwv[:, h, :], [(RHSk[0:n, h, :], TT[0:n, h, 0:n])], R=[RHSk, TT])
                kb.ts(nwc[:, :, 0:n], pwv, -1.0, None, ALU.mult, None, R=[pw_], W=[nwc])
                pv_ = nps(g)
                for h in range(4):
                    kb.mm(pv_, pv_[0:n, h * 128:(h + 1) * 128], [(TT[0:n, h, 0:n], RHSv[0:n, h, :]), (nwc[:, h, 0:n], S[:, h, :])],
                          R=[TT, RHSv, nwc, S])
                kb.cp(vnb[0:n, :, :], pv_[0:n, :].rearrange("p (h d) -> p h d", d=128), R=[pv_], W=[vnb], e="act")
                kb.cp(vn16[0:n, :, :], vnb[0:n, :, :], R=[vnb], W=[vn16])
                po1 = nps(g)
                for h in range(4):
                    kb.mm(po1, po1[0:n, h * 128:(h + 1) * 128], [(qT[:, h, tok0:tok0 + n], Sb[:, h, :])], R=[qT, Sb])
                po2 = nps(g)
                for h in range(4):
                    kb.mm(po2, po2[0:n, h * 128:(h + 1) * 128], [(attnT[0:n, h, 0:n], vn16[0:n, h, :])], R=[attnT, vn16])
                kb.tt(ty[0:n, :, :], po1[0:n, :].rearrange("p (h d) -> p h d", d=128), bc3(eg[0:n, :], n, 128), ALU.mult, R=[po1, eg], W=[ty])
                kb.tt(oo[0:n, :, :], po2[0:n, :].rearrange("p (h d) -> p h d", d=128), ty[0:n, :, :], ALU.add, R=[po2, ty], W=[oo])
                psu = nps(g)
                for h in range(4):
                    kb.mm(psu, psu[:, h * 128:(h + 1) * 128], [(kd[0:n, h, :], vnb[0:n, h, :])], R=[kd, vnb])
                kb.tt(S[:, :, :], S[:, :, :], bc3(egt[:, :], 128, 128), ALU.mult, R=[S, egt], W=[S])
                kb.tt(S[:, :, :], S[:, :, :], psu[:, :].rearrange("p (h d) -> p h d", d=128), ALU.add, R=[S, psu], W=[S])
                kb.cp(Sb[:], S[:], R=[S], W=[Sb], e="act")
                pzz = nps(g)
                kb.mm(pzz, pzz[0:n, :], [(g.hT[:, kc, tok0:tok0 + n], wz[:, kc, :]) for kc in range(NKC)], R=[g.hT, wz])
                kb.act(sz[0:n, :, :], pzz[0:n, :].rearrange("p (h d) -> p h d", d=128), AF.Silu, R=[pzz], W=[sz])
                for h in range(4):
                    kb.act(junk[0:n, :], oo[0:n, h, :], AF.Square, R=[oo], W=[junk, ss], accum_out=ss[0:n, h:h + 1])
                kb.ts(ss[0:n, :], ss[0:n, :], 1.0 / 128, EPS, ALU.mult, ALU.add, R=[ss], W=[ss])
                kb.act(ss[0:n, :], ss[0:n, :], AF.Sqrt, R=[ss], W=[ss])
                kb.op("dve", lambda e: e.reciprocal(out=ss[0:n, :], in_=ss[0:n, :]), R=[ss], W=[ss])
                kb.tt(oo[0:n, :, :], oo[0:n, :, :], bc3(ss[0:n, :], n, 128), ALU.mult, R=[oo, ss], W=[oo])
                kb.tt(oo[0:n, :, :], oo[0:n, :, :], nwb[0:n, :, :], ALU.mult, R=[oo, nwb], W=[oo])
                kb.tt(oo[0:n, :, :], oo[0:n, :, :], sz[0:n, :, :], ALU.mult, R=[oo, sz], W=[oo])
                og = ostg[ti % 2]
                p = nps(g)
                for h in range(4):
                    kb.tr(p, p[:, h * 128:h * 128 + n], oo[0:n, h, :], ident[0:n, 0:n], R=[oo, g.cst])
                kb.cp(og[:, :, 0:n], p[:, :].rearrange("p (a b) -> p a b", b=128)[:, :, 0:n], R=[p], W=[og])
                kb.dma("sp", g.oT[512:1024, tok0:tok0 + n].rearrange("(j p) t -> p j t", p=128), og[:, :, 0:n], R=[og], W=[g.oT])
            kb.dma("sp", g.dns_s[l].rearrange("h k v -> k h v"), S[:, :, :], R=[S], W=[g.dns_s])
            kb.barrier()


def load_conv_w(g, st, cw_dram, cb_dram, hist_dram, l, name):
    kb = g.kb
    ident = g.cst[:, 0:128]
    cwt = kb.sb(st, name + "t", [128, 12, 5])
    hfm = kb.sb(st, name + "h", [128, 12, 3])
    with ExitStack() as stx:
        cwl = kb.sb(stx, name + "l", [5, 1536])
        kb.op("dve", lambda e: e.memset(cwl[:], 0.0), W=[cwl])
        kb.dma("sp", cwl[0:4, :], cw_dram[l], R=[cw_dram], W=[cwl])
        if cb_dram is not None:
            kb.dma("sp", cwl[4:5, :], cb_dram[l:l + 1, :], R=[cb_dram], W=[cwl])
        sct = kb.sb(stx, name + "s", [3, 1536])
        kb.dma("sp", sct[:], hist_dram[l], R=[hist_dram], W=[sct])
        p = nps(g)
        for j in range(12):
            kb.tr(p, p[:, j * 5:(j + 1) * 5], cwl[0:5, j * 128:(j + 1) * 128], ident[0:5, 0:5], R=[cwl, g.cst])
        kb.cp(cwt[:, :, :], p[:, 0:60].rearrange("p (j i) -> p j i", i=5), R=[p], W=[cwt])
        p = nps(g)
        for j in range(12):
            kb.tr(p, p[:, j * 3:(j + 1) * 3], sct[0:3, j * 128:(j + 1) * 128], ident[0:3, 0:3], R=[sct, g.cst])
        kb.cp(hfm[:, :, :], p[:, 0:36].rearrange("p (j i) -> p j i", i=3), R=[p], W=[hfm])
        kb.barrier()
    return cwt, hfm


def conv_fm(g, wt, c0, cw, hist, ext, tmp, out):
    kb = g.kb
    ident = g.cst[:, 0:128]
    kb.op("dve", lambda e: e.memset(ext[:, 0:16], 0.0), W=[ext])
    kb.cp(ext[:, PS0 + 13:PS0 + 16], hist[1], R=[hist[0]], W=[ext])
    proj_fm(g, wt, c0, lambda tok0, n: (ext, ext[:, ucol(tok0):ucol(tok0) + n]))
    W_ = PW - 16
    kb.ts(tmp[:, 16:PW], ext[:, 13:13 + W_], cw[:, 0:1], cw[:, 4:5], ALU.mult, ALU.add, R=[ext], W=[tmp])
    for i in (1, 2, 3):
        kb.stt(tmp[:, 16:PW], ext[:, 13 + i:13 + i + W_], cw[:, i:i + 1], tmp[:, 16:PW], ALU.mult, ALU.add, R=[ext, tmp], W=[tmp])
    kb.act(out[:, 16:PW], tmp[:, 16:PW], AF.Silu, R=[tmp], W=[out])


def fm_to_tm(g, src, dst_fn, evi=0):
    kb = g.kb
    ident = g.cst[:, 0:128]
    for q4 in range(0, 17, 4):
        p = nps(g)
        tl = TTL[q4:q4 + 4]
        for i, (tok0, n) in enumerate(tl):
            c = ucol(tok0)
            kb.tr(p, p[0:n, i * 128:(i + 1) * 128], src[:, c:c + n], ident, R=[src, g.cst])
        for i, (tok0, n) in enumerate(tl):
            dT, dap = dst_fn(q4 + i, n)
            kb.cp(dap, p[0:n, i * 128:(i + 1) * 128], R=[p], W=[dT], e=("act" if (i + evi) % 2 else "dve"))


def bc3(ap2, n, k):
    return ap2.unsqueeze(2).to_broadcast([n, ap2.shape[1], k])


def phase_ssd(g, l):
    kb = g.kb
    ident = g.cst[:, 0:128]
    triu = g.cst[:, 128:256]
    ones = g.cst[:, 384:512]
    with ExitStack() as st:
        xtm = kb.sb(st, "s_xtm", [128, 17, 1024], BF16, multi=True)
        BmT = kb.sb(st, "s_BmT", [128, 2, NT], BF16, multi=True)
        CmT = kb.sb(st, "s_CmT", [128, 2, NT], BF16, multi=True)
        Btm = kb.sb(st, "s_Btm", [128, 17, 256], BF16, multi=True)
        dtt = kb.sb(st, "s_dt", [128, 17, 16], multi=True)
        dtA = kb.sb(st, "s_dtA", [128, 17, 16], multi=True)
        abc = kb.sb(st, "s_abc", [128, 16])
        kb.dma("sp", abc[:], g.ssm_a_log[l].partition_broadcast(128), R=[g.ssm_a_log], W=[abc])
        kb.act(abc[:], abc[:], AF.Exp, R=[abc], W=[abc])
        kb.ts(abc[:], abc[:], -1.0, None, ALU.mult, None, R=[abc], W=[abc])
        dtb = kb.sb(st, "s_dtb", [128, 16])
        kb.dma("sp", dtb[:], g.ssm_dt_bias[l].partition_broadcast(128), R=[g.ssm_dt_bias], W=[dtb])
        dsk = kb.sb(st, "s_dsk", [128, 16])
        kb.dma("sp", dsk[:], g.ssm_d[l].partition_broadcast(128), R=[g.ssm_d], W=[dsk])
        with ExitStack() as st1:
            cwt, hfm = load_conv_w(g, st1, g.ssm_conv_w, g.ssm_conv_b, g.state_ssm_conv, l, "s_cw")
            wb_ = [kb.sb(st1, "s_w%d" % i, [128, NKC, 512], BF16) for i in range(2)]
            ext = kb.sb(st1, "s_ext", [128, PW])
            tmp = kb.sb(st1, "s_tmp", [128, PW])
            xc = kb.sb(st1, "s_xc", [128, PW])
            cn = kb.sb(st1, "s_cn", [4, 512])
            for blk in range(3):
                wt = wb_[blk % 2]
                kb.dma("pool", wt[:], g.w_in[l, :, C_SSX + blk * 512:C_SSX + (blk + 1) * 512].rearrange("(kc p) c -> p kc c", p=128),
                       R=[g.w_in], W=[wt])
                for jj in range(4):
                    j = blk * 4 + jj
                    conv_fm(g, wt, jj * 128, cwt[:, j, :], (hfm, hfm[:, j, :]), ext, tmp, xc)
                    if j < 8:
                        fm_to_tm(g, xc, lambda ti, n, j=j: (xtm, xtm[0:n, ti, j * 128:(j + 1) * 128]), j)
                    elif j < 10:
                        gq = j - 8
                        for (tok0, n) in TCH:
                            kb.cp(BmT[:, gq, tok0:tok0 + n], xc[:, ucol(tok0):ucol(tok0) + n], R=[xc], W=[BmT], e="act")
                        fm_to_tm(g, xc, lambda ti, n, gq=gq: (Btm, Btm[0:n, ti, gq * 128:(gq + 1) * 128]), j)
                    else:
                        gq = j - 10
                        for (tok0, n) in TCH:
                            kb.cp(CmT[:, gq, tok0:tok0 + n], xc[:, ucol(tok0):ucol(tok0) + n], R=[xc], W=[CmT], e="act")
                p = nps(g)
                kb.mm(p, p[0:3, :], [(g.hT[:, kc, L - 3:L], wt[:, kc, :]) for kc in range(NKC)], R=[g.hT, wt])
                kb.cp(cn[0:3, :], p[0:3, :], R=[p], W=[cn])
                kb.dma("sp", g.ssc_p[l, :, blk * 512:(blk + 1) * 512], cn[0:3, :], R=[cn], W=[g.ssc_p])
                p = nps(g)
                kb.mm(p, p[0:LS, :], [(g.hT[:, kc, L:NT], wt[:, kc, :]) for kc in range(NKC)], R=[g.hT, wt])
                kb.cp(cn[0:LS, :], p[0:LS, :], R=[p], W=[cn])
                kb.dma("sp", g.ssc_s[l, :, blk * 512:(blk + 1) * 512], cn[1:LS, :], R=[cn], W=[g.ssc_s])
            wdt = kb.sb(st1, "s_wdt", [128, NKC, 16], BF16)
            kb.dma("pool", wdt[:], g.w_in[l, :, C_SSDT:C_SSDT + 16].rearrange("(kc p) c -> p kc c", p=128), R=[g.w_in], W=[wdt])
            for ti, (tok0, n) in enumerate(TTL):
                p = nps(g)
                kb.mm(p, p[0:n, 0:16], [(g.hT[:, kc, tok0:tok0 + n], wdt[:, kc, :]) for kc in range(NKC)], R=[g.hT, wdt])
                kb.tt(dtt[0:n, ti, :], p[0:n, 0:16], dtb[0:n, :], ALU.add, R=[p, dtb], W=[dtt])
            kb.act(dtt[:, :, :], dtt[:, :, :], AF.Exp, R=[dtt], W=[dtt])
            kb.act(dtt[:, :, :], dtt[:, :, :], AF.Ln, R=[dtt], W=[dtt], bias=1.0)
            kb.tt(dtA[:, :, :], dtt[:, :, :], abc[:, :].unsqueeze(1).to_broadcast([128, 17, 16]), ALU.mult, R=[dtt, abc], W=[dtA])
            kb.barrier()
        with ExitStack() as st2:
            wz = [kb.sb(st2, "s_wz%d" % i, [128, NKC, 512], BF16) for i in range(2)]
            for i in range(2):
                kb.dma("pool", wz[i][:], g.w_in[l, :, C_SSZ + i * 512:C_SSZ + (i + 1) * 512].rearrange("(kc p) c -> p kc c", p=128),
                       R=[g.w_in], W=[wz[i]])
            nwb = kb.sb(st2, "s_nwb", [128, 1024])
            kb.dma("sp", nwb[:], g.ssm_norm_w[l].partition_broadcast(128), R=[g.ssm_norm_w], W=[nwb])
            Wm = kb.sb(st2, "s_Wm", [128, 4, 128])
            Dm = kb.sb(st2, "s_Dm", [128, 16, 128])
            wts = kb.sb(st2, "s_wts", [128, 16, 128], BF16)
            cbm = kb.sb(st2, "s_cbm", [128, 2, 128])
            xdt = kb.sb(st2, "s_xdt", [128, 1024], BF16)
            xs2 = kb.sb(st2, "s_xs2", [128, 1024], BF16)
            yy = kb.sb(st2, "s_y", [128, 1024])
            ty = kb.sb(st2, "s_ty", [128, 1024])
            sz = kb.sb(st2, "s_sz", [128, 1024], BF16)
            hst = kb.sb(st2, "s_hst", [128, 1024])
            hstb = kb.sb(st2, "s_hstb", [128, 1024], BF16)
            cum = kb.sb(st2, "s_cum", [128, 16])
            totb = kb.sb(st2, "s_tot", [128, 16])
            ecum = kb.sb(st2, "s_ecum", [128, 16])
            edl = kb.sb(st2, "s_edl", [128, 16])
            elast = kb.sb(st2, "s_elast", [128, 16])
            ss = kb.sb(st2, "s_ss", [128, 2])
            junk = kb.sb(st2, "s_junk", [128, 512], BF16)
            ostg = [kb.sb(st2, "s_ostg%d" % i, [128, 8, 128], BF16) for i in range(2)]
            hin = Dm
            hout = Dm
            for ti, (tok0, n) in enumerate(TTL):
                if ti == 0:
                    kb.op("dve", lambda e: e.memset(hst[:], 0.0), W=[hst])
                    kb.op("dve", lambda e: e.memset(hstb[:], 0.0), W=[hstb])
                if ti == 16:
                    for half in range(2):
                        p = nps(g)
                        for i in range(4):
                            j = half * 4 + i
                            kb.tr(p, p[:, i * 128:(i + 1) * 128], hst[:, j * 128:(j + 1) * 128], ident, R=[hst, g.cst])
                        kb.cp(hout[:, 8 + half * 4:8 + half * 4 + 4, :], p[:, :].rearrange("p (a b) -> p a b", b=128), R=[p], W=[hout])
                    kb.dma("sp", g.ssh_p[l].rearrange("(j p) n -> p j n", p=128), hout[:, 8:16, :], R=[hout], W=[g.ssh_p])
                    kb.dma("sp", hin[:, 0:8, :], g.state_ssm_h[l].rearrange("(j p) n -> p j n", p=128), R=[g.state_ssm_h], W=[hin])
                    for half in range(2):
                        p = nps(g)
                        for i in range(4):
                            j = half * 4 + i
                            kb.tr(p, p[:, i * 128:(i + 1) * 128], hin[:, j, :], ident, R=[hin, g.cst])
                        kb.cp(hst[:, half * 512:(half + 1) * 512], p[:, :], R=[p], W=[hst])
                    kb.cp(hstb[:], hst[:], R=[hst], W=[hstb])
                dA = dtA[0:n, ti, :]
                p = nps(g)
                kb.mm(p, p[0:n, 0:16], [(triu[0:n, 0:n], dA)], R=[g.cst, dtA])
                kb.cp(cum[0:n, :], p[0:n, 0:16], R=[p], W=[cum])
                p = nps(g)
                kb.mm(p, p[:, 0:16], [(ones[0:n, :], dA)], R=[g.cst, dtA])
                kb.cp(totb[:], p[:, 0:16], R=[p], W=[totb])
                hpb = 4
                for q in range(0, 16, hpb):
                    kb.tt(Wm[0:n, :, 0:n], triu[0:n, 0:n].unsqueeze(1).to_broadcast([n, hpb, n]), bc3(dtA[0:n, ti, q:q + hpb], n, n), ALU.mult,
                          R=[g.cst, dtA], W=[Wm])
                    p = nps(g)
                    pv = p[0:n, 0:hpb * n].rearrange("p (h t) -> p h t", t=n)
                    kb.mm(p, pv, [(ones[0:n, 0:n], Wm[0:n, :, 0:n])], R=[g.cst, Wm])
                    for hh in range(hpb):
                        h = q + hh
                        kb.ts(Dm[0:n, h, 0:n], pv[:, hh, :], cum[0:n, h:h + 1], 0.0, ALU.subtract, ALU.min, R=[p, cum], W=[Dm])
                kb.act(Dm[0:n, :, 0:n], Dm[0:n, :, 0:n], AF.Exp, R=[Dm], W=[Dm])
                p = nps(g)
                for gq in range(2):
                    kb.mm(p, p[0:n, gq * 128:gq * 128 + n], [(BmT[:, gq, tok0:tok0 + n], CmT[:, gq, tok0:tok0 + n])], R=[BmT, CmT])
                kb.tt(cbm[0:n, :, 0:n], p[0:n, 0:256].rearrange("p (g t) -> p g t", t=128)[:, :, 0:n],
                      triu[0:n, 0:n].unsqueeze(1).to_broadcast([n, 2, n]), ALU.mult, R=[p, g.cst], W=[cbm])
                for gq in range(2):
                    kb.tt(wts[0:n, gq * 8:(gq + 1) * 8, 0:n], Dm[0:n, gq * 8:(gq + 1) * 8, 0:n],
                          cbm[0:n, gq:gq + 1, 0:n].to_broadcast([n, 8, n]), ALU.mult, R=[Dm, cbm], W=[wts])
                xv = xtm[0:n, ti, :].rearrange("p (h q) -> p h q", q=64)
                kb.tt(xdt[0:n, :].rearrange("p (h q) -> p h q", q=64), xv, bc3(dtt[0:n, ti, :], n, 64), ALU.mult, R=[xtm, dtt], W=[xdt])
                pyA = nps(g)
                pyB = nps(g)
                for h in range(16):
                    py = pyA if h < 8 else pyB
                    kb.mm(py, py[0:n, (h % 8) * 64:(h % 8 + 1) * 64], [(wts[0:n, h, 0:n], xdt[0:n, h * 64:(h + 1) * 64])], R=[wts, xdt])
                kb.act(ecum[0:n, :], cum[0:n, :], AF.Exp, R=[cum], W=[ecum])
                for gq in range(2):
                    pi = nps(g)
                    kb.mm(pi, pi[0:n, :], [(CmT[:, gq, tok0:tok0 + n], hstb[:, gq * 512:(gq + 1) * 512])], R=[CmT, hstb])
                    kb.tt(ty[0:n, gq * 512:(gq + 1) * 512].rearrange("p (h q) -> p h q", q=64), pi[0:n, :].rearrange("p (h q) -> p h q", q=64),
                          bc3(ecum[0:n, gq * 8:(gq + 1) * 8], n, 64), ALU.mult, R=[pi, ecum], W=[ty])
                    py = pyA if gq == 0 else pyB
                    kb.tt(yy[0:n, gq * 512:(gq + 1) * 512], ty[0:n, gq * 512:(gq + 1) * 512], py[0:n, :], ALU.add, R=[ty, py], W=[yy])
                kb.tt(ty[0:n, :].rearrange("p (h q) -> p h q", q=64), xv, bc3(dsk[0:n, :], n, 64), ALU.mult, R=[xtm, dsk], W=[ty])
                kb.tt(yy[0:n, :], yy[0:n, :], ty[0:n, :], ALU.add, R=[yy, ty], W=[yy])
                for i in range(2):
                    pzz = nps(g)
                    kb.mm(pzz, pzz[0:n, :], [(g.hT[:, kc, tok0:tok0 + n], wz[i][:, kc, :]) for kc in range(NKC)], R=[g.hT, wz[i]])
                    kb.act(sz[0:n, i * 512:(i + 1) * 512], pzz[0:n, :], AF.Silu, R=[pzz], W=[sz])
                kb.tt(yy[0:n, :], yy[0:n, :], sz[0:n, :], ALU.mult, R=[yy, sz], W=[yy])
                for gq in range(2):
                    kb.act(junk[0:n, :], yy[0:n, gq * 512:(gq + 1) * 512], AF.Square, R=[yy], W=[junk, ss], accum_out=ss[0:n, gq:gq + 1])
                kb.ts(ss[0:n, :], ss[0:n, :], 1.0 / 512, EPS, ALU.mult, ALU.add, R=[ss], W=[ss])
                kb.act(ss[0:n, :], ss[0:n, :], AF.Sqrt, R=[ss], W=[ss])
                kb.op("dve", lambda e: e.reciprocal(out=ss[0:n, :], in_=ss[0:n, :]), R=[ss], W=[ss])
                for gq in range(2):
                    kb.stt(yy[0:n, gq * 512:(gq + 1) * 512], yy[0:n, gq * 512:(gq + 1) * 512], ss[0:n, gq:gq + 1],
                           nwb[0:n, gq * 512:(gq + 1) * 512], ALU.mult, ALU.mult, R=[yy, ss, nwb], W=[yy])
                og = ostg[ti % 2]
                for half in range(2):
                    p = nps(g)
                    for i in range(4):
                        j = half * 4 + i
                        kb.tr(p, p[:, i * 128:i * 128 + n], yy[0:n, j * 128:(j + 1) * 128], ident[0:n, 0:n], R=[yy, g.cst])
                    kb.cp(og[:, half * 4:half * 4 + 4, 0:n], p[:, :].rearrange("p (a b) -> p a b", b=128)[:, :, 0:n], R=[p], W=[og],
                          e=("act" if half else "dve"))
                kb.dma("sp", g.oT[1536:2560, tok0:tok0 + n].rearrange("(j p) t -> p j t", p=128), og[:, :, 0:n], R=[og], W=[g.oT])
                kb.tt(edl[0:n, :], totb[0:n, :], cum[0:n, :], ALU.subtract, R=[totb, cum], W=[edl])
                kb.act(edl[0:n, :], edl[0:n, :], AF.Exp, R=[edl], W=[edl])
                kb.act(elast[:], totb[:], AF.Exp, R=[totb], W=[elast])
                kb.tt(xs2[0:n, :].rearrange("p (h q) -> p h q", q=64), xdt[0:n, :].rearrange("p (h q) -> p h q", q=64),
                      bc3(edl[0:n, :], n, 64), ALU.mult, R=[xdt, edl], W=[xs2])
                kb.tt(hst[:, :].rearrange("p (h q) -> p h q", q=64), hst[:, :].rearrange("p (h q) -> p h q", q=64),
                      bc3(elast[:, :], 128, 64), ALU.mult, R=[hst, elast], W=[hst])
                for gq in range(2):
                    ph = nps(g)
                    kb.mm(ph, ph[:, :], [(Btm[0:n, ti, gq * 128:(gq + 1) * 128], xs2[0:n, gq * 512:(gq + 1) * 512])], R=[Btm, xs2])
                    kb.tt(hst[:, gq * 512:(gq + 1) * 512], hst[:, gq * 512:(gq + 1) * 512], ph[:, :], ALU.add, R=[hst, ph], W=[hst])
                kb.cp(hstb[:], hst[:], R=[hst], W=[hstb], e="act")
            for half in range(2):
                p = nps(g)
                for i in range(4):
                    j = half * 4 + i
                    kb.tr(p, p[:, i * 128:(i + 1) * 128], hst[:, j * 128:(j + 1) * 128], ident, R=[hst, g.cst])
                kb.cp(hout[:, 8 + half * 4:8 + half * 4 + 4, :], p[:, :].rearrange("p (a b) -> p a b", b=128), R=[p], W=[hout])
            kb.dma("sp", g.ssh_s[l].rearrange("(j p) n -> p j n", p=128), hout[:, 8:16, :], R=[hout], W=[g.ssh_s])
            kb.barrier()


WEIGHT_NAMES = ["norm_mix_pre", "norm_mix_post", "norm_mlp_pre", "norm_mlp_post", "w_in", "pool_w", "pool_scale",
                "dn_conv_w", "dn_a_log", "dn_dt_bias", "dn_norm_w", "sb_bias", "ssm_conv_w", "ssm_conv_b", "ssm_a_log",
                "ssm_dt_bias", "ssm_d", "ssm_norm_w", "w_branch", "w_out", "w_up", "w_down"]


def core_inputs(inp, c, consts):
    m = {}
    m["x_prompt"] = np.ascontiguousarray(inp["x_prompt"][c % 4])
    m["x_sample"] = np.ascontiguousarray(inp["x_sample"][c])
    m["cache_sb_k"] = inp["cache_sb_k"].reshape(DEPTH * NPOOL * 128, 512)
    m["cache_sb_v"] = inp["cache_sb_v"].reshape(DEPTH * NPOOL * 128, 512)
    m["state_pool"] = np.ascontiguousarray(inp["state_pool"][:, c])
    m["state_dn_conv"] = np.ascontiguousarray(inp["state_dn_conv"][:, c])
    m["state_dn_s"] = np.ascontiguousarray(inp["state_dn_s"][:, c])
    m["state_ssm_conv"] = np.ascontiguousarray(inp["state_ssm_conv"][:, c])
    m["state_ssm_h"] = np.ascontiguousarray(inp["state_ssm_h"][:, c]).reshape(DEPTH, 1024, 128)
    m["page_table"] = np.ascontiguousarray(inp["page_table"][c:c + 1]).astype(np.int32)
    for n in WEIGHT_NAMES:
        m[n] = inp[n]
    m["consts"] = consts
    return m


_CACHE = {}


def kernel(**inp):
    inp = {k: np.asarray(v) for k, v in inp.items()}
    cd = make_consts()
    consts = np.ascontiguousarray(np.concatenate([cd[n] for n in CONST_ORDER], axis=1)).astype(np.float32)
    if "kb" not in _CACHE:
        _CACHE["kb"] = build()
    kb = _CACHE["kb"]
    ncores = 8
    in_maps = [core_inputs(inp, c, consts) for c in range(ncores)]
    res = run_bass_kernel_spmd(kb.nc, in_maps, core_ids=list(range(ncores)))
    r = res.results
    P = range(4)
    S = range(8)
    def stk(name, cores, shape):
        return np.stack([np.asarray(r[c][name]).reshape(shape) for c in cores], axis=1)
    y_prompt = np.stack([r[c]["y_prompt"] for c in P], axis=0)
    y_sample = np.stack([r[c]["y_sample"] for c in S], axis=0)
    outs = [y_prompt, y_sample,
            stk("k_p", P, (DEPTH, L, 4, 128)), stk("v_p", P, (DEPTH, L, 4, 128)),
            stk("pool_p", P, (DEPTH, 15, 512)), stk("dnc_p", P, (DEPTH, 3, 1536)),
            stk("dns_p", P, (DEPTH, 4, 128, 128)), stk("ssc_p", P, (DEPTH, 3, 1536)),
            stk("ssh_p", P, (DEPTH, 16, 64, 128)),
            stk("k_s", S, (DEPTH, LS, 4, 128)), stk("v_s", S, (DEPTH, LS, 4, 128)),
            stk("pool_s", S, (DEPTH, 15, 512)), stk("dnc_s", S, (DEPTH, 3, 1536)),
            stk("dns_s", S, (DEPTH, 4, 128, 128)), stk("ssc_s", S, (DEPTH, 3, 1536)),
            stk("ssh_s", S, (DEPTH, 16, 64, 128))]
    return tuple(np.ascontiguousarray(o.astype(np.float32)) for o in outs)
```

```python
import numpy as np
from contextlib import ExitStack
import concourse.bass as bass
import concourse.mybir as mybir
from concourse.bass_utils import run_bass_kernel_spmd

F32 = mybir.dt.float32
BF16 = mybir.dt.bfloat16
I32 = mybir.dt.int32
AF = mybir.ActivationFunctionType
ALU = mybir.AluOpType

D = 2048
L = 2048
LS = 4
NT = L + LS
DEPTH = 2
NKC = D // 128
INC = 14872
C_POOL = 0
C_DNQKV = 512
C_DNZ = 2048
C_DNB = 2560
C_DNA = 2564
C_SBQ = 2568
C_SBK = 3080
C_SBV = 3592
C_SSZ = 4104
C_SSX = 5128
C_SSB = 6152
C_SSC = 6408
C_SSDT = 6664
C_GATE = 6680
NPOOL = 1280
EPS = 1e-6
TCH = [(0, 512), (512, 512), (1024, 512), (1536, 512), (2048, 4)]
TTL = [(i * 128, 128) for i in range(16)] + [(2048, 4)]


class Buf:
    __slots__ = ("w", "r", "multi", "excl")

    def __init__(self, multi=False):
        self.w = {}
        self.r = {}
        self.multi = multi
        self.excl = False


class T:
    def __init__(self, t, multi=False):
        self.t = t
        self.b = Buf(multi)

    def __getitem__(self, idx):
        return self.t[idx]


class KB:
    def __init__(self):
        self.nc = bass.Bass("TRN2", target_bir_lowering=False)
        nc = self.nc
        self.es = ExitStack()
        self.engs = dict(pe=nc.tensor, act=nc.scalar, dve=nc.vector, pool=nc.gpsimd, sp=nc.sync)
        self.semobj = {}
        self.cnt = {}
        self.seen = {e: {} for e in self.engs}
        for e in self.engs:
            self.semobj[e] = self.es.enter_context(nc.semaphore("c_" + e))
            self.cnt[e] = 0
        self.slots = {}
        self.slotv = {}
        self.slotrr = {}
        for q, n in (("sp", 8), ("pool", 8), ("act", 4)):
            ks = []
            for i in range(n):
                k = "d_%s%d" % (q, i)
                self.semobj[k] = self.es.enter_context(nc.semaphore(k))
                self.slotv[k] = 0
                ks.append(k)
            self.slots[q] = ks
            self.slotrr[q] = 0

    def wait(self, e, key, val, force=False):
        if key == e and not force:
            return
        if self.seen[e].get(key, 0) >= val:
            return
        self.engs[e].wait_ge(self.semobj[key], val)
        self.seen[e][key] = val

    def _deps(self, R, W):
        deps = {}
        for b in R:
            for k, v in b.w.items():
                deps[k] = max(deps.get(k, 0), v)
        for b in W:
            if not b.multi:
                for k, v in b.w.items():
                    deps[k] = max(deps.get(k, 0), v)
            for k, v in b.r.items():
                deps[k] = max(deps.get(k, 0), v)
        return deps

    def _record(self, ev, R, W):
        k, v = ev
        for b in R:
            if b.r.get(k, 0) < v:
                b.r[k] = v
        for b in W:
            if b.multi:
                if b.w.get(k, 0) < v:
                    b.w[k] = v
            else:
                b.w = {k: v}
                b.r = {}

    def op(self, e, emit, R=(), W=(), signal=True):
        R = [x.b if isinstance(x, T) else x for x in R]
        W = [x.b if isinstance(x, T) else x for x in W]
        R0 = R
        W = W + [b for b in R if b.excl and b not in W]
        R = [b for b in R if not b.excl]
        for k, v in self._deps(R, W).items():
            self.wait(e, k, v)
        R = R0
        if e != "pe":
            v = 0
            for b in R:
                v = max(v, b.w.get(e, 0))
            if v > 0:
                self.wait(e, e, v, force=True)
        ins = emit(self.engs[e])
        if signal:
            self.cnt[e] += 1
            ins.then_inc(self.semobj[e], 1)
            ev = (e, self.cnt[e])
        else:
            ev = (e, self.cnt[e] + 1)
        self._record(ev, R, W)

    def dma(self, q, out, in_, R=(), W=(), **kw):
        R = [x.b if isinstance(x, T) else x for x in R]
        W = [x.b if isinstance(x, T) else x for x in W]
        ks = self.slots[q]
        k = ks[self.slotrr[q] % len(ks)]
        self.slotrr[q] += 1
        if self.slotv[k] > 0:
            self.wait(q, k, self.slotv[k])
        for dk, dv in self._deps(R, W).items():
            self.wait(q, dk, dv, force=True)
        self.slotv[k] += 16
        self.engs[q].dma_start(out=out, in_=in_, **kw).then_inc(self.semobj[k], 16)
        self._record((k, self.slotv[k]), R, W)

    def idma(self, out, in_, idx_ap, R=(), W=()):
        q = "pool"
        R = [x.b if isinstance(x, T) else x for x in R]
        W = [x.b if isinstance(x, T) else x for x in W]
        ks = self.slots[q]
        k = ks[self.slotrr[q] % len(ks)]
        self.slotrr[q] += 1
        if self.slotv[k] > 0:
            self.wait(q, k, self.slotv[k])
        for dk, dv in self._deps(R, W).items():
            self.wait(q, dk, dv, force=True)
        self.slotv[k] += 16
        self.engs[q].indirect_dma_start(out=out, out_offset=None, in_=in_,
                                        in_offset=bass.IndirectOffsetOnAxis(ap=idx_ap, axis=0)).then_inc(self.semobj[k], 16)
        self._record((k, self.slotv[k]), R, W)

    def barrier(self):
        for e in self.engs:
            for e2 in self.engs:
                if e2 != e and self.cnt[e2] > 0:
                    self.wait(e, e2, self.cnt[e2])
            for k, v in self.slotv.items():
                if v > 0:
                    self.wait(e, k, v)

    def sb(self, stack, name, shape, dt=F32, multi=False):
        self.uid = getattr(self, "uid", 0) + 1
        name = "%s_%d" % (name, self.uid)
        return T(stack.enter_context(self.nc.sbuf_tensor(name, list(shape), dt)), multi)

    def dram(self, name, shape, dt, kind):
        return T(self.nc.dram_tensor(name, list(shape), dt, kind=kind).ap(), True)

    def mm(self, ps, out_ap, pairs, R):
        n = len(pairs)
        for i, (a, b) in enumerate(pairs):
            self.op("pe", lambda e, a=a, b=b, i=i: e.matmul(out_ap, a, b, start=(i == 0), stop=(i == n - 1)),
                    R=R, W=[ps], signal=(i == n - 1))

    def tr(self, ps, out_ap, in_ap, ident_ap, R):
        self.op("pe", lambda e: e.transpose(out_ap, in_ap, ident_ap), R=R, W=[ps])

    def act(self, out, in_, func, R, W, **kw):
        self.op("act", lambda e: e.activation(out=out, in_=in_, func=func, **kw), R=R, W=W)

    def tt(self, out, in0, in1, op, R, W, e="dve"):
        self.op(e, lambda g: g.tensor_tensor(out=out, in0=in0, in1=in1, op=op), R=R, W=W)

    def ts(self, out, in0, s1, s2, op0, op1, R, W, e="dve"):
        if s2 is None:
            self.op(e, lambda g: g.tensor_scalar(out=out, in0=in0, scalar1=s1, scalar2=None, op0=op0), R=R, W=W)
        else:
            self.op(e, lambda g: g.tensor_scalar(out=out, in0=in0, scalar1=s1, scalar2=s2, op0=op0, op1=op1), R=R, W=W)

    def stt(self, out, in0, sc, in1, op0, op1, R, W):
        self.op("dve", lambda g: g.scalar_tensor_tensor(out=out, in0=in0, scalar=sc, in1=in1, op0=op0, op1=op1), R=R, W=W)

    def cp(self, out, in_, R, W, e="dve"):
        if e == "act":
            self.act(out, in_, AF.Copy, R, W)
        else:
            self.op(e, lambda g: g.tensor_copy(out=out, in_=in_), R=R, W=W)


def make_consts():
    j = np.arange(128)[:, None]
    t = np.arange(128)[None, :]
    c = {}
    c["ident"] = (j == t).astype(np.float32)
    c["triu"] = (j <= t).astype(np.float32)
    c["tgt"] = (j > t).astype(np.float32)
    c["ones"] = np.ones((128, 128), np.float32)
    c["tril"] = (j >= t).astype(np.float32)
    q = np.arange(512)[None, :]
    for r in range(4):
        c["am%d" % r] = ((r * 128 + j) < q).astype(np.float32)
    return c


CONST_ORDER = ["ident", "triu", "tgt", "ones", "tril", "am0", "am1", "am2", "am3"]
CONST_OFF = {}
_o = 0
for _n in CONST_ORDER:
    CONST_OFF[_n] = _o
    _o += 128 if not _n.startswith("am") else 512
CONST_W = _o


class Ctx:
    pass


def build(dbg=()):
    kb = KB()
    nc = kb.nc
    g = Ctx()
    g.kb = kb
    g.dbg = set(dbg)
    IN = lambda n, s, dt=F32: kb.dram(n, s, dt, "ExternalInput")
    OUT = lambda n, s, dt=F32: kb.dram(n, s, dt, "ExternalOutput")
    SCR = lambda n, s, dt=F32: kb.dram(n, s, dt, "Internal")
    g.x_prompt = IN("x_prompt", [L, D])
    g.x_sample = IN("x_sample", [LS, D])
    ncr = 128 if "small_cache" in g.dbg else DEPTH * NPOOL * 128
    g.cache_k = IN("cache_sb_k", [ncr, 512])
    g.cache_v = IN("cache_sb_v", [ncr, 512])
    g.state_pool = IN("state_pool", [DEPTH, 15, 512])
    g.state_dn_conv = IN("state_dn_conv", [DEPTH, 3, 1536])
    g.state_dn_s = IN("state_dn_s", [DEPTH, 4, 128, 128])
    g.state_ssm_conv = IN("state_ssm_conv", [DEPTH, 3, 1536])
    g.state_ssm_h = IN("state_ssm_h", [DEPTH, 1024, 128])
    g.page_table = IN("page_table", [1, 128], I32)
    for n in ["norm_mix_pre", "norm_mix_post", "norm_mlp_pre", "norm_mlp_post"]:
        setattr(g, n, IN(n, [DEPTH, D]))
    g.w_in = IN("w_in", [DEPTH, D, INC])
    g.pool_w = IN("pool_w", [DEPTH, 4, 128, 128])
    g.pool_scale = IN("pool_scale", [DEPTH, 512])
    g.dn_conv_w = IN("dn_conv_w", [DEPTH, 4, 1536])
    g.dn_a_log = IN("dn_a_log", [DEPTH, 4])
    g.dn_dt_bias = IN("dn_dt_bias", [DEPTH, 4])
    g.dn_norm_w = IN("dn_norm_w", [DEPTH, 128])
    g.sb_bias = IN("sb_bias", [DEPTH, 4])
    g.ssm_conv_w = IN("ssm_conv_w", [DEPTH, 4, 1536])
    g.ssm_conv_b = IN("ssm_conv_b", [DEPTH, 1536])
    g.ssm_a_log = IN("ssm_a_log", [DEPTH, 16])
    g.ssm_dt_bias = IN("ssm_dt_bias", [DEPTH, 16])
    g.ssm_d = IN("ssm_d", [DEPTH, 16])
    g.ssm_norm_w = IN("ssm_norm_w", [DEPTH, 1024])
    g.w_branch = IN("w_branch", [DEPTH, 2560, D])
    g.w_out = IN("w_out", [DEPTH, D, D])
    g.w_up = IN("w_up", [DEPTH, D, 4 * D])
    g.w_down = IN("w_down", [DEPTH, 4 * D, D])
    g.consts = IN("consts", [128, CONST_W])
    g.y_prompt = OUT("y_prompt", [L, D])
    g.y_sample = OUT("y_sample", [LS, D])
    g.k_p = OUT("k_p", [DEPTH, L, 512])
    g.v_p = OUT("v_p", [DEPTH, L, 512])
    g.pool_p = OUT("pool_p", [DEPTH, 15, 512])
    g.dnc_p = OUT("dnc_p", [DEPTH, 3, 1536])
    g.dns_p = OUT("dns_p", [DEPTH, 4, 128, 128])
    g.ssc_p = OUT("ssc_p", [DEPTH, 3, 1536])
    g.ssh_p = OUT("ssh_p", [DEPTH, 1024, 128])
    g.k_s = OUT("k_s", [DEPTH, LS, 512])
    g.v_s = OUT("v_s", [DEPTH, LS, 512])
    g.pool_s = OUT("pool_s", [DEPTH, 15, 512])
    g.dnc_s = OUT("dnc_s", [DEPTH, 3, 1536])
    g.dns_s = OUT("dns_s", [DEPTH, 4, 128, 128])
    g.ssc_s = OUT("ssc_s", [DEPTH, 3, 1536])
    g.ssh_s = OUT("ssh_s", [DEPTH, 1024, 128])
    g.xa = SCR("xa", [NT, D])
    g.xb = SCR("xb", [NT, D])
    g.oT = SCR("oT", [2560, NT], BF16)
    g.mT = SCR("mT", [D, NT], BF16)
    g.ftm = SCR("ftm", [NT, D])
    if "oT_in" in g.dbg:
        g.oT_in = IN("oT_in", [DEPTH, 2560, NT])
    if "oT_out" in g.dbg:
        g.oT_out = OUT("oT_out", [2560, NT], BF16)
    if "dump0" in g.dbg:
        g.x1_out = OUT("x1_out", [NT, D])
        g.x2_out = OUT("x2_out", [NT, D])
        g.mT_out = OUT("mT_out", [D, NT], BF16)
        g.ftm_out = OUT("ftm_out", [NT, D])
    if "hT_out" in g.dbg:
        g.hT_out = OUT("hT_out", [128, NKC, NT], BF16)

    with ExitStack() as top:
        g.cst = kb.sb(top, "cst", [128, 640])
        kb.dma("sp", g.cst[:], g.consts[:, 0:640], R=[g.consts], W=[g.cst])
        g.cstb = kb.sb(top, "cstb", [128, 640], BF16)
        kb.dma("pool", g.cstb[:], g.consts[:, 0:640], R=[g.consts], W=[g.cstb])
        g.ps = [T(top.enter_context(nc.psum_tensor("ps%d" % i, [128, 512], F32))) for i in range(8)]
        for p_ in g.ps:
            p_.b.excl = True
        g.psi = 0
        g.eps_t = kb.sb(top, "eps_t", [128, 1])
        kb.op("dve", lambda e: e.memset(g.eps_t[:], EPS), W=[g.eps_t])
        for l in range(1 if "mix_only" in g.dbg else DEPTH):
            xin_p = g.x_prompt if l == 0 else T(g.xb[0:L, :])
            xin_s = g.x_sample if l == 0 else T(g.xb[L:NT, :])
            if l > 0:
                xin_p.b = g.xb.b
                xin_s.b = g.xb.b
            xout_p = T(g.xb[0:L, :]) if l == 0 else g.y_prompt
            xout_s = T(g.xb[L:NT, :]) if l == 0 else g.y_sample
            if l == 0:
                xout_p.b = g.xb.b
                xout_s.b = g.xb.b
            with ExitStack() as lst:
                g.hT = kb.sb(lst, "hT", [128, NKC, NT], BF16, multi=True)
                phase_norm1(g, l, xin_p, xin_s)
                if "hT_out" in g.dbg and l == 0:
                    kb.dma("sp", g.hT_out[:], g.hT[:], R=[g.hT], W=[g.hT_out])
                if "oT_in" in g.dbg:
                    pass
                else:
                    if "no_pool" not in g.dbg:
                        phase_pool(g, l)
                    if "no_attn" not in g.dbg:
                        phase_attn(g, l)
                    if "no_dn" not in g.dbg:
                        phase_dn(g, l)
                    if "no_ssd" not in g.dbg:
                        phase_ssd(g, l)
                if "oT_out" in g.dbg and l == 0:
                    kb.barrier()
                    kb.dma("sp", g.oT_out[:], g.oT[:], R=[g.oT], W=[g.oT_out])
                if "mix_only" not in g.dbg:
                    phase_merge(g, l)
            if "mix_only" not in g.dbg:
                phase_mlp(g, l, xin_p, xin_s, xout_p, xout_s)
            if "dump0" in g.dbg and l == 0:
                kb.barrier()
                kb.dma("sp", g.x1_out[:], g.xa[:], R=[g.xa], W=[g.x1_out])
                kb.dma("sp", g.x2_out[:], g.xb[:], R=[g.xb], W=[g.x2_out])
                kb.dma("sp", g.mT_out[:], g.mT[:], R=[g.mT], W=[g.mT_out])
                kb.dma("sp", g.ftm_out[:], g.ftm[:], R=[g.ftm], W=[g.ftm_out])
                kb.barrier()
        kb.barrier()
    return kb


def nps(g):
    p = g.ps[g.psi % 6]
    g.psi += 1
    return p


def xrows(xp, xs, tok0, n):
    if tok0 < L:
        return xp, xp[tok0:tok0 + n, :]
    return xs, xs[tok0 - L:tok0 - L + n, :]


def rstd_from_ss(g, st, ss, n, tag):
    kb = g.kb
    kb.act(ss[0:n, :], ss[0:n, :], AF.Ln, R=[ss, g.eps_t], W=[ss], scale=1.0 / D, bias=g.eps_t[0:n, 0:1])
    kb.act(ss[0:n, :], ss[0:n, :], AF.Exp, R=[ss], W=[ss], scale=-0.5)


def tm_to_hT(g, src, n, tok0, dstT, evi=0):
    kb = g.kb
    ident = g.cst[:, CONST_OFF["ident"]:CONST_OFF["ident"] + 128]
    for j in range(4):
        p = nps(g)
        for i in range(4):
            kc = j * 4 + i
            kb.tr(p, p[:, i * 128:i * 128 + n], src[0:n, kc * 128:(kc + 1) * 128], ident[0:n, 0:n], R=[src, g.cst])
        o = dstT[:, j * 4:j * 4 + 4, tok0:tok0 + n]
        i_ = p[:, :].rearrange("p (a b) -> p a b", b=128)[:, :, 0:n]
        kb.cp(o, i_, R=[p], W=[dstT], e=("act" if (j + evi) % 2 else "dve"))


def phase_norm1(g, l, xin_p, xin_s):
    kb = g.kb
    with ExitStack() as st:
        gb = kb.sb(st, "gb", [128, D])
        kb.dma("sp", gb[:], g.norm_mix_pre[l].partition_broadcast(128), R=[g.norm_mix_pre], W=[gb])
        xt = [kb.sb(st, "xt%d" % i, [128, D]) for i in range(2)]
        xn = [kb.sb(st, "xn%d" % i, [128, D]) for i in range(2)]
        junk = kb.sb(st, "junk", [128, D], BF16)
        ss = [kb.sb(st, "ss%d" % i, [128, 1]) for i in range(2)]
        for ti, (tok0, n) in enumerate(TTL):
            b = ti % 2
            xsrc, rows = xrows(xin_p, xin_s, tok0, n)
            kb.dma("sp", xt[b][0:n, :], rows, R=[xsrc], W=[xt[b]])
            kb.act(junk[0:n, :], xt[b][0:n, :], AF.Square, R=[xt[b]], W=[junk, ss[b]], accum_out=ss[b][0:n, :])
            rstd_from_ss(g, st, ss[b], n, "n1")
            kb.stt(xn[b][0:n, :], xt[b][0:n, :], ss[b][0:n, :], gb[0:n, :], ALU.mult, ALU.mult, R=[xt[b], ss[b], gb], W=[xn[b]])
            tm_to_hT(g, xn[b], n, tok0, g.hT, ti)
        kb.barrier()


HALVES = [(0, 1024, [(0, 512), (512, 512)]), (1024, 1028, [(1024, 512), (1536, 512), (2048, 4)])]
BRCH = [list(range(0, 4)), list(range(4, 8)), list(range(8, 12)), list(range(12, 20))]


def phase_merge(g, l):
    kb = g.kb
    with ExitStack() as st:
        oTs = kb.sb(st, "oTs", [128, 20, 1028], BF16)
        wg = [kb.sb(st, "wg%d" % i, [128, NKC, 4, 128], BF16) for i in range(2)]
        wb = [kb.sb(st, "wb%d" % i, [128, 20, 128], BF16) for i in range(2)]
        sg = [kb.sb(st, "sg%d" % i, [128, 512]) for i in range(2)]
        tmp = [kb.sb(st, "mtmp%d" % i, [128, 512]) for i in range(2)]
        acc = [kb.sb(st, "macc%d" % i, [128, 512]) for i in range(2)]
        mo = [kb.sb(st, "mo%d" % i, [128, 1028], BF16) for i in range(2)]
        it = 0
        for (h0, hn, chunks) in HALVES:
            if "oT_in" in g.dbg:
                kb.dma("pool", oTs[:, :, 0:hn], g.oT_in[l, :, h0:h0 + hn].rearrange("(bc p) t -> p bc t", p=128),
                       R=[g.oT_in], W=[oTs])
            else:
                kb.dma("sp", oTs[:, :, 0:hn], g.oT[:, h0:h0 + hn].rearrange("(bc p) t -> p bc t", p=128),
                       R=[g.oT], W=[oTs])
            for dj in range(NKC):
                b = dj % 2
                for i in range(4):
                    c0 = C_GATE + i * D + dj * 128
                    kb.dma("pool", wg[b][:, :, i, :], g.w_in[l, :, c0:c0 + 128].rearrange("(kc p) c -> p kc c", p=128),
                           R=[g.w_in], W=[wg[b]])
                kb.dma("pool", wb[b][:], g.w_branch[l, :, dj * 128:(dj + 1) * 128].rearrange("(bc p) c -> p bc c", p=128),
                       R=[g.w_branch], W=[wb[b]])
                for (tok0, n) in chunks:
                    a = acc[it % 2]
                    for i in range(4):
                        pg = nps(g)
                        kb.mm(pg, pg[:, 0:n], [(wg[b][:, kc, i, :], g.hT[:, kc, tok0:tok0 + n]) for kc in range(NKC)],
                              R=[wg[b], g.hT])
                        pp = nps(g)
                        kb.mm(pp, pp[:, 0:n], [(wb[b][:, bc, :], oTs[:, bc, tok0 - h0:tok0 - h0 + n]) for bc in BRCH[i]],
                              R=[wb[b], oTs])
                        s_ = sg[i % 2]
                        kb.act(s_[:, 0:n], pg[:, 0:n], AF.Sigmoid, R=[pg], W=[s_])
                        last = (i == 3)
                        if i == 0:
                            kb.tt(a[:, 0:n], pp[:, 0:n], s_[:, 0:n], ALU.mult, R=[pp, s_], W=[a])
                        else:
                            t_ = tmp[i % 2]
                            kb.tt(t_[:, 0:n], pp[:, 0:n], s_[:, 0:n], ALU.mult, R=[pp, s_], W=[t_])
                            if last:
                                kb.tt(mo[b][:, tok0 - h0:tok0 - h0 + n], a[:, 0:n], t_[:, 0:n], ALU.add, R=[a, t_], W=[mo[b]])
                            else:
                                kb.tt(a[:, 0:n], a[:, 0:n], t_[:, 0:n], ALU.add, R=[a, t_], W=[a])
                    it += 1
                kb.dma("sp", g.mT[dj * 128:(dj + 1) * 128, h0:h0 + hn], mo[b][:, 0:hn], R=[mo[b]], W=[g.mT])
        kb.barrier()


def phase_mlp(g, l, xin_p, xin_s, xout_p, xout_s):
    kb = g.kb
    ident = g.cst[:, CONST_OFF["ident"]:CONST_OFF["ident"] + 128]
    for (h0, hn, chunks) in HALVES:
        tiles = [(t0, n) for (t0, n) in TTL if h0 <= t0 < h0 + hn]
        with ExitStack() as sth:
            h2T = kb.sb(sth, "h2T", [128, NKC, 1028], BF16, multi=True)
            with ExitStack() as st:
                wo = kb.sb(st, "wo", [128, NKC, D], BF16, multi=True)
                for ec in range(4):
                    kb.dma("pool", wo[:, :, ec * 512:(ec + 1) * 512],
                           g.w_out[l, :, ec * 512:(ec + 1) * 512].rearrange("(kc p) c -> p kc c", p=128), R=[g.w_out], W=[wo])
                mTs = kb.sb(st, "mTs", [128, NKC, 1028], BF16)
                kb.dma("sp", mTs[:, :, 0:hn], g.mT[:, h0:h0 + hn].rearrange("(kc p) t -> p kc t", p=128), R=[g.mT], W=[mTs])
                gpost = kb.sb(st, "gpost", [128, D])
                kb.dma("sp", gpost[:], g.norm_mix_post[l].partition_broadcast(128), R=[g.norm_mix_post], W=[gpost])
                gpre2 = kb.sb(st, "gpre2", [128, D])
                kb.dma("sp", gpre2[:], g.norm_mlp_pre[l].partition_broadcast(128), R=[g.norm_mlp_pre], W=[gpre2])
                mix = [kb.sb(st, "mix%d" % i, [128, D], multi=True) for i in range(2)]
                xt = [kb.sb(st, "xt2_%d" % i, [128, D]) for i in range(2)]
                junk = kb.sb(st, "junk2", [128, D], BF16)
                ss = [kb.sb(st, "ss2_%d" % i, [128, 1]) for i in range(4)]
                def s1_front(ti, tok0, n):
                    b = ti % 2
                    tl = tok0 - h0
                    xsrc, rows = xrows(xin_p, xin_s, tok0, n)
                    kb.dma("sp", xt[b][0:n, :], rows, R=[xsrc], W=[xt[b]])
                    for ec in range(4):
                        p = nps(g)
                        kb.mm(p, p[0:n, :], [(mTs[:, kc, tl:tl + n], wo[:, kc, ec * 512:(ec + 1) * 512]) for kc in range(NKC)],
                              R=[mTs, wo])
                        kb.cp(mix[b][0:n, ec * 512:(ec + 1) * 512], p[0:n, :], R=[p], W=[mix[b]], e=("act" if ec % 2 else "dve"))

                def s1_back(ti, tok0, n):
                    b = ti % 2
                    tl = tok0 - h0
                    s1 = ss[2 * b]
                    s2 = ss[2 * b + 1]
                    kb.act(junk[0:n, :], mix[b][0:n, :], AF.Square, R=[mix[b]], W=[junk, s1], accum_out=s1[0:n, :])
                    rstd_from_ss(g, st, s1, n, "a")
                    kb.stt(mix[b][0:n, :], mix[b][0:n, :], s1[0:n, :], gpost[0:n, :], ALU.mult, ALU.mult, R=[mix[b], s1, gpost], W=[mix[b]])
                    kb.tt(xt[b][0:n, :], xt[b][0:n, :], mix[b][0:n, :], ALU.add, R=[xt[b], mix[b]], W=[xt[b]])
                    kb.dma("sp", g.xa[tok0:tok0 + n, :], xt[b][0:n, :], R=[xt[b]], W=[g.xa])
                    kb.act(junk[0:n, :], xt[b][0:n, :], AF.Square, R=[xt[b]], W=[junk, s2], accum_out=s2[0:n, :])
                    rstd_from_ss(g, st, s2, n, "b")
                    kb.stt(mix[b][0:n, :], xt[b][0:n, :], s2[0:n, :], gpre2[0:n, :], ALU.mult, ALU.mult, R=[xt[b], s2, gpre2, mix[b]], W=[mix[b]])
                    tm_to_hT(g, mix[b], n, tl, h2T, ti)

                s1_front(0, *tiles[0])
                for ti, (tok0, n) in enumerate(tiles):
                    if ti + 1 < len(tiles):
                        s1_front(ti + 1, *tiles[ti + 1])
                    s1_back(ti, tok0, n)
                kb.barrier()
            with ExitStack() as st:
                fT = kb.sb(st, "fT", [128, 64, 1028], BF16, multi=True)
                with ExitStack() as st2:
                    wu = [kb.sb(st2, "wu%d" % i, [128, NKC, 256], BF16) for i in range(2)]
                    rl = [kb.sb(st2, "rl%d" % i, [128, 512]) for i in range(2)]
                    it = 0
                    for fb in range(32):
                        b = fb % 2
                        kb.dma("pool", wu[b][:], g.w_up[l, :, fb * 256:(fb + 1) * 256].rearrange("(kc p) c -> p kc c", p=128),
                               R=[g.w_up], W=[wu[b]])
                        for j in range(2):
                            for (tok0, n) in chunks:
                                cl = tok0 - h0
                                p = nps(g)
                                kb.mm(p, p[:, 0:n], [(wu[b][:, kc, j * 128:(j + 1) * 128], h2T[:, kc, cl:cl + n]) for kc in range(NKC)],
                                      R=[wu[b], h2T])
                                r_ = rl[it % 2]
                                kb.act(r_[:, 0:n], p[:, 0:n], AF.Relu, R=[p], W=[r_])
                                kb.tt(fT[:, fb * 2 + j, cl:cl + n], r_[:, 0:n], r_[:, 0:n], ALU.mult, R=[r_], W=[fT])
                                it += 1
                    kb.barrier()
                with ExitStack() as st2:
                    wd = [kb.sb(st2, "wd%d" % i, [128, 64, 128], BF16) for i in range(2)]
                    fo = [kb.sb(st2, "fo%d" % i, [128, 512]) for i in range(2)]
                    stg = [kb.sb(st2, "stg%d" % i, [128, 4, 128]) for i in range(2)]
                    it = 0
                    for dj in range(NKC):
                        b = dj % 2
                        kb.dma("pool", wd[b][:], g.w_down[l, :, dj * 128:(dj + 1) * 128].rearrange("(fc p) c -> p fc c", p=128),
                               R=[g.w_down], W=[wd[b]])
                        for (tok0, n) in chunks:
                            cl = tok0 - h0
                            p = nps(g)
                            kb.mm(p, p[:, 0:n], [(wd[b][:, fc, :], fT[:, fc, cl:cl + n]) for fc in range(64)], R=[wd[b], fT])
                            f_ = fo[it % 2]
                            s_ = stg[it % 2]
                            kb.cp(f_[:, 0:n], p[:, 0:n], R=[p], W=[f_], e="act")
                            p2 = nps(g)
                            nt_ = (n + 127) // 128
                            for a in range(nt_):
                                na = min(128, n - a * 128)
                                kb.tr(p2, p2[0:na, a * 128:(a + 1) * 128], f_[:, a * 128:a * 128 + na], ident, R=[f_, g.cst])
                            if n == 512:
                                kb.cp(s_[:, :, :], p2[:, :].rearrange("p (a c) -> p a c", c=128), R=[p2], W=[s_])
                                kb.dma("sp", g.ftm[tok0:tok0 + n, dj * 128:(dj + 1) * 128].rearrange("(a p) c -> p a c", p=128),
                                       s_[:, :, :], R=[s_], W=[g.ftm])
                            else:
                                kb.cp(s_[0:n, 0, :], p2[0:n, 0:128], R=[p2], W=[s_])
                                kb.dma("sp", g.ftm[tok0:tok0 + n, dj * 128:(dj + 1) * 128], s_[0:n, 0, :], R=[s_], W=[g.ftm])
                            it += 1
                    kb.barrier()
        with ExitStack() as st:
            gp2 = kb.sb(st, "gp2", [128, D])
            kb.dma("sp", gp2[:], g.norm_mlp_post[l].partition_broadcast(128), R=[g.norm_mlp_post], W=[gp2])
            ft = [kb.sb(st, "ft%d" % i, [128, D]) for i in range(2)]
            x1 = [kb.sb(st, "x1_%d" % i, [128, D]) for i in range(2)]
            junk = kb.sb(st, "junk3", [128, D], BF16)
            ss = [kb.sb(st, "ss3_%d" % i, [128, 1]) for i in range(2)]
            for ti, (tok0, n) in enumerate(tiles):
                b = ti % 2
                kb.dma("sp", ft[b][0:n, :], g.ftm[tok0:tok0 + n, :], R=[g.ftm], W=[ft[b]])
                kb.dma("sp", x1[b][0:n, :], g.xa[tok0:tok0 + n, :], R=[g.xa], W=[x1[b]])
                kb.act(junk[0:n, :], ft[b][0:n, :], AF.Square, R=[ft[b]], W=[junk, ss[b]], accum_out=ss[b][0:n, :])
                rstd_from_ss(g, st, ss[b], n, "c")
                kb.stt(ft[b][0:n, :], ft[b][0:n, :], ss[b][0:n, :], gp2[0:n, :], ALU.mult, ALU.mult, R=[ft[b], ss[b], gp2], W=[ft[b]])
                kb.tt(x1[b][0:n, :], x1[b][0:n, :], ft[b][0:n, :], ALU.add, R=[x1[b], ft[b]], W=[x1[b]])
                xdst, rows = xrows(xout_p, xout_s, tok0, n)
                kb.dma("sp", rows, x1[b][0:n, :], R=[x1[b]], W=[xdst])
            kb.barrier()


PW = 16 + L + 16 + LS
PS0 = 16 + L


def ucol(tok0):
    return 16 + tok0 if tok0 < L else PS0 + 16 + (tok0 - L)


def phase_pool(g, l):
    kb = g.kb
    ident = g.cst[:, 0:128]
    with ExitStack() as st:
        wpl = kb.sb(st, "wpl", [128, NKC, 512], BF16)
        kb.dma("pool", wpl[:], g.w_in[l, :, C_POOL:C_POOL + 512].rearrange("(kc p) c -> p kc c", p=128), R=[g.w_in], W=[wpl])
        pw = kb.sb(st, "pw", [128, 4, 128], BF16)
        kb.dma("pool", pw[:], g.pool_w[l].rearrange("g c d -> c g d"), R=[g.pool_w], W=[pw])
        psc = kb.sb(st, "psc", [128, 4])
        kb.dma("sp", psc[:], g.pool_scale[l].rearrange("(g p) -> p g", p=128), R=[g.pool_scale], W=[psc],
               allow_slow_non_contiguous=True)
        invs = kb.sb(st, "invs", [128, 4, 16])
        for gi in range(4):
            w = 2 << gi
            for pos in range(15):
                kb.op("dve", lambda e, gi=gi, pos=pos, w=w: e.memset(invs[:, gi, pos:pos + 1], 1.0 / min(pos + 1, w)), W=[invs])
        spt = kb.sb(st, "spt", [16, 512])
        kb.dma("sp", spt[0:15, :], g.state_pool[l], R=[g.state_pool], W=[spt])
        A = kb.sb(st, "pA", [128, PW])
        B = kb.sb(st, "pB", [128, PW])
        C = kb.sb(st, "pC", [128, PW])
        yb = kb.sb(st, "pyb", [128, PW], BF16)
        tmpf = kb.sb(st, "ptmp", [128, 16])
        opT = [kb.sb(st, "opT%d" % i, [128, NT], BF16) for i in range(2)]
        for gi in range(4):
            w = 2 << gi
            kb.op("dve", lambda e: e.memset(A[:, 0:16], 0.0), W=[A])
            kb.op("dve", lambda e: e.memset(A[:, PS0:PS0 + 1], 0.0), W=[A])
            p = nps(g)
            kb.tr(p, p[:, 0:15], spt[0:15, gi * 128:(gi + 1) * 128], ident[0:15, 0:15], R=[spt, g.cst])
            kb.cp(A[:, PS0 + 1:PS0 + 16], p[:, 0:15], R=[p], W=[A])
            for ci, (tok0, n) in enumerate(TCH):
                p = nps(g)
                kb.mm(p, p[:, 0:n], [(wpl[:, kc, gi * 128:(gi + 1) * 128], g.hT[:, kc, tok0:tok0 + n]) for kc in range(NKC)],
                      R=[wpl, g.hT])
                c0 = ucol(tok0)
                kb.cp(A[:, c0:c0 + n], p[:, 0:n], R=[p], W=[A], e=("act" if ci % 2 else "dve"))
            src = A
            bufs = [B, C]
            sh = 1
            lo = 0
            for stp in range(gi + 1):
                dst = bufs[stp % 2]
                lo += sh
                kb.tt(dst[:, lo:PW], src[:, lo:PW], src[:, lo - sh:PW - sh], ALU.add, R=[src], W=[dst])
                src = dst
                sh *= 2
            kb.stt(yb[:, 16:PW], src[:, 16:PW], 1.0 / w, A[:, 16:PW], ALU.mult, ALU.subtract, R=[src, A], W=[yb])
            kb.tt(tmpf[:, 0:15], src[:, 16:31], invs[:, gi, 0:15], ALU.mult, R=[src, invs], W=[tmpf])
            kb.tt(yb[:, 16:31], tmpf[:, 0:15], A[:, 16:31], ALU.subtract, R=[tmpf, A, yb], W=[yb])
            o_ = opT[gi % 2]
            for ci, (tok0, n) in enumerate(TCH):
                p = nps(g)
                c0 = ucol(tok0)
                kb.mm(p, p[:, 0:n], [(pw[:, gi, :], yb[:, c0:c0 + n])], R=[pw, yb])
                kb.ts(o_[:, tok0:tok0 + n], p[:, 0:n], psc[:, gi:gi + 1], None, ALU.mult, None, R=[p, psc], W=[o_])
            kb.dma("sp", g.oT[gi * 128:(gi + 1) * 128, :], o_[:, :], R=[o_], W=[g.oT])
        pn = kb.sb(st, "pn", [16, 512])
        p = nps(g)
        kb.mm(p, p[0:15, :], [(g.hT[:, kc, L - 15:L], wpl[:, kc, :]) for kc in range(NKC)], R=[g.hT, wpl])
        kb.cp(pn[0:15, :], p[0:15, :], R=[p], W=[pn])
        kb.dma("sp", g.pool_p[l], pn[0:15, :], R=[pn], W=[g.pool_p])
        pn2 = kb.sb(st, "pn2", [4, 512])
        p = nps(g)
        kb.mm(p, p[0:LS, :], [(g.hT[:, kc, L:NT], wpl[:, kc, :]) for kc in range(NKC)], R=[g.hT, wpl])
        kb.cp(pn2[0:LS, :], p[0:LS, :], R=[p], W=[pn2])
        kb.dma("sp", g.pool_s[l, 15 - LS:15, :], pn2[0:LS, :], R=[pn2], W=[g.pool_s])
        kb.dma("sp", g.pool_s[l, 0:15 - LS, :], g.state_pool[l, LS:15, :], R=[g.state_pool], W=[g.pool_s])
        kb.barrier()


def proj_fm(g, wt, c0, dst_fn, scale=None, evs=("act", "dve")):
    kb = g.kb
    for ci, (tok0, n) in enumerate(TCH):
        p = nps(g)
        kb.mm(p, p[:, 0:n], [(wt[:, kc, c0:c0 + 128], g.hT[:, kc, tok0:tok0 + n]) for kc in range(NKC)], R=[wt, g.hT])
        dT, dap = dst_fn(tok0, n)
        e = evs[ci % len(evs)]
        if scale is None:
            kb.cp(dap, p[:, 0:n], R=[p], W=[dT], e=e)
        elif e == "act":
            kb.act(dap, p[:, 0:n], AF.Copy, R=[p], W=[dT], scale=scale)
        else:
            kb.ts(dap, p[:, 0:n], scale, None, ALU.mult, None, R=[p], W=[dT])


def phase_attn(g, l):
    kb = g.kb
    ident = g.cst[:, 0:128]
    SC = 128.0 ** -0.5
    with ExitStack() as st:
        qT = kb.sb(st, "qT", [128, 4, NT], BF16, multi=True)
        kT = kb.sb(st, "kT", [128, 4, NT], BF16, multi=True)
        vtm = kb.sb(st, "vtm", [128, 16, 512], BF16, multi=True)
        vs = kb.sb(st, "vs", [4, 512], BF16)
        am = kb.sb(st, "am", [128, 4, 512], BF16)
        kb.dma("pool", am[:], g.consts[:, 640:2688].rearrange("p (r q) -> p r q", q=512), R=[g.consts], W=[am])
        bsb = kb.sb(st, "bsb", [128, 4])
        kb.dma("sp", bsb[:], g.sb_bias[l].partition_broadcast(128), R=[g.sb_bias], W=[bsb])
        ntgt = kb.sb(st, "ntgt", [128, 128], BF16)
        kb.ts(ntgt[:], g.cst[:, 256:384], -1.0, None, ALU.mult, None, R=[g.cst], W=[ntgt])
        nonesf = kb.sb(st, "nonesf", [128, 128])
        kb.ts(nonesf[:], g.cst[:, 384:512], -1.0, None, ALU.mult, None, R=[g.cst], W=[nonesf])
        ntgtf = kb.sb(st, "ntgtf", [128, 128])
        kb.ts(ntgtf[:], g.cst[:, 256:384], -1.0, None, ALU.mult, None, R=[g.cst], W=[ntgtf])
        zer = kb.sb(st, "zer", [128, 128], BF16)
        kb.op("dve", lambda e: e.memset(zer[:], 0.0), W=[zer])
        with ExitStack() as st2:
            wb_ = [kb.sb(st2, "aw%d" % i, [128, NKC, 512], BF16) for i in range(2)]
            stg = [kb.sb(st2, "astg%d" % i, [128, 512]) for i in range(2)]
            if "stop_a" in g.dbg:
                kb.barrier()
                return
            kb.dma("pool", wb_[0][:], g.w_in[l, :, C_SBQ:C_SBQ + 512].rearrange("(kc p) c -> p kc c", p=128), R=[g.w_in], W=[wb_[0]])
            kb.dma("pool", wb_[1][:], g.w_in[l, :, C_SBK:C_SBK + 512].rearrange("(kc p) c -> p kc c", p=128), R=[g.w_in], W=[wb_[1]])
            for h in range(4):
                proj_fm(g, wb_[0], h * 128, lambda tok0, n, h=h: (qT, qT[:, h, tok0:tok0 + n]), scale=SC)
            for h in range(4):
                proj_fm(g, wb_[1], h * 128, lambda tok0, n, h=h: (kT, kT[:, h, tok0:tok0 + n]))
            if "stop_b" in g.dbg:
                kb.barrier()
                return
            kb.dma("pool", wb_[0][:], g.w_in[l, :, C_SBV:C_SBV + 512].rearrange("(kc p) c -> p kc c", p=128), R=[g.w_in], W=[wb_[0]])
            it = 0
            for which, wt in (("k", wb_[1]), ("v", wb_[0])):
                if which == "v" and "stop_c" in g.dbg:
                    break
                for ti, (tok0, n) in enumerate(TTL):
                    if "stop_d" in g.dbg and tok0 >= L:
                        break
                    p = nps(g)
                    kb.mm(p, p[0:n, :], [(g.hT[:, kc, tok0:tok0 + n], wt[:, kc, :]) for kc in range(NKC)], R=[g.hT, wt])
                    s_ = stg[it % 2]
                    it += 1
                    kb.cp(s_[0:n, :], p[0:n, :], R=[p], W=[s_], e="act")
                    if which == "v":
                        if tok0 < L:
                            kb.cp(vtm[0:n, ti, :], p[0:n, :], R=[p], W=[vtm])
                        else:
                            kb.cp(vs[0:n, :], p[0:n, :], R=[p], W=[vs])
                    dst = (g.k_p if which == "k" else g.v_p) if tok0 < L else (g.k_s if which == "k" else g.v_s)
                    r0 = tok0 if tok0 < L else tok0 - L
                    kb.dma("sp", dst[l, r0:r0 + n, :], s_[0:n, :], R=[s_], W=[dst])
            kb.barrier()
        with ExitStack() as st2:
            NB = 4
            ex = [kb.sb(st2, "aex%d" % i, [128, 512]) for i in range(NB)]
            spb = [kb.sb(st2, "aspb%d" % i, [128, 512], BF16) for i in range(NB)]
            spm = [kb.sb(st2, "aspm%d" % i, [128, 512], BF16) for i in range(NB)]
            t1 = [kb.sb(st2, "at1%d" % i, [128, 512]) for i in range(NB)]
            ee = [kb.sb(st2, "aee%d" % i, [128, 512]) for i in range(NB)]
            att = [kb.sb(st2, "aatt%d" % i, [128, 512], BF16) for i in range(NB)]
            attm = [kb.sb(st2, "aattm%d" % i, [128, 512], BF16) for i in range(NB)]
            acc = [kb.sb(st2, "aacc%d" % i, [128, 512]) for i in range(2)]
            osb = [kb.sb(st2, "aosb%d" % i, [128, 512], BF16) for i in range(2)]

            def stream(h, qc, sid):
                po = g.ps[6 + sid]
                ac = acc[sid]
                nkb = 4 * qc + 4
                it = 0
                for kb_ in range(nkb - 1, -1, -1):
                    b = sid * 2 + it % 2
                    it += 1
                    r = kb_ - 4 * qc
                    first = (kb_ == nkb - 1)
                    pz = nps(g)
                    kb.mm(pz, pz[:, :], [(kT[:, h, kb_ * 128:(kb_ + 1) * 128], qT[:, h, qc * 512:(qc + 1) * 512])], R=[kT, qT])
                    yield
                    kb.act(ex[b][:], pz[:, :], AF.Exp, R=[pz, bsb], W=[ex[b]], bias=bsb[:, h:h + 1])
                    kb.act(spb[b][:], ex[b][:], AF.Ln, R=[ex[b]], W=[spb[b]], bias=1.0)
                    sp_ = spb[b]
                    if r >= 0:
                        kb.tt(spm[b][:], spb[b][:], am[:, r, :], ALU.mult, R=[spb[b], am], W=[spm[b]])
                        sp_ = spm[b]
                    kb.stt(t1[b][:], pz[:, :], bsb[:, h:h + 1], spb[b][:], ALU.add, ALU.subtract, R=[pz, bsb, spb[b]], W=[t1[b]])
                    yield
                    ps2 = nps(g)
                    pairs = [(ntgt[:], sp_[:])]
                    if not first:
                        pairs.append((nonesf[:], ac[:]))
                    kb.mm(ps2, ps2[:, :], pairs, R=[ntgt, nonesf, sp_, ac])
                    yield
                    kb.tt(ee[b][:], ps2[:, :], t1[b][:], ALU.add, R=[ps2, t1[b]], W=[ee[b]])
                    kb.act(att[b][:], ee[b][:], AF.Exp, R=[ee[b]], W=[att[b]])
                    a_ = att[b]
                    if r >= 0:
                        kb.tt(attm[b][:], att[b][:], am[:, r, :], ALU.mult, R=[att[b], am], W=[attm[b]])
                        a_ = attm[b]
                    if first:
                        kb.cp(ac[:], sp_[:], R=[sp_], W=[ac])
                    elif kb_ > 0:
                        kb.tt(ac[:], ac[:], sp_[:], ALU.add, R=[ac, sp_], W=[ac])
                    yield
                    kb.op("pe", lambda e, kb_=kb_, a_=a_, first=first: e.matmul(po[:, :], vtm[:, kb_, h * 128:(h + 1) * 128], a_[:],
                                                                               start=first, stop=(kb_ == 0)),
                          R=[vtm, a_], W=[po])
                    yield
                o_ = osb[sid]
                kb.cp(o_[:], po[:, :], R=[po], W=[o_], e="act")
                kb.dma("sp", g.oT[1024 + h * 128:1024 + (h + 1) * 128, qc * 512:(qc + 1) * 512], o_[:], R=[o_], W=[g.oT])

            todo = [(h, qc) for h in range(0 if "no_attn_p" in g.dbg else 4) for qc in (3, 0, 2, 1)]
            active = [None, None]
            while todo or any(a is not None for a in active):
                for sid in range(2):
                    if active[sid] is None and todo:
                        h, qc = todo.pop(0)
                        active[sid] = stream(h, qc, sid)
                    if active[sid] is not None:
                        try:
                            next(active[sid])
                        except StopIteration:
                            active[sid] = None
            kb.barrier()
        with ExitStack() as st2:
            if "no_attn_s" in g.dbg:
                return
            ptb = kb.sb(st2, "ptb", [128, 128], I32)
            kb.dma("sp", ptb[:], g.page_table[0].partition_broadcast(128), R=[g.page_table], W=[ptb])
            iop = kb.sb(st2, "iop", [128, 1], I32)
            kb.op("pool", lambda e: e.iota(iop[:], [[0, 1]], base=l * NPOOL * 128, channel_multiplier=1), W=[iop])
            idx = kb.sb(st2, "idx", [128, 128], I32)
            kb.ts(idx[:], ptb[:], 128.0, iop[:, 0:1], ALU.mult, ALU.add, R=[ptb, iop], W=[idx])
            segm = kb.sb(st2, "segm", [128, 16, 32])
            kb.op("dve", lambda e: e.memset(segm[:], 1.0), W=[segm])
            kb.op("dve", lambda e: e.memset(segm[:, :, 0:1], 0.0), W=[segm])
            carry = kb.sb(st2, "carry", [128, 16])
            kpg = [kb.sb(st2, "kpg%d" % i, [128, 512]) for i in range(4)]
            kTp = [kb.sb(st2, "kTp%d" % i, [128, 4, 128], BF16) for i in range(4)]
            vpg = kb.sb(st2, "vpg", [128, 32, 512], BF16, multi=True)
            sxe = kb.sb(st2, "sxe", [128, 512])
            zs = kb.sb(st2, "zs", [128, 512])
            ssp = kb.sb(st2, "ssp", [128, 512])
            sinc = kb.sb(st2, "sinc", [128, 512])
            st1 = kb.sb(st2, "st1", [128, 512])
            sS = kb.sb(st2, "sS", [128, 512])
            satt = kb.sb(st2, "satt", [128, 512], BF16)
            po = g.ps[6]
            kb.op("pe", lambda e: e.matmul(po[:, 0:16], zer[:, :], zer[:, 0:16], start=True, stop=False), R=[zer], W=[po])
            pzn = nps(g)
            for h in range(4):
                kb.mm(pzn, pzn[0:LS, h * 4:(h + 1) * 4], [(kT[:, h, L:NT], qT[:, h, L:NT])], R=[kT, qT])
            n_ex = kb.sb(st2, "n_ex", [4, 16])
            n_sp = kb.sb(st2, "n_sp", [4, 16])
            n_spm = kb.sb(st2, "n_spm", [4, 16])
            n_t1 = kb.sb(st2, "n_t1", [4, 16])
            n_e = kb.sb(st2, "n_e", [4, 16])
            n_att = kb.sb(st2, "n_att", [4, 16])
            n_attb = kb.sb(st2, "n_attb", [4, 16], BF16)
            am4 = g.cst[0:4, 256:260]
            m4 = kb.sb(st2, "m4", [4, 4, 4])
            for h in range(4):
                kb.cp(m4[0:4, h, :], am[0:4, 0, 0:4], R=[am], W=[m4])
            for h in range(4):
                kb.act(n_ex[:, h * 4:(h + 1) * 4], pzn[0:LS, h * 4:(h + 1) * 4], AF.Exp, R=[pzn, bsb], W=[n_ex], bias=bsb[0:LS, h:h + 1])
            kb.act(n_sp[:], n_ex[:], AF.Ln, R=[n_ex], W=[n_sp], bias=1.0)
            m4f = m4[:, :, :].rearrange("p a b -> p (a b)")
            kb.tt(n_spm[:], n_sp[:], m4f, ALU.mult, R=[n_sp, m4], W=[n_spm])
            psn = nps(g)
            kb.mm(psn, psn[0:LS, 0:16], [(ntgtf[0:LS, 0:LS], n_spm[:])], R=[ntgtf, n_spm])
            for h in range(4):
                kb.stt(n_t1[:, h * 4:(h + 1) * 4], pzn[0:LS, h * 4:(h + 1) * 4], bsb[0:LS, h:h + 1], n_sp[:, h * 4:(h + 1) * 4],
                       ALU.add, ALU.subtract, R=[pzn, bsb, n_sp], W=[n_t1])
            kb.tt(n_e[:], psn[0:LS, 0:16], n_t1[:], ALU.add, R=[psn, n_t1], W=[n_e])
            kb.act(n_att[:], n_e[:], AF.Exp, R=[n_e], W=[n_att])
            kb.tt(n_attb[:], n_att[:], m4f, ALU.mult, R=[n_att, m4], W=[n_attb])
            for h in range(4):
                kb.op("pe", lambda e, h=h: e.matmul(po[:, h * 4:(h + 1) * 4], vs[0:LS, h * 128:(h + 1) * 128], n_attb[:, h * 4:(h + 1) * 4],
                                                    start=False, stop=False), R=[vs, n_attb], W=[po])
            pcr = nps(g)
            kb.mm(pcr, pcr[:, 0:16], [(g.cst[0:LS, 384:512], n_spm[:])], R=[g.cst, n_spm])
            kb.cp(carry[:], pcr[:, 0:16], R=[pcr], W=[carry])
            pgi = 0
            for G in range(3, -1, -1):
                pz = zs
                pzv = zs[:, :].rearrange("k (h q s) -> k h q s", h=4, q=4)
                for s_ in range(32):
                    page = G * 32 + 31 - s_
                    b = pgi % 4
                    pgi += 1
                    kb.idma(kpg[b][:], g.cache_k[:, :], idx[:, page:page + 1], R=[g.cache_k, idx], W=[kpg[b]])
                    kb.idma(vpg[:, s_, :], g.cache_v[:, :], idx[:, page:page + 1], R=[g.cache_v, idx], W=[vpg])
                    pt_ = nps(g)
                    for h in range(4):
                        kb.tr(pt_, pt_[:, h * 128:(h + 1) * 128], kpg[b][:, h * 128:(h + 1) * 128], ident, R=[kpg[b], g.cst])
                    kb.cp(kTp[b][:, :, :], pt_[:, :].rearrange("p (h k) -> p h k", h=4), R=[pt_], W=[kTp[b]], e=("act" if s_ % 2 else "dve"))
                    pzp = nps(g)
                    for h in range(4):
                        kb.mm(pzp, pzp[:, h * 4:(h + 1) * 4], [(kTp[b][:, h, :], qT[:, h, L:NT])], R=[kTp[b], qT])
                    kb.cp(pzv[:, :, :, s_], pzp[:, 0:16].rearrange("k (h q) -> k h q", h=4), R=[pzp], W=[zs])
                for h in range(4):
                    kb.act(sxe[:, h * 128:(h + 1) * 128], pz[:, h * 128:(h + 1) * 128], AF.Exp, R=[pz, bsb], W=[sxe], bias=bsb[:, h:h + 1])
                kb.act(ssp[:], sxe[:], AF.Ln, R=[sxe], W=[ssp], bias=1.0)
                pw_ = nps(g)
                kb.mm(pw_, pw_[:, :], [(g.cst[:, 256:384], ssp[:])], R=[g.cst, ssp])
                pp_ = nps(g)
                kb.mm(pp_, pp_[:, :], [(g.cst[:, 384:512], ssp[:])], R=[g.cst, ssp])
                kb.op("dve", lambda e: e.tensor_tensor_scan(out=sinc[:], data0=segm[:, :, :].rearrange("p a b -> p (a b)"), data1=pp_[:, :],
                                                            initial=0.0, op0=ALU.mult, op1=ALU.add), R=[segm, pp_], W=[sinc])
                kb.tt(sS[:], sinc[:], pp_[:, :], ALU.subtract, R=[sinc, pp_], W=[sS])
                kb.tt(sS[:], sS[:], pw_[:, :], ALU.add, R=[sS, pw_], W=[sS])
                sSv = sS[:, :].rearrange("p (a b) -> p a b", b=32)
                kb.tt(sSv, sSv, carry[:, :].unsqueeze(2).to_broadcast([128, 16, 32]), ALU.add, R=[sS, carry], W=[sS])
                for h in range(4):
                    kb.stt(st1[:, h * 128:(h + 1) * 128], pz[:, h * 128:(h + 1) * 128], bsb[:, h:h + 1], ssp[:, h * 128:(h + 1) * 128],
                           ALU.add, ALU.subtract, R=[pz, bsb, ssp], W=[st1])
                kb.tt(st1[:], st1[:], sS[:], ALU.subtract, R=[st1, sS], W=[st1])
                kb.act(satt[:], st1[:], AF.Exp, R=[st1], W=[satt])
                sincv = sinc[:, :].rearrange("p (a b) -> p a b", b=32)
                kb.tt(carry[:, :].unsqueeze(2), carry[:, :].unsqueeze(2), sincv[:, :, 31:32], ALU.add, R=[carry, sinc], W=[carry])
                sattv = satt[:, :].rearrange("k (h q s) -> k h q s", h=4, q=4)
                for s_ in range(32):
                    for h in range(4):
                        kb.op("pe", lambda e, s_=s_, h=h: e.matmul(po[:, h * 4:(h + 1) * 4], vpg[:, s_, h * 128:(h + 1) * 128], sattv[:, h, :, s_],
                                                                    start=False, stop=False), R=[vpg, satt], W=[po])
            oss = kb.sb(st2, "oss", [128, 16], BF16)
            kb.cp(oss[:], po[:, 0:16], R=[po], W=[oss])
            for h in range(4):
                kb.dma("sp", g.oT[1024 + h * 128:1024 + (h + 1) * 128, L:NT], oss[:, h * 4:(h + 1) * 4], R=[oss], W=[g.oT])
            kb.barrier()


def phase_dn(g, l):
    kb = g.kb
    ident = g.cst[:, 0:128]
    triu = g.cst[:, 128:256]
    tgt = g.cst[:, 256:384]
    ones = g.cst[:, 384:512]
    identb = g.cstb[:, 0:128]
    onesb = g.cstb[:, 384:512]
    with ExitStack() as st:
        qT = kb.sb(st, "d_qT", [128, 4, NT], BF16, multi=True)
        kT = kb.sb(st, "d_kT", [128, 4, NT], BF16, multi=True)
        ktm = kb.sb(st, "d_ktm", [128, 17, 512], BF16, multi=True)
        vtm = kb.sb(st, "d_vtm", [128, 17, 512], BF16, multi=True)
        bg = kb.sb(st, "d_bg", [128, 17, 8], multi=True)
        with ExitStack() as st1:
            cwt, hfm = load_conv_w(g, st1, g.dn_conv_w, None, g.state_dn_conv, l, "d_cw")
            wb_ = [kb.sb(st1, "d_w%d" % i, [128, NKC, 512], BF16) for i in range(2)]
            ext = kb.sb(st1, "d_ext", [128, PW])
            tmp = kb.sb(st1, "d_tmp", [128, PW])
            xc = kb.sb(st1, "d_xc", [128, PW])
            sqb = kb.sb(st1, "d_sqb", [128, PW], BF16)
            rn = [kb.sb(st1, "d_rn%d" % i, [128, 512]) for i in range(2)]
            cn = kb.sb(st1, "d_cn", [4, 512])
            it = 0
            for blk in range(3):
                wt = wb_[blk % 2]
                kb.dma("pool", wt[:], g.w_in[l, :, C_DNQKV + blk * 512:C_DNQKV + (blk + 1) * 512].rearrange("(kc p) c -> p kc c", p=128),
                       R=[g.w_in], W=[wt])
                for h in range(4):
                    j = blk * 4 + h
                    conv_fm(g, wt, h * 128, cwt[:, j, :], (hfm, hfm[:, j, :]), ext, tmp, xc)
                    if blk < 2:
                        kb.act(sqb[:, 16:PW], xc[:, 16:PW], AF.Square, R=[xc], W=[sqb])
                        for (tok0, n) in TCH:
                            c = ucol(tok0)
                            p = nps(g)
                            kb.mm(p, p[:, 0:n], [(onesb, sqb[:, c:c + n])], R=[g.cstb, sqb])
                            r_ = rn[it % 2]
                            it += 1
                            kb.act(r_[:, 0:n], p[:, 0:n], AF.Ln, R=[p], W=[r_], bias=g.eps_t[:, 0:1])
                            kb.act(r_[:, 0:n], r_[:, 0:n], AF.Exp, R=[r_], W=[r_], scale=-0.5)
                            if blk == 0:
                                kb.stt(qT[:, h, tok0:tok0 + n], xc[:, c:c + n], 128.0 ** -0.5, r_[:, 0:n], ALU.mult, ALU.mult, R=[xc, r_], W=[qT])
                            else:
                                kb.tt(xc[:, c:c + n], xc[:, c:c + n], r_[:, 0:n], ALU.mult, R=[xc, r_], W=[xc])
                                kb.cp(kT[:, h, tok0:tok0 + n], xc[:, c:c + n], R=[xc], W=[kT], e="act")
                        if blk == 1:
                            fm_to_tm(g, xc, lambda ti, n, h=h: (ktm, ktm[0:n, ti, h * 128:(h + 1) * 128]), h)
                    else:
                        fm_to_tm(g, xc, lambda ti, n, h=h: (vtm, vtm[0:n, ti, h * 128:(h + 1) * 128]), h)
                p = nps(g)
                kb.mm(p, p[0:3, :], [(g.hT[:, kc, L - 3:L], wt[:, kc, :]) for kc in range(NKC)], R=[g.hT, wt])
                kb.cp(cn[0:3, :], p[0:3, :], R=[p], W=[cn])
                kb.dma("sp", g.dnc_p[l, :, blk * 512:(blk + 1) * 512], cn[0:3, :], R=[cn], W=[g.dnc_p])
                p = nps(g)
                kb.mm(p, p[0:LS, :], [(g.hT[:, kc, L:NT], wt[:, kc, :]) for kc in range(NKC)], R=[g.hT, wt])
                kb.cp(cn[0:LS, :], p[0:LS, :], R=[p], W=[cn])
                kb.dma("sp", g.dnc_s[l, :, blk * 512:(blk + 1) * 512], cn[1:LS, :], R=[cn], W=[g.dnc_s])
            wbg = kb.sb(st1, "d_wbg", [128, NKC, 8], BF16)
            kb.dma("pool", wbg[:], g.w_in[l, :, C_DNB:C_DNB + 8].rearrange("(kc p) c -> p kc c", p=128), R=[g.w_in], W=[wbg])
            nea = kb.sb(st1, "d_nea", [128, 4])
            kb.dma("sp", nea[:], g.dn_a_log[l].partition_broadcast(128), R=[g.dn_a_log], W=[nea])
            kb.act(nea[:], nea[:], AF.Exp, R=[nea], W=[nea])
            kb.ts(nea[:], nea[:], -1.0, None, ALU.mult, None, R=[nea], W=[nea])
            dtb = kb.sb(st1, "d_dtb", [128, 4])
            kb.dma("sp", dtb[:], g.dn_dt_bias[l].partition_broadcast(128), R=[g.dn_dt_bias], W=[dtb])
            for ti, (tok0, n) in enumerate(TTL):
                p = nps(g)
                kb.mm(p, p[0:n, 0:8], [(g.hT[:, kc, tok0:tok0 + n], wbg[:, kc, :]) for kc in range(NKC)], R=[g.hT, wbg])
                kb.cp(bg[0:n, ti, 0:4], p[0:n, 0:4], R=[p], W=[bg])
                kb.tt(bg[0:n, ti, 4:8], p[0:n, 4:8], dtb[0:n, :], ALU.add, R=[p, dtb], W=[bg])
            kb.act(bg[:, :, 0:4], bg[:, :, 0:4], AF.Sigmoid, R=[bg], W=[bg])
            kb.act(bg[:, :, 4:8], bg[:, :, 4:8], AF.Exp, R=[bg], W=[bg])
            kb.act(bg[:, :, 4:8], bg[:, :, 4:8], AF.Ln, R=[bg], W=[bg], bias=1.0)
            kb.tt(bg[:, :, 4:8], bg[:, :, 4:8], nea[:, :].unsqueeze(1).to_broadcast([128, 17, 4]), ALU.mult, R=[bg, nea], W=[bg])
            kb.barrier()
        with ExitStack() as st2:
            wz = kb.sb(st2, "d_wz", [128, NKC, 512], BF16)
            kb.dma("pool", wz[:], g.w_in[l, :, C_DNZ:C_DNZ + 512].rearrange("(kc p) c -> p kc c", p=128), R=[g.w_in], W=[wz])
            nwb = kb.sb(st2, "d_nwb", [128, 4, 128])
            for h in range(4):
                kb.dma("sp", nwb[:, h, :], g.dn_norm_w[l].partition_broadcast(128), R=[g.dn_norm_w], W=[nwb])
            F = lambda nm: kb.sb(st2, nm, [128, 4, 128])
            Bt = lambda nm: kb.sb(st2, nm, [128, 4, 128], BF16)
            Wg, A1, A2 = F("d_Wg"), F("d_A1"), F("d_A2")
            Nf = Wg
            curM = [F("d_M0"), F("d_M1")]
            curN = [F("d_N0"), F("d_N1")]
            Pp = [F("d_P0"), F("d_P1")]
            Qq = [F("d_Q0"), F("d_Q1")]
            attnT, sz, vn16 = Bt("d_at"), Bt("d_sz"), Bt("d_vn16")
            RHSv, kd, nwc, vnb = F("d_rv"), F("d_kd"), F("d_nwc"), F("d_vn")
            RHSk = A1
            S = F("d_S")
            Sb = Bt("d_Sb")
            oo = F("d_oo")
            ty = F("d_ty")
            gcum = kb.sb(st2, "d_gcum", [128, 4])
            gtot = kb.sb(st2, "d_gtot", [128, 4])
            eg = kb.sb(st2, "d_eg", [128, 4])
            bge = kb.sb(st2, "d_bge", [128, 4])
            egd = kb.sb(st2, "d_egd", [128, 4])
            egt = kb.sb(st2, "d_egt", [128, 4])
            ss = kb.sb(st2, "d_ss", [128, 4])
            junk = kb.sb(st2, "d_junk", [128, 128], BF16)
            ostg = [kb.sb(st2, "d_ostg%d" % i, [128, 4, 128], BF16) for i in range(2)]
            TTk = [F("d_TT0"), F("d_TT1")]
            attnT2 = [attnT, Bt("d_at1")]
            RHSv2 = [RHSv, F("d_rv1")]
            nwc2 = [nwc, F("d_nwc1")]
            kd2 = [kd, F("d_kd1")]
            sz2 = [sz, Bt("d_sz1")]
            eg2 = [eg, kb.sb(st2, "d_eg1", [128, 4])]
            egt2 = [egt, kb.sb(st2, "d_egt1", [128, 4])]

            def prep(ti, tok0, n):
                k2 = ti % 2
                attnT, RHSv, nwc, kd, sz, eg, egt = attnT2[k2], RHSv2[k2], nwc2[k2], kd2[k2], sz2[k2], eg2[k2], egt2[k2]
                gq = bg[0:n, ti, 4:8]
                be = bg[0:n, ti, 0:4]
                p = nps(g)
                kb.mm(p, p[0:n, 0:4], [(triu[0:n, 0:n], gq)], R=[g.cst, bg])
                kb.cp(gcum[0:n, :], p[0:n, 0:4], R=[p], W=[gcum])
                p = nps(g)
                kb.mm(p, p[:, 0:4], [(ones[0:n, :], gq)], R=[g.cst, bg])
                kb.cp(gtot[:], p[:, 0:4], R=[p], W=[gtot])
                kb.tt(Wg[0:n, :, 0:n], triu[0:n, 0:n].unsqueeze(1).to_broadcast([n, 4, n]), bc3(gq, n, n), ALU.mult, R=[g.cst, bg], W=[Wg])
                yield
                pr = nps(g)
                prv = pr[0:n, 0:4 * n].rearrange("p (h t) -> p h t", t=n)
                kb.mm(pr, prv, [(ones[0:n, 0:n], Wg[0:n, :, 0:n])], R=[g.cst, Wg])
                for h in range(4):
                    kb.ts(A1[0:n, h, 0:n], prv[:, h, :], gcum[0:n, h:h + 1], 0.0, ALU.subtract, ALU.max, R=[pr, gcum], W=[A1])
                    kb.ts(A2[0:n, h, 0:n], prv[:, h, :], gcum[0:n, h:h + 1], 0.0, ALU.subtract, ALU.min, R=[pr, gcum], W=[A2])
                yield
                kb.act(A1[0:n, :, 0:n], A1[0:n, :, 0:n], AF.Exp, R=[A1], W=[A1], scale=-1.0)
                kb.act(A2[0:n, :, 0:n], A2[0:n, :, 0:n], AF.Exp, R=[A2], W=[A2])
                kb.tt(A1[0:n, :, 0:n], A1[0:n, :, 0:n], tgt[0:n, 0:n].unsqueeze(1).to_broadcast([n, 4, n]), ALU.mult, R=[A1, g.cst], W=[A1])
                kb.tt(A2[0:n, :, 0:n], A2[0:n, :, 0:n], triu[0:n, 0:n].unsqueeze(1).to_broadcast([n, 4, n]), ALU.mult, R=[A2, g.cst], W=[A2])
                yield
                pk = nps(g)
                pkv = pk[0:n, 0:4 * n].rearrange("p (h t) -> p h t", t=n)
                for h in range(4):
                    kb.mm(pk, pkv[:, h, :], [(kT[:, h, tok0:tok0 + n], kT[:, h, tok0:tok0 + n])], R=[kT])
                kb.stt(Nf[0:n, :, 0:n], pkv, -1.0, A1[0:n, :, 0:n], ALU.mult, ALU.mult, R=[pk, A1], W=[Nf])
                kb.tt(Nf[0:n, :, 0:n], Nf[0:n, :, 0:n], bc3(be, n, n), ALU.mult, R=[Nf, bg], W=[Nf])
                kb.cp(curN[0][0:n, :, 0:n], Nf[0:n, :, 0:n], R=[Nf], W=[curN[0]], e="act")
                yield
                pm = nps(g)
                pmv = pm[0:n, 0:4 * n].rearrange("p (h t) -> p h t", t=n)
                for h in range(4):
                    kb.tr(pm, pmv[:, h, :], Nf[0:n, h, 0:n], ident[0:n, 0:n], R=[Nf, g.cst])
                kb.cp(curM[0][0:n, :, 0:n], pmv, R=[pm], W=[curM[0]])
                pq = nps(g)
                pqv = pq[0:n, 0:4 * n].rearrange("p (h t) -> p h t", t=n)
                for h in range(4):
                    kb.mm(pq, pqv[:, h, :], [(kT[:, h, tok0:tok0 + n], qT[:, h, tok0:tok0 + n])], R=[kT, qT])
                kb.tt(attnT[0:n, :, 0:n], pqv, A2[0:n, :, 0:n], ALU.mult, R=[pq, A2], W=[attnT])
                yield
                idb = ident[0:n, 0:n].unsqueeze(1).to_broadcast([n, 4, n])
                kb.tt(Pp[0][0:n, :, 0:n], curM[0][0:n, :, 0:n], idb, ALU.add, R=[curM[0], g.cst], W=[Pp[0]])
                kb.tt(Qq[0][0:n, :, 0:n], curN[0][0:n, :, 0:n], idb, ALU.add, R=[curN[0], g.cst], W=[Qq[0]])
                nsq = 0
                while (2 << nsq) < n:
                    nsq += 1
                cp_, cq_, cm_, cn_ = Pp[0], Qq[0], curM[0], curN[0]
                for k in range(nsq):
                    lastk = (k == nsq - 1)
                    nm_, nn_ = curM[(k + 1) % 2], curN[(k + 1) % 2]
                    np_, nq_ = Pp[(k + 1) % 2], Qq[(k + 1) % 2]
                    pa = nps(g)
                    pav = pa[0:n, 0:4 * n].rearrange("p (h t) -> p h t", t=n)
                    for h in range(4):
                        kb.mm(pa, pav[:, h, :], [(cn_[0:n, h, 0:n], cm_[0:n, h, 0:n])], R=[cn_, cm_])
                    kb.cp(nm_[0:n, :, 0:n], pav, R=[pa], W=[nm_], e="act")
                    if not lastk:
                        pb = nps(g)
                        pbv = pb[0:n, 0:4 * n].rearrange("p (h t) -> p h t", t=n)
                        for h in range(4):
                            kb.mm(pb, pbv[:, h, :], [(cm_[0:n, h, 0:n], cn_[0:n, h, 0:n])], R=[cn_, cm_])
                        kb.cp(nn_[0:n, :, 0:n], pbv, R=[pb], W=[nn_], e="act")
                    yield
                    pc = nps(g)
                    pcv = pc[0:n, 0:4 * n].rearrange("p (h t) -> p h t", t=n)
                    for h in range(4):
                        kb.mm(pc, pcv[:, h, :], [(cq_[0:n, h, 0:n], nm_[0:n, h, 0:n])], R=[cq_, nm_])
                    kb.tt(np_[0:n, :, 0:n], pcv, cp_[0:n, :, 0:n], ALU.add, R=[pc, cp_], W=[np_])
                    if not lastk:
                        pd = nps(g)
                        pdv = pd[0:n, 0:4 * n].rearrange("p (h t) -> p h t", t=n)
                        for h in range(4):
                            kb.mm(pd, pdv[:, h, :], [(cp_[0:n, h, 0:n], nn_[0:n, h, 0:n])], R=[cp_, nn_])
                        kb.tt(nq_[0:n, :, 0:n], pdv, cq_[0:n, :, 0:n], ALU.add, R=[pd, cq_], W=[nq_])
                    cp_, cq_, cm_, cn_ = np_, nq_, nm_, nn_
                    yield
                TT = TTk[k2]
                kb.cp(TT[0:n, :, 0:n], cp_[0:n, :, 0:n], R=[cp_], W=[TT])
                kb.act(eg[0:n, :], gcum[0:n, :], AF.Exp, R=[gcum], W=[eg])
                kb.tt(bge[0:n, :], eg[0:n, :], be, ALU.mult, R=[eg, bg], W=[bge])
                kb.tt(egd[0:n, :], gtot[0:n, :], gcum[0:n, :], ALU.subtract, R=[gtot, gcum], W=[egd])
                kb.act(egd[0:n, :], egd[0:n, :], AF.Exp, R=[egd], W=[egd])
                kb.act(egt[:], gtot[:], AF.Exp, R=[gtot], W=[egt])
                yield
                vv = vtm[0:n, ti, :].rearrange("p (h d) -> p h d", d=128)
                kv = ktm[0:n, ti, :].rearrange("p (h d) -> p h d", d=128)
                kb.tt(RHSv[0:n, :, :], vv, bc3(be, n, 128), ALU.mult, R=[vtm, bg], W=[RHSv])
                kb.tt(RHSk[0:n, :, :], kv, bc3(bge[0:n, :], n, 128), ALU.mult, R=[ktm, bge], W=[RHSk])
                kb.tt(kd[0:n, :, :], kv, bc3(egd[0:n, :], n, 128), ALU.mult, R=[ktm, egd], W=[kd])
                yield
                pw_ = nps(g)
                pwv = pw_[:, 0:4 * n].rearrange("p (h t) -> p h t", t=n)
                for h in range(4):
                    kb.mm(pw_, pwv[:, h, :], [(RHSk[0:n, h, :], TT[0:n, h, 0:n])], R=[RHSk, TT])
                kb.ts(nwc[:, :, 0:n], pwv, -1.0, None, ALU.mult, None, R=[pw_], W=[nwc])
                yield
                pzz = nps(g)
                kb.mm(pzz, pzz[0:n, :], [(g.hT[:, kc, tok0:tok0 + n], wz[:, kc, :]) for kc in range(NKC)], R=[g.hT, wz])
                kb.act(sz[0:n, :, :], pzz[0:n, :].rearrange("p (h d) -> p h d", d=128), AF.Silu, R=[pzz], W=[sz])

            def recur(ti, tok0, n):
                k2 = ti % 2
                attnT, RHSv, nwc, kd, sz, eg, egt, TT = attnT2[k2], RHSv2[k2], nwc2[k2], kd2[k2], sz2[k2], eg2[k2], egt2[k2], TTk[k2]
                if ti == 0:
                    kb.op("dve", lambda e: e.memset(S[:], 0.0), W=[S])
                    kb.op("dve", lambda e: e.memset(Sb[:], 0.0), W=[Sb])
                if ti == 16:
                    kb.dma("sp", g.dns_p[l].rearrange("h k v -> k h v"), S[:, :, :], R=[S], W=[g.dns_p])
                    kb.dma("sp", S[:, :, :], g.state_dn_s[l].rearrange("h k v -> k h v"), R=[g.state_dn_s], W=[S])
                    kb.cp(Sb[:], S[:], R=[S], W=[Sb])
                pv_ = nps(g)
                for h in range(4):
                    kb.mm(pv_, pv_[0:n, h * 128:(h + 1) * 128], [(TT[0:n, h, 0:n], RHSv[0:n, h, :]), (nwc[:, h, 0:n], S[:, h, :])],
                          R=[TT, RHSv, nwc, S])
                kb.cp(vnb[0:n, :, :], pv_[0:n, :].rearrange("p (h d) -> p h d", d=128), R=[pv_], W=[vnb], e="act")
                kb.cp(vn16[0:n, :, :], vnb[0:n, :, :], R=[vnb], W=[vn16])
                yield
                po1 = nps(g)
                for h in range(4):
                    kb.mm(po1, po1[0:n, h * 128:(h + 1) * 128], [(qT[:, h, tok0:tok0 + n], Sb[:, h, :])], R=[qT, Sb])
                psu = nps(g)
                for h in range(4):
                    kb.mm(psu, psu[:, h * 128:(h + 1) * 128], [(kd[0:n, h, :], vnb[0:n, h, :])], R=[kd, vnb])
                kb.tt(ty[0:n, :, :], po1[0:n, :].rearrange("p (h d) -> p h d", d=128), bc3(eg[0:n, :], n, 128), ALU.mult, R=[po1, eg], W=[ty])
                kb.tt(S[:, :, :], S[:, :, :], bc3(egt[:, :], 128, 128), ALU.mult, R=[S, egt], W=[S])
                kb.tt(S[:, :, :], S[:, :, :], psu[:, :].rearrange("p (h d) -> p h d", d=128), ALU.add, R=[S, psu], W=[S])
                kb.cp(Sb[:], S[:], R=[S], W=[Sb], e="act")
                yield
                po2 = nps(g)
                for h in range(4):
                    kb.mm(po2, po2[0:n, h * 128:(h + 1) * 128], [(attnT[0:n, h, 0:n], vn16[0:n, h, :])], R=[attnT, vn16])
                kb.tt(oo[0:n, :, :], po2[0:n, :].rearrange("p (h d) -> p h d", d=128), ty[0:n, :, :], ALU.add, R=[po2, ty], W=[oo])
                yield
                for h in range(4):
                    kb.act(junk[0:n, :], oo[0:n, h, :], AF.Square, R=[oo], W=[junk, ss], accum_out=ss[0:n, h:h + 1])
                kb.act(ss[0:n, :], ss[0:n, :], AF.Ln, R=[ss, g.eps_t], W=[ss], scale=1.0 / 128, bias=g.eps_t[0:n, 0:1])
                kb.act(ss[0:n, :], ss[0:n, :], AF.Exp, R=[ss], W=[ss], scale=-0.5)
                yield
                kb.tt(oo[0:n, :, :], oo[0:n, :, :], bc3(ss[0:n, :], n, 128), ALU.mult, R=[oo, ss], W=[oo])
                kb.tt(oo[0:n, :, :], oo[0:n, :, :], nwb[0:n, :, :], ALU.mult, R=[oo, nwb], W=[oo])
                kb.tt(oo[0:n, :, :], oo[0:n, :, :], sz[0:n, :, :], ALU.mult, R=[oo, sz], W=[oo])
                yield
                og = ostg[ti % 2]
                p = nps(g)
                for h in range(4):
                    kb.tr(p, p[:, h * 128:h * 128 + n], oo[0:n, h, :], ident[0:n, 0:n], R=[oo, g.cst])
                kb.cp(og[:, :, 0:n], p[:, :].rearrange("p (a b) -> p a b", b=128)[:, :, 0:n], R=[p], W=[og])
                kb.dma("sp", g.oT[512:1024, tok0:tok0 + n].rearrange("(j p) t -> p j t", p=128), og[:, :, 0:n], R=[og], W=[g.oT])

            for _ in prep(0, *TTL[0]):
                pass
            for ti, (tok0, n) in enumerate(TTL):
                gens = [recur(ti, tok0, n)]
                if ti + 1 < len(TTL):
                    gens.append(prep(ti + 1, *TTL[ti + 1]))
                while gens:
                    for gen in list(gens):
                        try:
                            next(gen)
                        except StopIteration:
                            gens.remove(gen)
            kb.dma("sp", g.dns_s[l].rearrange("h k v -> k h v"), S[:, :, :], R=[S], W=[g.dns_s])
            kb.barrier()


def load_conv_w(g, st, cw_dram, cb_dram, hist_dram, l, name):
    kb = g.kb
    ident = g.cst[:, 0:128]
    cwt = kb.sb(st, name + "t", [128, 12, 5])
    hfm = kb.sb(st, name + "h", [128, 12, 3])
    with ExitStack() as stx:
        cwl = kb.sb(stx, name + "l", [5, 1536])
        kb.op("dve", lambda e: e.memset(cwl[:], 0.0), W=[cwl])
        kb.dma("sp", cwl[0:4, :], cw_dram[l], R=[cw_dram], W=[cwl])
        if cb_dram is not None:
            kb.dma("sp", cwl[4:5, :], cb_dram[l:l + 1, :], R=[cb_dram], W=[cwl])
        sct = kb.sb(stx, name + "s", [3, 1536])
        kb.dma("sp", sct[:], hist_dram[l], R=[hist_dram], W=[sct])
        p = nps(g)
        for j in range(12):
            kb.tr(p, p[:, j * 5:(j + 1) * 5], cwl[0:5, j * 128:(j + 1) * 128], ident[0:5, 0:5], R=[cwl, g.cst])
        kb.cp(cwt[:, :, :], p[:, 0:60].rearrange("p (j i) -> p j i", i=5), R=[p], W=[cwt])
        p = nps(g)
        for j in range(12):
            kb.tr(p, p[:, j * 3:(j + 1) * 3], sct[0:3, j * 128:(j + 1) * 128], ident[0:3, 0:3], R=[sct, g.cst])
        kb.cp(hfm[:, :, :], p[:, 0:36].rearrange("p (j i) -> p j i", i=3), R=[p], W=[hfm])
        kb.barrier()
    return cwt, hfm


def conv_fm(g, wt, c0, cw, hist, ext, tmp, out):
    kb = g.kb
    ident = g.cst[:, 0:128]
    kb.op("dve", lambda e: e.memset(ext[:, 0:16], 0.0), W=[ext])
    kb.cp(ext[:, PS0 + 13:PS0 + 16], hist[1], R=[hist[0]], W=[ext])
    proj_fm(g, wt, c0, lambda tok0, n: (ext, ext[:, ucol(tok0):ucol(tok0) + n]))
    W_ = PW - 16
    kb.ts(tmp[:, 16:PW], ext[:, 13:13 + W_], cw[:, 0:1], cw[:, 4:5], ALU.mult, ALU.add, R=[ext], W=[tmp])
    for i in (1, 2, 3):
        kb.stt(tmp[:, 16:PW], ext[:, 13 + i:13 + i + W_], cw[:, i:i + 1], tmp[:, 16:PW], ALU.mult, ALU.add, R=[ext, tmp], W=[tmp])
    kb.act(out[:, 16:PW], tmp[:, 16:PW], AF.Silu, R=[tmp], W=[out])


def fm_to_tm(g, src, dst_fn, evi=0):
    kb = g.kb
    ident = g.cst[:, 0:128]
    for q4 in range(0, 17, 4):
        p = nps(g)
        tl = TTL[q4:q4 + 4]
        for i, (tok0, n) in enumerate(tl):
            c = ucol(tok0)
            kb.tr(p, p[0:n, i * 128:(i + 1) * 128], src[:, c:c + n], ident, R=[src, g.cst])
        for i, (tok0, n) in enumerate(tl):
            dT, dap = dst_fn(q4 + i, n)
            kb.cp(dap, p[0:n, i * 128:(i + 1) * 128], R=[p], W=[dT], e=("act" if (i + evi) % 2 else "dve"))


def bc3(ap2, n, k):
    return ap2.unsqueeze(2).to_broadcast([n, ap2.shape[1], k])


def phase_ssd(g, l):
    kb = g.kb
    ident = g.cst[:, 0:128]
    triu = g.cst[:, 128:256]
    ones = g.cst[:, 384:512]
    with ExitStack() as st:
        xtm = kb.sb(st, "s_xtm", [128, 17, 1024], BF16, multi=True)
        BmT = kb.sb(st, "s_BmT", [128, 2, NT], BF16, multi=True)
        CmT = kb.sb(st, "s_CmT", [128, 2, NT], BF16, multi=True)
        Btm = kb.sb(st, "s_Btm", [128, 17, 256], BF16, multi=True)
        dtt = kb.sb(st, "s_dt", [128, 17, 16], multi=True)
        dtA = kb.sb(st, "s_dtA", [128, 17, 16], multi=True)
        abc = kb.sb(st, "s_abc", [128, 16])
        kb.dma("sp", abc[:], g.ssm_a_log[l].partition_broadcast(128), R=[g.ssm_a_log], W=[abc])
        kb.act(abc[:], abc[:], AF.Exp, R=[abc], W=[abc])
        kb.ts(abc[:], abc[:], -1.0, None, ALU.mult, None, R=[abc], W=[abc])
        dtb = kb.sb(st, "s_dtb", [128, 16])
        kb.dma("sp", dtb[:], g.ssm_dt_bias[l].partition_broadcast(128), R=[g.ssm_dt_bias], W=[dtb])
        dsk = kb.sb(st, "s_dsk", [128, 16])
        kb.dma("sp", dsk[:], g.ssm_d[l].partition_broadcast(128), R=[g.ssm_d], W=[dsk])
        with ExitStack() as st1:
            cwt, hfm = load_conv_w(g, st1, g.ssm_conv_w, g.ssm_conv_b, g.state_ssm_conv, l, "s_cw")
            wb_ = [kb.sb(st1, "s_w%d" % i, [128, NKC, 512], BF16) for i in range(2)]
            ext = kb.sb(st1, "s_ext", [128, PW])
            tmp = kb.sb(st1, "s_tmp", [128, PW])
            xc = kb.sb(st1, "s_xc", [128, PW])
            cn = kb.sb(st1, "s_cn", [4, 512])
            for blk in range(3):
                wt = wb_[blk % 2]
                kb.dma("pool", wt[:], g.w_in[l, :, C_SSX + blk * 512:C_SSX + (blk + 1) * 512].rearrange("(kc p) c -> p kc c", p=128),
                       R=[g.w_in], W=[wt])
                for jj in range(4):
                    j = blk * 4 + jj
                    conv_fm(g, wt, jj * 128, cwt[:, j, :], (hfm, hfm[:, j, :]), ext, tmp, xc)
                    if j < 8:
                        fm_to_tm(g, xc, lambda ti, n, j=j: (xtm, xtm[0:n, ti, j * 128:(j + 1) * 128]), j)
                    elif j < 10:
                        gq = j - 8
                        for (tok0, n) in TCH:
                            kb.cp(BmT[:, gq, tok0:tok0 + n], xc[:, ucol(tok0):ucol(tok0) + n], R=[xc], W=[BmT], e="act")
                        fm_to_tm(g, xc, lambda ti, n, gq=gq: (Btm, Btm[0:n, ti, gq * 128:(gq + 1) * 128]), j)
                    else:
                        gq = j - 10
                        for (tok0, n) in TCH:
                            kb.cp(CmT[:, gq, tok0:tok0 + n], xc[:, ucol(tok0):ucol(tok0) + n], R=[xc], W=[CmT], e="act")
                p = nps(g)
                kb.mm(p, p[0:3, :], [(g.hT[:, kc, L - 3:L], wt[:, kc, :]) for kc in range(NKC)], R=[g.hT, wt])
                kb.cp(cn[0:3, :], p[0:3, :], R=[p], W=[cn])
                kb.dma("sp", g.ssc_p[l, :, blk * 512:(blk + 1) * 512], cn[0:3, :], R=[cn], W=[g.ssc_p])
                p = nps(g)
                kb.mm(p, p[0:LS, :], [(g.hT[:, kc, L:NT], wt[:, kc, :]) for kc in range(NKC)], R=[g.hT, wt])
                kb.cp(cn[0:LS, :], p[0:LS, :], R=[p], W=[cn])
                kb.dma("sp", g.ssc_s[l, :, blk * 512:(blk + 1) * 512], cn[1:LS, :], R=[cn], W=[g.ssc_s])
            wdt = kb.sb(st1, "s_wdt", [128, NKC, 16], BF16)
            kb.dma("pool", wdt[:], g.w_in[l, :, C_SSDT:C_SSDT + 16].rearrange("(kc p) c -> p kc c", p=128), R=[g.w_in], W=[wdt])
            for ti, (tok0, n) in enumerate(TTL):
                p = nps(g)
                kb.mm(p, p[0:n, 0:16], [(g.hT[:, kc, tok0:tok0 + n], wdt[:, kc, :]) for kc in range(NKC)], R=[g.hT, wdt])
                kb.tt(dtt[0:n, ti, :], p[0:n, 0:16], dtb[0:n, :], ALU.add, R=[p, dtb], W=[dtt])
            kb.act(dtt[:, :, :], dtt[:, :, :], AF.Exp, R=[dtt], W=[dtt])
            kb.act(dtt[:, :, :], dtt[:, :, :], AF.Ln, R=[dtt], W=[dtt], bias=1.0)
            kb.tt(dtA[:, :, :], dtt[:, :, :], abc[:, :].unsqueeze(1).to_broadcast([128, 17, 16]), ALU.mult, R=[dtt, abc], W=[dtA])
            kb.barrier()
        with ExitStack() as st2:
            wz = [kb.sb(st2, "s_wz%d" % i, [128, NKC, 512], BF16) for i in range(2)]
            for i in range(2):
                kb.dma("pool", wz[i][:], g.w_in[l, :, C_SSZ + i * 512:C_SSZ + (i + 1) * 512].rearrange("(kc p) c -> p kc c", p=128),
                       R=[g.w_in], W=[wz[i]])
            nwb = kb.sb(st2, "s_nwb", [128, 1024])
            kb.dma("sp", nwb[:], g.ssm_norm_w[l].partition_broadcast(128), R=[g.ssm_norm_w], W=[nwb])
            Wm = kb.sb(st2, "s_Wm", [128, 4, 128])
            Dm = kb.sb(st2, "s_Dm", [128, 16, 128])
            wts = kb.sb(st2, "s_wts", [128, 16, 128], BF16)
            cbm = kb.sb(st2, "s_cbm", [128, 2, 128])
            xdt = kb.sb(st2, "s_xdt", [128, 1024], BF16)
            xs2 = kb.sb(st2, "s_xs2", [128, 1024], BF16)
            yy = kb.sb(st2, "s_y", [128, 1024])
            ty = kb.sb(st2, "s_ty", [128, 1024])
            sz = kb.sb(st2, "s_sz", [128, 1024], BF16)
            hst = kb.sb(st2, "s_hst", [128, 1024])
            hstb = kb.sb(st2, "s_hstb", [128, 1024], BF16)
            cum = kb.sb(st2, "s_cum", [128, 16])
            totb = kb.sb(st2, "s_tot", [128, 16])
            ecum = kb.sb(st2, "s_ecum", [128, 16])
            edl = kb.sb(st2, "s_edl", [128, 16])
            elast = kb.sb(st2, "s_elast", [128, 16])
            ss = kb.sb(st2, "s_ss", [128, 2])
            junk = kb.sb(st2, "s_junk", [128, 512], BF16)
            ostg = [kb.sb(st2, "s_ostg%d" % i, [128, 8, 128], BF16) for i in range(2)]
            hin = Dm
            hout = Dm
            for ti, (tok0, n) in enumerate(TTL):
                if ti == 0:
                    kb.op("dve", lambda e: e.memset(hst[:], 0.0), W=[hst])
                    kb.op("dve", lambda e: e.memset(hstb[:], 0.0), W=[hstb])
                if ti == 16:
                    for half in range(2):
                        p = nps(g)
                        for i in range(4):
                            j = half * 4 + i
                            kb.tr(p, p[:, i * 128:(i + 1) * 128], hst[:, j * 128:(j + 1) * 128], ident, R=[hst, g.cst])
                        kb.cp(hout[:, 8 + half * 4:8 + half * 4 + 4, :], p[:, :].rearrange("p (a b) -> p a b", b=128), R=[p], W=[hout])
                    kb.dma("sp", g.ssh_p[l].rearrange("(j p) n -> p j n", p=128), hout[:, 8:16, :], R=[hout], W=[g.ssh_p])
                    kb.dma("sp", hin[:, 0:8, :], g.state_ssm_h[l].rearrange("(j p) n -> p j n", p=128), R=[g.state_ssm_h], W=[hin])
                    for half in range(2):
                        p = nps(g)
                        for i in range(4):
                            j = half * 4 + i
                            kb.tr(p, p[:, i * 128:(i + 1) * 128], hin[:, j, :], ident, R=[hin, g.cst])
                        kb.cp(hst[:, half * 512:(half + 1) * 512], p[:, :], R=[p], W=[hst])
                    kb.cp(hstb[:], hst[:], R=[hst], W=[hstb])
                dA = dtA[0:n, ti, :]
                p = nps(g)
                kb.mm(p, p[0:n, 0:16], [(triu[0:n, 0:n], dA)], R=[g.cst, dtA])
                kb.cp(cum[0:n, :], p[0:n, 0:16], R=[p], W=[cum])
                p = nps(g)
                kb.mm(p, p[:, 0:16], [(ones[0:n, :], dA)], R=[g.cst, dtA])
                kb.cp(totb[:], p[:, 0:16], R=[p], W=[totb])
                hpb = 4
                for q in range(0, 16, hpb):
                    kb.tt(Wm[0:n, :, 0:n], triu[0:n, 0:n].unsqueeze(1).to_broadcast([n, hpb, n]), bc3(dtA[0:n, ti, q:q + hpb], n, n), ALU.mult,
                          R=[g.cst, dtA], W=[Wm])
                    p = nps(g)
                    pv = p[0:n, 0:hpb * n].rearrange("p (h t) -> p h t", t=n)
                    kb.mm(p, pv, [(ones[0:n, 0:n], Wm[0:n, :, 0:n])], R=[g.cst, Wm])
                    for hh in range(hpb):
                        h = q + hh
                        kb.ts(Dm[0:n, h, 0:n], pv[:, hh, :], cum[0:n, h:h + 1], 0.0, ALU.subtract, ALU.min, R=[p, cum], W=[Dm])
                kb.act(Dm[0:n, :, 0:n], Dm[0:n, :, 0:n], AF.Exp, R=[Dm], W=[Dm])
                p = nps(g)
                for gq in range(2):
                    kb.mm(p, p[0:n, gq * 128:gq * 128 + n], [(BmT[:, gq, tok0:tok0 + n], CmT[:, gq, tok0:tok0 + n])], R=[BmT, CmT])
                kb.tt(cbm[0:n, :, 0:n], p[0:n, 0:256].rearrange("p (g t) -> p g t", t=128)[:, :, 0:n],
                      triu[0:n, 0:n].unsqueeze(1).to_broadcast([n, 2, n]), ALU.mult, R=[p, g.cst], W=[cbm])
                for gq in range(2):
                    kb.tt(wts[0:n, gq * 8:(gq + 1) * 8, 0:n], Dm[0:n, gq * 8:(gq + 1) * 8, 0:n],
                          cbm[0:n, gq:gq + 1, 0:n].to_broadcast([n, 8, n]), ALU.mult, R=[Dm, cbm], W=[wts])
                xv = xtm[0:n, ti, :].rearrange("p (h q) -> p h q", q=64)
                kb.tt(xdt[0:n, :].rearrange("p (h q) -> p h q", q=64), xv, bc3(dtt[0:n, ti, :], n, 64), ALU.mult, R=[xtm, dtt], W=[xdt])
                pyA = nps(g)
                pyB = nps(g)
                for h in range(16):
                    py = pyA if h < 8 else pyB
                    kb.mm(py, py[0:n, (h % 8) * 64:(h % 8 + 1) * 64], [(wts[0:n, h, 0:n], xdt[0:n, h * 64:(h + 1) * 64])], R=[wts, xdt])
                kb.act(ecum[0:n, :], cum[0:n, :], AF.Exp, R=[cum], W=[ecum])
                for gq in range(2):
                    pi = nps(g)
                    kb.mm(pi, pi[0:n, :], [(CmT[:, gq, tok0:tok0 + n], hstb[:, gq * 512:(gq + 1) * 512])], R=[CmT, hstb])
                    kb.tt(ty[0:n, gq * 512:(gq + 1) * 512].rearrange("p (h q) -> p h q", q=64), pi[0:n, :].rearrange("p (h q) -> p h q", q=64),
                          bc3(ecum[0:n, gq * 8:(gq + 1) * 8], n, 64), ALU.mult, R=[pi, ecum], W=[ty])
                    py = pyA if gq == 0 else pyB
                    kb.tt(yy[0:n, gq * 512:(gq + 1) * 512], ty[0:n, gq * 512:(gq + 1) * 512], py[0:n, :], ALU.add, R=[ty, py], W=[yy])
                kb.tt(ty[0:n, :].rearrange("p (h q) -> p h q", q=64), xv, bc3(dsk[0:n, :], n, 64), ALU.mult, R=[xtm, dsk], W=[ty])
                kb.tt(yy[0:n, :], yy[0:n, :], ty[0:n, :], ALU.add, R=[yy, ty], W=[yy])
                for i in range(2):
                    pzz = nps(g)
                    kb.mm(pzz, pzz[0:n, :], [(g.hT[:, kc, tok0:tok0 + n], wz[i][:, kc, :]) for kc in range(NKC)], R=[g.hT, wz[i]])
                    kb.act(sz[0:n, i * 512:(i + 1) * 512], pzz[0:n, :], AF.Silu, R=[pzz], W=[sz])
                kb.tt(yy[0:n, :], yy[0:n, :], sz[0:n, :], ALU.mult, R=[yy, sz], W=[yy])
                for gq in range(2):
                    kb.act(junk[0:n, :], yy[0:n, gq * 512:(gq + 1) * 512], AF.Square, R=[yy], W=[junk, ss], accum_out=ss[0:n, gq:gq + 1])
                kb.ts(ss[0:n, :], ss[0:n, :], 1.0 / 512, EPS, ALU.mult, ALU.add, R=[ss], W=[ss])
                kb.act(ss[0:n, :], ss[0:n, :], AF.Sqrt, R=[ss], W=[ss])
                kb.op("dve", lambda e: e.reciprocal(out=ss[0:n, :], in_=ss[0:n, :]), R=[ss], W=[ss])
                for gq in range(2):
                    kb.stt(yy[0:n, gq * 512:(gq + 1) * 512], yy[0:n, gq * 512:(gq + 1) * 512], ss[0:n, gq:gq + 1],
                           nwb[0:n, gq * 512:(gq + 1) * 512], ALU.mult, ALU.mult, R=[yy, ss, nwb], W=[yy])
                og = ostg[ti % 2]
                for half in range(2):
                    p = nps(g)
                    for i in range(4):
                        j = half * 4 + i
                        kb.tr(p, p[:, i * 128:i * 128 + n], yy[0:n, j * 128:(j + 1) * 128], ident[0:n, 0:n], R=[yy, g.cst])
                    kb.cp(og[:, half * 4:half * 4 + 4, 0:n], p[:, :].rearrange("p (a b) -> p a b", b=128)[:, :, 0:n], R=[p], W=[og],
                          e=("act" if half else "dve"))
                kb.dma("sp", g.oT[1536:2560, tok0:tok0 + n].rearrange("(j p) t -> p j t", p=128), og[:, :, 0:n], R=[og], W=[g.oT])
                kb.tt(edl[0:n, :], totb[0:n, :], cum[0:n, :], ALU.subtract, R=[totb, cum], W=[edl])
                kb.act(edl[0:n, :], edl[0:n, :], AF.Exp, R=[edl], W=[edl])
                kb.act(elast[:], totb[:], AF.Exp, R=[totb], W=[elast])
                kb.tt(xs2[0:n, :].rearrange("p (h q) -> p h q", q=64), xdt[0:n, :].rearrange("p (h q) -> p h q", q=64),
                      bc3(edl[0:n, :], n, 64), ALU.mult, R=[xdt, edl], W=[xs2])
                kb.tt(hst[:, :].rearrange("p (h q) -> p h q", q=64), hst[:, :].rearrange("p (h q) -> p h q", q=64),
                      bc3(elast[:, :], 128, 64), ALU.mult, R=[hst, elast], W=[hst])
                for gq in range(2):
                    ph = nps(g)
                    kb.mm(ph, ph[:, :], [(Btm[0:n, ti, gq * 128:(gq + 1) * 128], xs2[0:n, gq * 512:(gq + 1) * 512])], R=[Btm, xs2])
                    kb.tt(hst[:, gq * 512:(gq + 1) * 512], hst[:, gq * 512:(gq + 1) * 512], ph[:, :], ALU.add, R=[hst, ph], W=[hst])
                kb.cp(hstb[:], hst[:], R=[hst], W=[hstb], e="act")
            for half in range(2):
                p = nps(g)
                for i in range(4):
                    j = half * 4 + i
                    kb.tr(p, p[:, i * 128:(i + 1) * 128], hst[:, j * 128:(j + 1) * 128], ident, R=[hst, g.cst])
                kb.cp(hout[:, 8 + half * 4:8 + half * 4 + 4, :], p[:, :].rearrange("p (a b) -> p a b", b=128), R=[p], W=[hout])
            kb.dma("sp", g.ssh_s[l].rearrange("(j p) n -> p j n", p=128), hout[:, 8:16, :], R=[hout], W=[g.ssh_s])
            kb.barrier()


WEIGHT_NAMES = ["norm_mix_pre", "norm_mix_post", "norm_mlp_pre", "norm_mlp_post", "w_in", "pool_w", "pool_scale",
                "dn_conv_w", "dn_a_log", "dn_dt_bias", "dn_norm_w", "sb_bias", "ssm_conv_w", "ssm_conv_b", "ssm_a_log",
                "ssm_dt_bias", "ssm_d", "ssm_norm_w", "w_branch", "w_out", "w_up", "w_down"]


def core_inputs(inp, c, consts):
    m = {}
    m["x_prompt"] = np.ascontiguousarray(inp["x_prompt"][c % 4])
    m["x_sample"] = np.ascontiguousarray(inp["x_sample"][c])
    m["cache_sb_k"] = inp["cache_sb_k"].reshape(DEPTH * NPOOL * 128, 512)
    m["cache_sb_v"] = inp["cache_sb_v"].reshape(DEPTH * NPOOL * 128, 512)
    m["state_pool"] = np.ascontiguousarray(inp["state_pool"][:, c])
    m["state_dn_conv"] = np.ascontiguousarray(inp["state_dn_conv"][:, c])
    m["state_dn_s"] = np.ascontiguousarray(inp["state_dn_s"][:, c])
    m["state_ssm_conv"] = np.ascontiguousarray(inp["state_ssm_conv"][:, c])
    m["state_ssm_h"] = np.ascontiguousarray(inp["state_ssm_h"][:, c]).reshape(DEPTH, 1024, 128)
    m["page_table"] = np.ascontiguousarray(inp["page_table"][c:c + 1]).astype(np.int32)
    for n in WEIGHT_NAMES:
        m[n] = inp[n]
    m["consts"] = consts
    return m


_CACHE = {}


def kernel(**inp):
    inp = {k: np.asarray(v) for k, v in inp.items()}
    cd = make_consts()
    consts = np.ascontiguousarray(np.concatenate([cd[n] for n in CONST_ORDER], axis=1)).astype(np.float32)
    if "kb" not in _CACHE:
        _CACHE["kb"] = build()
    kb = _CACHE["kb"]
    ncores = 8
    in_maps = [core_inputs(inp, c, consts) for c in range(ncores)]
    res = run_bass_kernel_spmd(kb.nc, in_maps, core_ids=list(range(ncores)))
    r = res.results
    P = range(4)
    S = range(8)
    def stk(name, cores, shape):
        return np.stack([np.asarray(r[c][name]).reshape(shape) for c in cores], axis=1)
    y_prompt = np.stack([r[c]["y_prompt"] for c in P], axis=0)
    y_sample = np.stack([r[c]["y_sample"] for c in S], axis=0)
    outs = [y_prompt, y_sample,
            stk("k_p", P, (DEPTH, L, 4, 128)), stk("v_p", P, (DEPTH, L, 4, 128)),
            stk("pool_p", P, (DEPTH, 15, 512)), stk("dnc_p", P, (DEPTH, 3, 1536)),
            stk("dns_p", P, (DEPTH, 4, 128, 128)), stk("ssc_p", P, (DEPTH, 3, 1536)),
            stk("ssh_p", P, (DEPTH, 16, 64, 128)),
            stk("k_s", S, (DEPTH, LS, 4, 128)), stk("v_s", S, (DEPTH, LS, 4, 128)),
            stk("pool_s", S, (DEPTH, 15, 512)), stk("dnc_s", S, (DEPTH, 3, 1536)),
            stk("dns_s", S, (DEPTH, 4, 128, 128)), stk("ssc_s", S, (DEPTH, 3, 1536)),
            stk("ssh_s", S, (DEPTH, 16, 64, 128))]
    return tuple(np.ascontiguousarray(o.astype(np.float32)) for o in outs)
```
